# Optimizing a Trainium2 kernel written in Bass

```python
import jax, jax.numpy as jnp
from jax import lax
import numpy as np

D_MODEL = 1024
BATCH = 4
SEQ = 4096
DEPTH = 4
DEC_BATCH = 16
DEC_SEQ = 2048
PAST_LEN = 128

ATT_HEADS = 16
ATT_KV_HEADS = 4
ATT_GROUP = ATT_HEADS // ATT_KV_HEADS
ATT_HEAD_DIM = 64
ATT_WIDTH = ATT_HEADS * ATT_HEAD_DIM
ATT_KV_WIDTH = ATT_KV_HEADS * ATT_HEAD_DIM
WINDOW = 128
ATT_BLOCK = 128
ROPE_THETA = 500000.0
ROPE_DIM = ATT_HEAD_DIM // 4
M_HEADS = 8
M_HEAD_DIM = 128
M_WIDTH = M_HEADS * M_HEAD_DIM
M_CHUNK = 128
CONV_K = 3
N_BRANCH = 2
NORM_EPS = 1e-6
NEG = -1e30

SPLITS = (ATT_WIDTH, ATT_KV_WIDTH, ATT_KV_WIDTH, ATT_WIDTH,
          M_WIDTH, M_WIDTH, M_WIDTH, M_WIDTH, M_WIDTH,
          M_HEADS, M_HEADS, M_HEADS, M_HEADS,
          N_BRANCH * D_MODEL)
IN_DIM = sum(SPLITS)

kernel_name = "hybrid_swa_mlstm_bidir_encoder"


def rmsnorm(x, g):
    xf = x.astype(jnp.float32)
    y = xf * lax.rsqrt(jnp.mean(xf * xf, axis=-1, keepdims=True) + NORM_EPS) * g.astype(jnp.float32)
    return y.astype(x.dtype)


def partial_rope(x):
    S = x.shape[1]
    half = ROPE_DIM // 2
    inv = jnp.power(jnp.float32(ROPE_THETA), -jnp.arange(half, dtype=jnp.float32) * 2.0 / ROPE_DIM)
    ang = jnp.arange(S, dtype=jnp.float32)[:, None] * inv[None, :]
    cos = jnp.cos(ang)[None, :, None, :]
    sin = jnp.sin(ang)[None, :, None, :]
    xf = x.astype(jnp.float32)
    x1 = xf[..., :half]
    x2 = xf[..., half:ROPE_DIM]
    out = jnp.concatenate([x1 * cos - x2 * sin, x2 * cos + x1 * sin, xf[..., ROPE_DIM:]], axis=-1)
    return out.astype(x.dtype)


def windowed_gqa_sink(q, k, v, sink):
    B, S = q.shape[0], q.shape[1]
    nb = S // ATT_BLOCK
    qb = q.reshape(B, nb, ATT_BLOCK, ATT_KV_HEADS, ATT_GROUP, ATT_HEAD_DIM)

    def band(t):
        tp = jnp.pad(t, ((0, 0), (ATT_BLOCK, ATT_BLOCK), (0, 0), (0, 0)))
        tp = tp.reshape(B, nb + 2, ATT_BLOCK, ATT_KV_HEADS, ATT_HEAD_DIM)
        return jnp.concatenate([tp[:, :-2], tp[:, 1:-1], tp[:, 2:]], axis=2)

    kb, vb = band(k), band(v)
    s = jnp.einsum("bnqhgd,bnkhd->bnhgqk", qb, kb).astype(jnp.float32) * (ATT_HEAD_DIM ** -0.5)
    blk = jnp.arange(nb)[:, None, None]
    qpos = blk * ATT_BLOCK + jnp.arange(ATT_BLOCK)[None, :, None]
    kpos = (blk - 1) * ATT_BLOCK + jnp.arange(3 * ATT_BLOCK)[None, None, :]
    valid = (jnp.abs(kpos - qpos) <= WINDOW) & (kpos >= 0) & (kpos < S)
    s = jnp.where(valid[None, :, None, None], s, NEG)
    sink_l = jnp.broadcast_to(sink.astype(jnp.float32).reshape(1, 1, ATT_KV_HEADS, ATT_GROUP, 1, 1), s.shape[:-1] + (1,))
    p = jax.nn.softmax(jnp.concatenate([s, sink_l], axis=-1), axis=-1)[..., :-1]
    o = jnp.einsum("bnhgqk,bnkhd->bnqhgd", p.astype(v.dtype), vb)
    return o.reshape(B, S, ATT_WIDTH)


def centred_conv(x, w):
    S = x.shape[1]
    pad = CONV_K // 2
    xp = jnp.pad(x, ((0, 0), (pad, pad), (0, 0)))
    out = xp[:, 0:S] * w[0]
    for j in range(1, CONV_K):
        out = out + xp[:, j:j + S] * w[j]
    return out


def mlstm_chunkwise(q, k, v, log_i, log_f):
    B, H, S, dk = q.shape
    dv = v.shape[-1]
    L = M_CHUNK
    nc = S // L

    def chunks(t):
        return jnp.moveaxis(t.reshape((B, H, nc, L) + t.shape[3:]), 2, 0)

    qc, kc, vc, lic, lfc = chunks(q), chunks(k), chunks(v), chunks(log_i), chunks(log_f)
    b = jnp.cumsum(lfc, axis=-1)
    lower = jnp.tril(jnp.ones((L, L), dtype=bool))
    log_d = jnp.where(lower, b[..., :, None] - b[..., None, :] + lic[..., None, :], NEG)
    g = b[..., -1:] - b + lic
    qk = jnp.einsum("nbhld,nbhsd->nbhls", qc, kc)

    def step(carry, inp):
        C, n, m = carry
        q_, k_, v_, ld, qk_, b_, g_ = inp
        m_inter = b_ + m[..., None]
        m_t = jnp.maximum(m_inter, ld.max(axis=-1))
        w_inter = jnp.exp(m_inter - m_t)
        p = jnp.exp(ld - m_t[..., None]) * qk_
        num = w_inter[..., None] * jnp.einsum("bhld,bhde->bhle", q_, C) + jnp.einsum("bhls,bhse->bhle", p, v_)
        den = w_inter * jnp.einsum("bhld,bhd->bhl", q_, n) + p.sum(axis=-1)
        h = num / jnp.maximum(jnp.abs(den), jnp.exp(-m_t))[..., None]
        b_last = b_[..., -1]
        m_new = jnp.maximum(b_last + m, g_.max(axis=-1))
        w_c = jnp.exp(b_last + m - m_new)
        w_k = jnp.exp(g_ - m_new[..., None])
        C = w_c[..., None, None] * C + jnp.einsum("bhl,bhld,bhle->bhde", w_k, k_, v_)
        n = w_c[..., None] * n + jnp.einsum("bhl,bhld->bhd", w_k, k_)
        return (C, n, m_new), h

    init = (jnp.zeros((B, H, dk, dv), jnp.float32), jnp.zeros((B, H, dk), jnp.float32),
            jnp.full((B, H), NEG, jnp.float32))
    _, h = lax.scan(step, init, (qc, kc, vc, log_d, qk, b, g))
    return jnp.moveaxis(h, 0, 2).reshape(B, H, S, dv)


def mlstm_bidir(q, k, v, i_f, f_f, i_b, f_b):
    def to_bhs(t):
        return jnp.moveaxis(t.astype(jnp.float32), 1, 2)

    def flip(t):
        return jnp.flip(t, axis=2)

    q, k, v = to_bhs(q), to_bhs(k) * (M_HEAD_DIM ** -0.5), to_bhs(v)
    h_fwd = mlstm_chunkwise(q, k, v, to_bhs(i_f), jax.nn.log_sigmoid(to_bhs(f_f)))
    h_bwd = flip(mlstm_chunkwise(flip(q), flip(k), flip(v), flip(to_bhs(i_b)),
                                 flip(jax.nn.log_sigmoid(to_bhs(f_b)))))
    return jnp.moveaxis(h_fwd + h_bwd, 2, 1)


def encoder_layer(x, norm_g, w_in, b_in, q_norm_g, k_norm_g, sink, conv_w, m_norm_g, w_att_out, w_m_out, w_out):
    dt = x.dtype
    B, S, _ = x.shape
    xn = rmsnorm(x, norm_g)
    proj = xn @ w_in + b_in
    idx = [int(i) for i in np.cumsum(SPLITS)[:-1]]
    (aq, ak, av, az, mq, mk, mv, mo, mz, i_f, f_f, i_b, f_b, gates) = jnp.split(proj, idx, axis=-1)

    aq = partial_rope(rmsnorm(aq.reshape(B, S, ATT_HEADS, ATT_HEAD_DIM), q_norm_g))
    ak = partial_rope(rmsnorm(ak.reshape(B, S, ATT_KV_HEADS, ATT_HEAD_DIM), k_norm_g))
    av = av.reshape(B, S, ATT_KV_HEADS, ATT_HEAD_DIM)
    att = windowed_gqa_sink(aq, ak, av, sink)
    branch_a = (att * jax.nn.silu(az)) @ w_att_out

    qk_m = jax.nn.silu(centred_conv(jnp.concatenate([mq, mk], axis=-1), conv_w))
    mq, mk = qk_m[..., :M_WIDTH], qk_m[..., M_WIDTH:]
    hs = (M_HEADS, M_HEAD_DIM)
    h = mlstm_bidir(mq.reshape(B, S, *hs), mk.reshape(B, S, *hs), mv.reshape(B, S, *hs), i_f, f_f, i_b, f_b)
    h = jax.nn.sigmoid(mo.astype(jnp.float32)).reshape(B, S, *hs) * h
    h = h * lax.rsqrt(jnp.mean(h * h, axis=-1, keepdims=True) + NORM_EPS) * m_norm_g.astype(jnp.float32).reshape(hs)
    h = h.reshape(B, S, M_WIDTH).astype(dt)
    branch_m = (h * jax.nn.silu(mz)) @ w_m_out

    gates = jax.nn.sigmoid(gates)
    merged = gates[..., :D_MODEL] * branch_a + gates[..., D_MODEL:] * branch_m
    return (x + merged @ w_out).astype(dt)


def trunk(x, norm_g, w_in, b_in, q_norm_g, k_norm_g, sink, conv_w, m_norm_g, w_att_out, w_m_out, w_out):
    for l in range(DEPTH):
        x = encoder_layer(x, norm_g[l], w_in[l], b_in[l], q_norm_g[l], k_norm_g[l], sink[l], conv_w[l],
                          m_norm_g[l], w_att_out[l], w_m_out[l], w_out[l])
    return x


def setup_inputs(seed: int = 0) -> dict:
    key = jax.random.key(seed)
    ks = jax.random.split(key, 16)
    f32 = jnp.float32
    nrm = jax.random.normal
    x_prompt = nrm(ks[0], (BATCH, SEQ, D_MODEL), f32)
    x_sample = nrm(ks[1], (DEC_BATCH, DEC_SEQ, D_MODEL), f32)
    norm_g = 1.0 + 0.02 * nrm(ks[2], (DEPTH, D_MODEL), f32)
    w_in = nrm(ks[3], (DEPTH, D_MODEL, IN_DIM), f32) * (D_MODEL ** -0.5)
    b_in = 0.01 * nrm(ks[4], (DEPTH, IN_DIM), f32)
    f_bias = jnp.linspace(3.0, 6.0, M_HEADS, dtype=f32) + 0.1 * nrm(ks[5], (DEPTH, 2, M_HEADS), f32)
    off_ff = sum(SPLITS[:10])
    off_fb = sum(SPLITS[:12])
    b_in = b_in.at[:, off_ff:off_ff + M_HEADS].set(f_bias[:, 0]).at[:, off_fb:off_fb + M_HEADS].set(f_bias[:, 1])
    q_norm_g = 1.0 + 0.02 * nrm(ks[6], (DEPTH, ATT_HEAD_DIM), f32)
    k_norm_g = 1.0 + 0.02 * nrm(ks[7], (DEPTH, ATT_HEAD_DIM), f32)
    sink = 0.5 * nrm(ks[8], (DEPTH, ATT_HEADS), f32)
    conv_w = nrm(ks[9], (DEPTH, CONV_K, 2 * M_WIDTH), f32) * (CONV_K ** -0.5)
    m_norm_g = 1.0 + 0.02 * nrm(ks[10], (DEPTH, M_WIDTH), f32)
    w_att_out = nrm(ks[11], (DEPTH, ATT_WIDTH, D_MODEL), f32) * (ATT_WIDTH ** -0.5)
    w_m_out = nrm(ks[12], (DEPTH, M_WIDTH, D_MODEL), f32) * (M_WIDTH ** -0.5)
    w_out = nrm(ks[13], (DEPTH, D_MODEL, D_MODEL), f32) * (D_MODEL ** -0.5)
    return {"x_prompt": x_prompt, "x_sample": x_sample, "norm_g": norm_g, "w_in": w_in, "b_in": b_in,
            "q_norm_g": q_norm_g, "k_norm_g": k_norm_g, "sink": sink, "conv_w": conv_w,
            "m_norm_g": m_norm_g, "w_att_out": w_att_out, "w_m_out": w_m_out, "w_out": w_out}


def reference(x_prompt, x_sample, norm_g, w_in, b_in, q_norm_g, k_norm_g, sink, conv_w, m_norm_g, w_att_out, w_m_out, w_out):
    y_prompt = trunk(x_prompt, norm_g, w_in, b_in, q_norm_g, k_norm_g, sink, conv_w, m_norm_g, w_att_out, w_m_out, w_out)
    y_sample = trunk(x_sample, norm_g, w_in, b_in, q_norm_g, k_norm_g, sink, conv_w, m_norm_g, w_att_out, w_m_out, w_out)
    return (y_prompt, y_sample)
```

```python
import numpy as np
import concourse.bass as bass
import concourse.mybir as mybir
from concourse.bass_utils import run_bass_kernel_spmd

F32 = mybir.dt.float32
BF16 = mybir.dt.bfloat16
ALU = mybir.AluOpType
AF = mybir.ActivationFunctionType
AX = mybir.AxisListType
ENGS = ("sync", "scalar", "vector", "gpsimd", "tensor")

D = 1024
IN_DIM = 9760
EPS = 1e-6
O_AQ, O_AK, O_AV, O_AZ = 0, 1024, 1280, 1536
O_MQ, O_MK, O_MV, O_MO, O_MZ = 2560, 3584, 4608, 5632, 6656
O_G4, O_MG = 7680, 7712


class Tok:
    __slots__ = ("sem", "val")

    def __init__(self, sem, val):
        self.sem = sem
        self.val = val


class Buf:
    __slots__ = ("t", "wtok", "rtoks", "dsem")

    def __init__(self, t):
        self.t = t
        self.wtok = None
        self.rtoks = []
        self.dsem = None

    def __getitem__(self, idx):
        return self.t[idx]


class Prog:
    def __init__(self, nc):
        self.nc = nc
        self.q = {e: [] for e in ENGS}
        self.esem = {}
        self.ecnt = {e: 0 for e in ENGS}
        self.waited = {e: {} for e in ENGS}
        self.cms = []
        self.sem_cms = []
        self.sem_pool = []
        self.phase_slots = []
        self.pending = []
        self.nsem = 0
        for e in ENGS:
            self.esem[e] = self.new_sem("p_" + e)

    def enter(self, cm):
        v = cm.__enter__()
        self.cms.append(cm)
        return v

    def new_sem(self, name):
        self.nsem += 1
        cm = self.nc.semaphore(name)
        v = cm.__enter__()
        self.sem_cms.append(cm)
        return v

    def sb(self, name, shape, dt):
        self.nsem += 1
        return Buf(self.enter(self.nc.sbuf_tensor("s%d_%s" % (self.nsem, name), shape, dt)))

    def ps(self, name, shape, dt=F32):
        self.nsem += 1
        return Buf(self.enter(self.nc.psum_tensor("p%d_%s" % (self.nsem, name), shape, dt)))

    def wait(self, eng, tok):
        if tok is None:
            return
        w = self.waited[eng]
        k = id(tok.sem)
        if w.get(k, 0) >= tok.val:
            return
        w[k] = tok.val
        self.q[eng].append(lambda e, s=tok.sem, v=tok.val: e.wait_ge(s, v))

    def _deps(self, eng, reads, writes):
        for b in reads:
            self.wait(eng, b.wtok)
        for b in writes:
            self.wait(eng, b.wtok)
            for t in b.rtoks:
                self.wait(eng, t)

    def _post(self, tok, reads, writes):
        for b in reads:
            b.rtoks.append(tok)
            if len(b.rtoks) > 24:
                b.rtoks = b.rtoks[-24:]
        for b in writes:
            b.wtok = tok
            b.rtoks = []

    def op(self, eng, fn, reads=(), writes=()):
        self._deps(eng, reads, writes)
        self.ecnt[eng] += 1
        sem = self.esem[eng]
        tok = Tok(sem, self.ecnt[eng])
        self.q[eng].append(lambda e, fn=fn, sem=sem: fn(e).then_inc(sem, 1))
        self._post(tok, reads, writes)
        return tok

    def mm(self, fns, reads=(), writes=()):
        eng = "tensor"
        self._deps(eng, reads, writes)
        for fn in fns[:-1]:
            self.q[eng].append(lambda e, fn=fn: fn(e))
        self.ecnt[eng] += 1
        sem = self.esem[eng]
        tok = Tok(sem, self.ecnt[eng])
        self.q[eng].append(lambda e, fn=fns[-1], sem=sem: fn(e).then_inc(sem, 1))
        self._post(tok, reads, writes)
        return tok

    def dma(self, out_ap, in_ap, reads=(), writes=(), slot=None, eng="sync"):
        self._deps(eng, reads, writes)
        if slot.dsem is None:
            slot.dsem = self.sem_pool.pop() if self.sem_pool else [self.new_sem("d%d" % self.nsem), 0]
            self.phase_slots.append(slot)
        slot.dsem[1] += 16
        sem, val = slot.dsem
        tok = Tok(sem, val)
        self.q[eng].append(lambda e, o=out_ap, i=in_ap, s=sem: e.dma_start(out=o, in_=i).then_inc(s, 16))
        self._post(tok, reads, writes)
        return tok

    def replay(self, block):
        q = self.q

        @block.sync
        def _(e):
            for f in q["sync"]:
                f(e)

        @block.scalar
        def _(e):
            for f in q["scalar"]:
                f(e)

        @block.vector
        def _(e):
            for f in q["vector"]:
                f(e)

        @block.gpsimd
        def _(e):
            for f in q["gpsimd"]:
                f(e)

        @block.tensor
        def _(e):
            for f in q["tensor"]:
                f(e)

    def close(self):
        for cm in reversed(self.cms):
            cm.__exit__(None, None, None)
        self.cms = []
        for cm in reversed(self.sem_cms):
            cm.__exit__(None, None, None)
        self.sem_cms = []


def bcast_ap(t, off, dims):
    return bass.AP(t, off, dims)


class Cfg:
    def __init__(self, nseg=3, segb=16, tb=4, depth=4):
        self.NSEG, self.SEGB, self.TB, self.DEPTH = nseg, segb, tb, depth
        self.NBLK = nseg * segb
        self.T = self.NBLK * 128
        self.TT = tb * 128
        self.NT = self.NBLK // tb
        self.SEGT = segb * 128
        self.TPS = segb // tb


def build(cfg, debug=False):
    nc = bass.Bass("TRN2", target_bir_lowering=False)
    T, TT, NT, NBLK, NSEG, SEGB, TB, SEGT, L = cfg.T, cfg.TT, cfg.NT, cfg.NBLK, cfg.NSEG, cfg.SEGB, cfg.TB, cfg.SEGT, cfg.DEPTH
    NB = max(NSEG - 1, 1)

    def din(name, shape, dt=F32):
        return nc.dram_tensor(name, list(shape), dt, kind="ExternalInput").ap()

    def dscr(name, shape, dt):
        return nc.dram_tensor(name, list(shape), dt, kind=("ExternalOutput" if debug else "Internal")).ap()

    xT_in = din("xT", [D, T])
    flags_in = din("flags", [128, NB])
    ropeq_in = din("ropeq", [2, 128, T])
    ropek_in = din("ropek", [2, 128, T])
    consts_in = din("consts", [128, 6, 128])
    w_in_d = din("w_in", [L, D, IN_DIM])
    b_in_d = din("b_in", [L, IN_DIM])
    biasT_d = din("biasT", [L, 128, 50])
    gT_d = din("gT", [L, 128, 8])
    mgT_d = din("mgT", [L, 128, 8])
    gqk_d = din("gqk", [L, 128, 2])
    sink_d = din("sink", [L, 16])
    cwT_d = din("cwT", [L, 128, 16, 3])
    wa_d = din("w_att_out", [L, D, D])
    wm_d = din("w_m_out", [L, D, D])
    wo_d = din("w_out", [L, D, D])
    yT_out = nc.dram_tensor("yT", [D, T], F32, kind="ExternalOutput").ap()

    xs_d = dscr("xs", [D, T], F32)
    QT = dscr("QT", [D, T], BF16)
    KT = dscr("KT", [256, T], BF16)
    VA = dscr("VA", [T, 256], BF16)
    ZAT = dscr("ZAT", [D, T], BF16)
    MQT = dscr("MQT", [D, T], BF16)
    MKT = dscr("MKT", [D, T], BF16)
    MV = dscr("MV", [T, D], BF16)
    MO = dscr("MO", [T, D], BF16)
    MZ = dscr("MZ", [T, D], BF16)
    GTS = dscr("GTS", [T, 32], F32)
    GAT = dscr("GAT", [2 * D, T], BF16)
    HB = dscr("HB", [T, D], F32)
    HN = dscr("HN", [T, D], BF16)
    AGT = dscr("AGT", [D, T], BF16)

    P = Prog(nc)
    op, mm, dma = P.op, P.mm, P.dma

    res_bufs = []
    for l in range(L):
        src = xT_in if l == 0 else res_bufs[-1][1]
        dst = yT_out if (L - 1 - l) % 2 == 0 else xs_d
        res_bufs.append((src, dst))

    cst32 = P.sb("cst32", [128, 6, 128], F32)
    cstbf = P.sb("cstbf", [128, 6, 128], BF16)
    flg = P.sb("flg", [128, NB], F32)
    amask = P.sb("amask", [128, 2 + 2 * NB, 512], BF16)
    dma(cst32[:], consts_in, writes=[cst32], slot=cst32)
    dma(flg[:], flags_in, writes=[flg], slot=flg)
    op("vector", lambda e: e.tensor_copy(out=cstbf[:], in_=cst32[:]), reads=[cst32], writes=[cstbf])
    IDENT, TRIU, TRIL, BD64, RPERM, ONES = range(6)
    for g in range(4):
        op("vector", lambda e, g=g: e.tensor_copy(out=amask[:, 0, g * 128:(g + 1) * 128], in_=cst32[:, TRIL, :]), reads=[cst32], writes=[amask])
        op("vector", lambda e, g=g: e.tensor_copy(out=amask[:, 1, g * 128:(g + 1) * 128], in_=cst32[:, TRIU, :]), reads=[cst32], writes=[amask])
    for b in range(NSEG - 1):
        op("vector", lambda e, b=b: e.tensor_scalar(out=amask[:, 2 + 2 * b, :], in0=amask[:, 0, :], scalar1=flg[:, b:b + 1], scalar2=None, op0=ALU.mult), reads=[flg, amask], writes=[amask])
        op("vector", lambda e, b=b: e.tensor_scalar(out=amask[:, 3 + 2 * b, :], in0=amask[:, 1, :], scalar1=flg[:, b:b + 1], scalar2=None, op0=ALU.mult), reads=[flg, amask], writes=[amask])

    def sync_stores(bufs):
        for b in bufs:
            for t in list(b.rtoks) + [b.wtok]:
                if t is not None:
                    P.wait("sync", t)
                    P.pending.append(t)

    def phase_scope():
        return len(P.cms)

    def phase_end(mark):
        toks = list(P.pending)
        P.pending = []
        for e in ("scalar", "vector", "gpsimd", "tensor"):
            if P.ecnt[e] > 0:
                toks.append(Tok(P.esem[e], P.ecnt[e]))
        for e in ENGS:
            for t in toks:
                P.wait(e, t)
        for sl in P.phase_slots:
            P.sem_pool.append(sl.dsem)
            sl.dsem = None
        P.phase_slots = []
        while len(P.cms) > mark:
            cm = P.cms.pop()
            cm.__exit__(None, None, None)

    for l in range(L):
        x_src, x_dst = res_bufs[l]
        mark = phase_scope()
        xn = P.sb("xn", [128, 8, T], BF16)
        xn_tiles = [Buf(xn.t) for _ in range(NT)]
        gT = P.sb("gT", [128, 8], F32)
        biasT = P.sb("biasT", [128, 50], F32)
        hbias = P.sb("hbias", [128, 50], F32)
        gqk = P.sb("gqk", [128, 2], F32)
        cw = P.sb("cw", [128, 16, 3], F32)
        dma(gT[:], gT_d[l], writes=[gT], slot=gT)
        dma(biasT[:], biasT_d[l], writes=[biasT], slot=biasT)
        dma(gqk[:], gqk_d[l], writes=[gqk], slot=gqk)
        dma(cw[:], cwT_d[l], writes=[cw], slot=cw)
        op("vector", lambda e: e.tensor_scalar(out=hbias[:], in0=biasT[:], scalar1=0.5, scalar2=None, op0=ALU.mult), reads=[biasT], writes=[hbias])
        KS = 128.0 ** -0.5
        op("vector", lambda e: e.tensor_scalar(out=cw[:, 0:8, :], in0=cw[:, 0:8, :], scalar1=0.5, scalar2=None, op0=ALU.mult), reads=[cw], writes=[cw])
        op("vector", lambda e: e.tensor_scalar(out=cw[:, 8:16, :], in0=cw[:, 8:16, :], scalar1=0.5 * KS, scalar2=None, op0=ALU.mult), reads=[cw], writes=[cw])

        mark1 = phase_scope()
        xst = [P.sb("xst%d" % i, [128, 8, TT], F32) for i in range(2)]
        sqb = [P.sb("sqb%d" % i, [128, 8, TT], BF16) for i in range(2)]
        rsb = [P.sb("rsb%d" % i, [128, TT], F32) for i in range(2)]
        pss = [P.ps("pss%d" % i, [128, TT]) for i in range(2)]
        xv = x_src.rearrange("(c p) t -> p c t", p=128)
        for t in range(NT):
            xb, sq, rs, pp = xst[t % 2], sqb[t % 2], rsb[t % 2], pss[t % 2]
            dma(xb[:], xv[:, :, t * TT:(t + 1) * TT], writes=[xb], slot=xb)
            op("scalar", lambda e, xb=xb, sq=sq: e.activation(out=sq[:], in_=xb[:], func=AF.Square), reads=[xb], writes=[sq])
            mm([lambda e, c=c, sq=sq, pp=pp: e.matmul(pp[:], cstbf[:, ONES, :], sq[:, c, :], start=(c == 0), stop=(c == 7)) for c in range(8)], reads=[sq, cstbf], writes=[pp])
            op("scalar", lambda e, rs=rs, pp=pp: e.activation(out=rs[:], in_=pp[:], func=AF.Ln, bias=EPS, scale=1.0 / D), reads=[pp], writes=[rs])
            op("scalar", lambda e, rs=rs: e.activation(out=rs[:], in_=rs[:], func=AF.Exp, scale=-0.5), reads=[rs], writes=[rs])
            rs_bc = bass.AP(rs.t, 0, [[TT, 128], [0, 8], [1, TT]])
            op("vector", lambda e, xb=xb, rs_bc=rs_bc, t=t: e.tensor_tensor(out=xn[:, :, t * TT:(t + 1) * TT], in0=xb[:], in1=rs_bc, op=ALU.mult), reads=[xb, rs], writes=[xn_tiles[t]])

        phase_end(mark1)
        WS = 512
        wst = [P.sb("wst%d" % i, [128, 8, WS], F32) for i in range(1)]
        wbf = [P.sb("wbf%d" % i, [128, 8, WS], BF16) for i in range(2)]
        slab_ctr = [0]
        wv = w_in_d[l].rearrange("(kc p) c -> p kc c", p=128)

        def load_slab(c0, ncols):
            i = slab_ctr[0] % 2
            slab_ctr[0] += 1
            ws_, wb_ = wst[0], wbf[i]
            dma(ws_[:, :, 0:ncols], wv[:, :, c0:c0 + ncols], writes=[ws_], slot=ws_)
            for kc in range(8):
                eng = "gpsimd" if kc % 2 == 0 else "vector"
                op(eng, lambda e, kc=kc, ws_=ws_, wb_=wb_: e.tensor_scalar(out=wb_[:, kc, 0:ncols], in0=ws_[:, kc, 0:ncols], scalar1=gT[:, kc:kc + 1], scalar2=None, op0=ALU.mult), reads=[ws_, gT], writes=[wb_])
            return wb_

        psA = [P.ps("psA%d" % i, [128, 512]) for i in range(3)]
        psB = [P.ps("psB%d" % i, [128, 512]) for i in range(4)]
        pa_ctr = [0]

        def next_psA():
            pa_ctr[0] += 1
            return psA[pa_ctr[0] % 3]

        def fm_matmul(pp, wb_, cc, t):
            mm([lambda e, kc=kc: e.matmul(pp[:, 0:TT], wb_[:, kc, cc * 128:(cc + 1) * 128], xn[:, kc, t * TT:(t + 1) * TT], start=(kc == 0), stop=(kc == 7)) for kc in range(8)],
               reads=[wb_, xn_tiles[t]], writes=[pp])

        stg = [P.sb("stg%d" % i, [128, 512], BF16) for i in range(4)]
        stg_ctr = [0]

        def next_stg():
            stg_ctr[0] += 1
            return stg[stg_ctr[0] % 4]

        markA = phase_scope()
        wqk = P.sb("wqk", [128, 8, 1280], BF16)
        for (c0, ncols) in ((0, 512), (512, 512), (1024, 256)):
            wb_ = load_slab(c0, ncols)
            op("gpsimd", lambda e, wb_=wb_, c0=c0, ncols=ncols: e.tensor_copy(out=wqk[:, :, c0:c0 + ncols], in_=wb_[:, :, 0:ncols]), reads=[wb_], writes=[wqk])
        rq = [P.sb("rq%d" % i, [128, 2, TT], F32) for i in range(2)]
        rk = [P.sb("rk%d" % i, [128, 2, TT], F32) for i in range(2)]
        q0 = [P.sb("q0_%d" % i, [128, TT], F32) for i in range(3)]
        sq1 = [P.sb("sq1_%d" % i, [128, TT], BF16) for i in range(3)]
        rs1 = [P.sb("rs1_%d" % i, [128, TT], F32) for i in range(3)]
        qn = [P.sb("qn_%d" % i, [128, TT], BF16) for i in range(3)]
        t1b = [P.sb("t1b_%d" % i, [128, TT], F32) for i in range(2)]
        t2b = [P.sb("t2b_%d" % i, [128, TT], F32) for i in range(2)]
        itemsA = [(t, cc) for t in range(NT) for cc in range(10)]
        ppA = {}

        def a_rope_loads(t):
            dma(rq[t % 2][:], ropeq_in[:, :, t * TT:(t + 1) * TT].rearrange("a p t -> p a t"), writes=[rq[t % 2]], slot=rq[t % 2])
            dma(rk[t % 2][:], ropek_in[:, :, t * TT:(t + 1) * TT].rearrange("a p t -> p a t"), writes=[rk[t % 2]], slot=rk[t % 2])

        def a_stage1(i):
            t, cc = itemsA[i]
            if cc == 0:
                a_rope_loads(t)
            pp = next_psA()
            fm_matmul(pp, wqk, cc, t)
            a_q0, a_sq = q0[i % 3], sq1[i % 3]
            op("scalar", lambda e, a=a_q0, pp=pp, cc=cc: e.activation(out=a[:], in_=pp[:, 0:TT], func=AF.Identity, bias=biasT[:, cc:cc + 1], scale=1.0), reads=[pp, biasT], writes=[a_q0])
            op("scalar", lambda e, a=a_sq, pp=pp, cc=cc: e.activation(out=a[:], in_=pp[:, 0:TT], func=AF.Square, bias=biasT[:, cc:cc + 1], scale=1.0), reads=[pp, biasT], writes=[a_sq])

        def a_stage2(i):
            t, cc = itemsA[i]
            gcol = 0 if cc < 8 else 1
            p2 = psB[i % 2]
            a_q0, a_sq, a_rs, a_qn = q0[i % 3], sq1[i % 3], rs1[i % 3], qn[i % 3]
            mm([lambda e, a=a_sq, p2=p2: e.matmul(p2[:, 0:TT], cstbf[:, BD64, :], a[:], start=True, stop=True)], reads=[a_sq, cstbf], writes=[p2])
            op("scalar", lambda e, a=a_rs, p2=p2: e.activation(out=a[:], in_=p2[:, 0:TT], func=AF.Ln, bias=EPS, scale=1.0 / 64), reads=[p2], writes=[a_rs])
            op("scalar", lambda e, a=a_rs: e.activation(out=a[:], in_=a[:], func=AF.Exp, scale=-0.5), reads=[a_rs], writes=[a_rs])
            op("vector", lambda e, a=a_qn, b=a_q0, c=a_rs, gcol=gcol: e.scalar_tensor_tensor(out=a[:], in0=b[:], scalar=gqk[:, gcol:gcol + 1], in1=c[:], op0=ALU.mult, op1=ALU.mult), reads=[a_q0, a_rs, gqk], writes=[a_qn])

        def a_stage3(i):
            t, cc = itemsA[i]
            isq = cc < 8
            rt = rq[t % 2] if isq else rk[t % 2]
            p3 = psB[2 + i % 2]
            a_qn, a_t1, a_t2 = qn[i % 3], t1b[i % 2], t2b[i % 2]
            mm([lambda e, a=a_qn, p3=p3: e.matmul(p3[:, 0:TT], cstbf[:, RPERM, :], a[:], start=True, stop=True)], reads=[a_qn, cstbf], writes=[p3])
            op("gpsimd", lambda e, a=a_t1, b=a_qn, rt=rt: e.tensor_tensor(out=a[:], in0=b[:], in1=rt[:, 0, :], op=ALU.mult), reads=[a_qn, rt], writes=[a_t1])
            op("vector", lambda e, a=a_t2, p3=p3, rt=rt: e.tensor_tensor(out=a[:], in0=p3[:, 0:TT], in1=rt[:, 1, :], op=ALU.mult), reads=[p3, rt], writes=[a_t2])
            sg = next_stg()
            op("gpsimd", lambda e, sg=sg, a=a_t1, b=a_t2: e.tensor_tensor(out=sg[:, 0:TT], in0=a[:], in1=b[:], op=ALU.add), reads=[a_t1, a_t2], writes=[sg])
            if isq:
                dst = QT[cc * 128:(cc + 1) * 128, t * TT:(t + 1) * TT]
            else:
                dst = KT[(cc - 8) * 128:(cc - 7) * 128, t * TT:(t + 1) * TT]
            dma(dst, sg[:, 0:TT], reads=[sg], slot=sg)

        NA = len(itemsA)
        for i in range(NA + 2):
            if i < NA:
                a_stage1(i)
            if 0 <= i - 1 < NA:
                a_stage2(i - 1)
            if 0 <= i - 2 < NA:
                a_stage3(i - 2)

        sync_stores(stg)
        phase_end(markA)
        markB = phase_scope()
        sg32 = [P.sb("sg32_%d" % i, [128, 512], F32) for i in range(2)]
        fmB = [("az", O_AZ, 8, ZAT, 10), ("mg", O_MG, 16, GAT, 34)]
        it = 0
        for (kind, cbase, nch, dstT, bidx) in fmB:
            for s0 in range(0, nch, 4):
                nchs = min(4, nch - s0)
                wb_ = load_slab(cbase + s0 * 128, nchs * 128)
                for cc in range(nchs):
                    gch = bidx + s0 + cc
                    for t in range(NT):
                        pp = next_psA()
                        fm_matmul(pp, wb_, cc, t)
                        sg = next_stg()
                        if kind == "az":
                            op("scalar", lambda e, sg=sg, pp=pp, gch=gch: e.activation(out=sg[:, 0:TT], in_=pp[:, 0:TT], func=AF.Silu, bias=biasT[:, gch:gch + 1], scale=1.0), reads=[pp, biasT], writes=[sg])
                        else:
                            tmp = sg32[it % 2]
                            it += 1
                            op("scalar", lambda e, tmp=tmp, pp=pp, gch=gch: e.activation(out=tmp[:, 0:TT], in_=pp[:, 0:TT], func=AF.Tanh, bias=hbias[:, gch:gch + 1], scale=0.5), reads=[pp, hbias], writes=[tmp])
                            op("vector", lambda e, sg=sg, tmp=tmp: e.tensor_scalar(out=sg[:, 0:TT], in0=tmp[:, 0:TT], scalar1=0.5, scalar2=0.5, op0=ALU.mult, op1=ALU.add), reads=[tmp], writes=[sg])
                        r0 = (s0 + cc) * 128
                        dma(dstT[r0:r0 + 128, t * TT:(t + 1) * TT], sg[:, 0:TT], reads=[sg], slot=sg)

        sync_stores(stg)
        phase_end(markB)
        markC = phase_scope()
        rawb = [[P.sb("raw%d_%d" % (r_, s), [128, SEGT + 2], BF16) for s in range(NSEG)] for r_ in range(2)]
        dg = P.sb("dg", [128, 16, 3, 128], BF16)
        cvu = [P.sb("cvu%d" % i, [128, TT], F32) for i in range(3)]
        cvo = [P.sb("cvo%d" % i, [128, TT], BF16) for i in range(4)]
        for ch in range(16):
            for j in range(3):
                eng = "vector" if (ch * 3 + j) % 2 == 0 else "gpsimd"
                op(eng, lambda e, ch=ch, j=j: e.tensor_scalar(out=dg[:, ch, j, :], in0=cst32[:, IDENT, :], scalar1=cw[:, ch, j:j + 1], scalar2=None, op0=ALU.mult), reads=[cst32, cw], writes=[dg])
        for r_ in range(2):
            for s in range(NSEG):
                op("gpsimd", lambda e, s=s, r_=r_: e.memset(rawb[r_][s][:, 0:1], 0.0), writes=[rawb[r_][s]])
                op("gpsimd", lambda e, s=s, r_=r_: e.memset(rawb[r_][s][:, SEGT + 1:SEGT + 2], 0.0), writes=[rawb[r_][s]])
        slabC = {}

        def c_main(ch):
            s0, cc = (ch // 4) * 4, ch % 4
            if cc == 0:
                slabC[s0] = load_slab(O_MQ + s0 * 128, 512)
            wb_ = slabC[s0]
            raw = rawb[ch % 2]
            gch = 18 + ch
            for t in range(NT):
                s = t // cfg.TPS
                tl = t % cfg.TPS
                pp = next_psA()
                fm_matmul(pp, wb_, cc, t)
                op("scalar", lambda e, s=s, tl=tl, pp=pp, gch=gch, raw=raw: e.activation(out=raw[s][:, 1 + tl * TT:1 + (tl + 1) * TT], in_=pp[:, 0:TT], func=AF.Identity, bias=biasT[:, gch:gch + 1], scale=1.0), reads=[pp, biasT], writes=[raw[s]])
            for b in range(NSEG - 1):
                op("vector", lambda e, b=b, raw=raw: e.tensor_scalar(out=raw[b][:, SEGT + 1:SEGT + 2], in0=raw[b + 1][:, 1:2], scalar1=flg[:, b:b + 1], scalar2=None, op0=ALU.mult), reads=[raw[b + 1], flg], writes=[raw[b]])
                op("vector", lambda e, b=b, raw=raw: e.tensor_scalar(out=raw[b + 1][:, 0:1], in0=raw[b][:, SEGT:SEGT + 1], scalar1=flg[:, b:b + 1], scalar2=None, op0=ALU.mult), reads=[raw[b], flg], writes=[raw[b + 1]])

        cctr = [0]

        def c_conv(ch):
            raw = rawb[ch % 2]
            dstT = MQT if ch < 8 else MKT
            r0 = (ch % 8) * 128
            inv_s = 1.0 if ch < 8 else 1.0 / KS
            for t in range(NT):
                s = t // cfg.TPS
                tl = t % cfg.TPS
                cctr[0] += 1
                pc = psB[cctr[0] % 2]
                u = cvu[cctr[0] % 3]
                o = cvo[cctr[0] % 4]
                rw = raw[s]
                mm([lambda e, pc=pc, ch=ch, j=j, rw=rw, tl=tl: e.matmul(pc[:, 0:TT], dg[:, ch, j, :], rw[:, tl * TT + j:tl * TT + j + TT], start=(j == 0), stop=(j == 2)) for j in range(3)],
                   reads=[rw, dg], writes=[pc])
                op("scalar", lambda e, u=u, pc=pc, inv_s=inv_s: e.activation(out=u[:], in_=pc[:, 0:TT], func=AF.Tanh, scale=inv_s), reads=[pc], writes=[u])
                op("vector", lambda e, u=u, pc=pc, o=o: e.scalar_tensor_tensor(out=o[:], in0=u[:], scalar=1.0, in1=pc[:, 0:TT], op0=ALU.add, op1=ALU.mult), reads=[u, pc], writes=[o])
                dma(dstT[r0:r0 + 128, t * TT:(t + 1) * TT], o[:], reads=[o], slot=o)

        c_main(0)
        for ch in range(16):
            if ch + 1 < 16:
                c_main(ch + 1)
            c_conv(ch)
        sync_stores(cvo)
        phase_end(markC)
        bbc = [P.sb("bbc%d" % i, [128, 512], F32) for i in range(2)]
        tm32 = [P.sb("tm32_%d" % i, [128, 512], F32) for i in range(2)]
        tmo = [P.sb("tmo_%d" % i, [128, 512], BF16) for i in range(3)]
        g32 = [P.sb("g32_%d" % i, [128, 32], F32) for i in range(2)]
        tmD = [("v", O_AV, 256, VA, 0), ("v", O_MV, 512, MV, 0), ("v", O_MV + 512, 512, MV, 512),
               ("o", O_MO, 512, MO, 0), ("o", O_MO + 512, 512, MO, 512),
               ("z", O_MZ, 512, MZ, 0), ("z", O_MZ + 512, 512, MZ, 512), ("g", O_G4, 32, GTS, 0)]
        it = 0
        for si, (kind, c0, ncols, dstD, dc0) in enumerate(tmD):
            wb_ = load_slab(c0, ncols)
            bb = bbc[si % 2]
            dma(bb[:, 0:ncols], bass.AP(b_in_d.tensor, l * IN_DIM + c0, [[0, 128], [1, ncols]]), writes=[bb], slot=bb)
            for blk in range(NBLK):
                t = blk // TB
                pp = next_psA()
                mm([lambda e, kc=kc, pp=pp, blk=blk, ncols=ncols, wb_=wb_: e.matmul(pp[:, 0:ncols], xn[:, kc, blk * 128:(blk + 1) * 128], wb_[:, kc, 0:ncols], start=(kc == 0), stop=(kc == 7)) for kc in range(8)],
                   reads=[wb_, xn_tiles[t]], writes=[pp])
                r0 = blk * 128
                if kind == "v":
                    o_ = tmo[it % 3]
                    it += 1
                    op("vector", lambda e, o_=o_, pp=pp, bb=bb, ncols=ncols: e.tensor_tensor(out=o_[:, 0:ncols], in0=pp[:, 0:ncols], in1=bb[:, 0:ncols], op=ALU.add), reads=[pp, bb], writes=[o_])
                    dma(dstD[r0:r0 + 128, dc0:dc0 + ncols], o_[:, 0:ncols], reads=[o_], slot=o_)
                elif kind == "o":
                    o_, tmp = tmo[it % 3], tm32[it % 2]
                    it += 1
                    op("vector", lambda e, tmp=tmp, pp=pp, bb=bb, ncols=ncols: e.tensor_tensor(out=tmp[:, 0:ncols], in0=pp[:, 0:ncols], in1=bb[:, 0:ncols], op=ALU.add), reads=[pp, bb], writes=[tmp])
                    op("scalar", lambda e, tmp=tmp, ncols=ncols: e.activation(out=tmp[:, 0:ncols], in_=tmp[:, 0:ncols], func=AF.Tanh, scale=0.5), reads=[tmp], writes=[tmp])
                    op("gpsimd", lambda e, tmp=tmp, o_=o_, ncols=ncols: e.tensor_scalar(out=o_[:, 0:ncols], in0=tmp[:, 0:ncols], scalar1=0.5, scalar2=0.5, op0=ALU.mult, op1=ALU.add), reads=[tmp], writes=[o_])
                    dma(dstD[r0:r0 + 128, dc0:dc0 + ncols], o_[:, 0:ncols], reads=[o_], slot=o_)
                elif kind == "z":
                    o_, tmp = tmo[it % 3], tm32[it % 2]
                    it += 1
                    op("vector", lambda e, tmp=tmp, pp=pp, bb=bb, ncols=ncols: e.tensor_tensor(out=tmp[:, 0:ncols], in0=pp[:, 0:ncols], in1=bb[:, 0:ncols], op=ALU.add), reads=[pp, bb], writes=[tmp])
                    op("scalar", lambda e, tmp=tmp, o_=o_, ncols=ncols: e.activation(out=o_[:, 0:ncols], in_=tmp[:, 0:ncols], func=AF.Silu), reads=[tmp], writes=[o_])
                    dma(dstD[r0:r0 + 128, dc0:dc0 + ncols], o_[:, 0:ncols], reads=[o_], slot=o_)
                else:
                    o_ = g32[it % 2]
                    it += 1
                    op("vector", lambda e, o_=o_, pp=pp, bb=bb: e.tensor_tensor(out=o_[:], in0=pp[:, 0:32], in1=bb[:, 0:32], op=ALU.add), reads=[pp, bb], writes=[o_])
                    dma(dstD[r0:r0 + 128, 0:32], o_[:], reads=[o_], slot=o_)
        sync_stores(tmo + g32)
        phase_end(mark)

        mark = phase_scope()
        HW_ = TT + 256
        Qs = [P.sb("Qs%d" % i, [64, 16, TT], BF16) for i in range(2)]
        Ks = [P.sb("Ks%d" % i, [64, 4, HW_], BF16) for i in range(2)]
        Vs = [P.sb("Vs%d" % i, [128, TB + 2, 256], BF16) for i in range(2)]
        Zs = [P.sb("Zs%d" % i, [64, 16, TT], BF16) for i in range(2)]
        AGs = [P.sb("AGs%d" % i, [64, 16, TT], BF16) for i in range(2)]
        pT = [P.sb("pT%d" % i, [128, 512], BF16) for i in range(9)]
        lnd = [P.sb("lnd%d" % i, [64, 512], F32) for i in range(2)]
        zr = [P.sb("zr%d" % i, [64, 512], F32) for i in range(2)]
        skr = P.sb("skr", [2, 16], F32)
        ske = P.sb("ske", [2, 16], F32)
        skhl = P.sb("skhl", [2, 16], BF16)
        sktmp = P.sb("sktmp", [2, 16], F32)
        skrow = P.sb("skrow", [2, 16, 128], BF16)
        ones2 = P.sb("ones2", [2, 64], BF16)
        psS = [P.ps("psS%d" % i, [128, 512]) for i in range(4)]
        psO = [P.ps("psO%d" % i, [64, 512]) for i in range(2)]
        psD = [P.ps("psD%d" % i, [64, 512]) for i in range(2)]
        dma(skr[:], bass.AP(sink_d.tensor, l * 16, [[0, 2], [1, 16]]), writes=[skr], slot=skr)
        op("scalar", lambda e: e.activation(out=ske[:], in_=skr[:], func=AF.Exp), reads=[skr], writes=[ske])
        op("vector", lambda e: e.tensor_copy(out=skhl[:], in_=ske[:]), reads=[ske], writes=[skhl])
        op("vector", lambda e: e.tensor_tensor(out=sktmp[:], in0=ske[:], in1=skhl[:], op=ALU.subtract), reads=[ske, skhl], writes=[sktmp])
        op("vector", lambda e: e.tensor_copy(out=skhl[:], in_=sktmp[:]), reads=[sktmp], writes=[skhl])
        op("vector", lambda e: e.tensor_copy(out=skhl[0:1, :], in_=ske[0:1, :]), reads=[ske], writes=[skhl])
        op("vector", lambda e: e.tensor_copy(out=skrow[:], in_=bass.AP(skhl.t, 0, [[16, 2], [1, 16], [0, 128]])), reads=[skhl], writes=[skrow])
        op("vector", lambda e: e.memset(ones2[:], 1.0), writes=[ones2])
        sctr = [0]
        pctr = [0]
        for t in range(NT):
            i2 = t % 2
            Qb, Kb, Vb, Zb, Ab = Qs[i2], Ks[i2], Vs[i2], Zs[i2], AGs[i2]
            b0 = t * TB
            lo_blk = max(b0 - 1, 0)
            hi_blk = min(b0 + TB + 1, NBLK)
            dma(Qb[:], QT[:, t * TT:(t + 1) * TT].rearrange("(h d) t -> d h t", d=64), writes=[Qb], slot=Qb)
            dma(Zb[:], ZAT[:, t * TT:(t + 1) * TT].rearrange("(h d) t -> d h t", d=64), writes=[Zb], slot=Zb)
            ko = (lo_blk - (b0 - 1)) * 128
            dma(Kb[:, :, ko:ko + (hi_blk - lo_blk) * 128], KT[:, lo_blk * 128:hi_blk * 128].rearrange("(j d) t -> d j t", d=64), writes=[Kb], slot=Kb)
            vo = lo_blk - (b0 - 1)
            dma(Vb[:, vo:vo + (hi_blk - lo_blk), :], VA[lo_blk * 128:hi_blk * 128, :].rearrange("(n p) c -> p n c", p=128), writes=[Vb], slot=Vb)
            items2 = []
            for nb in range(TB):
                n = b0 + nb
                seg, nis = n // SEGB, n % SEGB
                kbs = []
                if n > 0:
                    if nis > 0:
                        kbs.append((nb, 0))
                    else:
                        kbs.append((nb, 2 + 2 * (seg - 1)))
                kbs.append((nb + 1, None))
                if n < NBLK - 1:
                    if nis < SEGB - 1:
                        kbs.append((nb + 2, 1))
                    else:
                        kbs.append((nb + 2, 3 + 2 * seg))
                for j in range(4):
                    items2.append((nb, j, kbs))
            ptsd = {}

            def s1(k, Kb=Kb, Qb=Qb):
                nb, j, kbs = items2[k]
                pts = []
                for (ks, mi) in kbs:
                    sctr[0] += 1
                    pS = psS[sctr[0] % 4]
                    pctr[0] += 1
                    pt = pT[pctr[0] % 9]
                    mm([lambda e, pS=pS, ks=ks, j=j, nb=nb, Kb=Kb, Qb=Qb: e.matmul(pS[:], Kb[:, j, ks * 128:(ks + 1) * 128], Qb[:, 4 * j:4 * j + 4, nb * 128:(nb + 1) * 128], start=True, stop=True)],
                       reads=[Kb, Qb], writes=[pS])
                    op("scalar", lambda e, pS=pS, pt=pt: e.activation(out=pt[:], in_=pS[:], func=AF.Exp), reads=[pS], writes=[pt])
                    if mi is not None:
                        eng = "gpsimd" if mi % 2 == 0 else "vector"
                        op(eng, lambda e, pt=pt, mi=mi: e.tensor_tensor(out=pt[:], in0=pt[:], in1=amask[:, mi, :], op=ALU.mult), reads=[pt, amask], writes=[pt])
                    pts.append((pt, ks))
                ptsd[k] = pts

            def s2(k, Vb=Vb, Zb=Zb, Ab=Ab):
                nb, j, kbs = items2[k]
                pts = ptsd.pop(k)
                pO, pD = psO[k % 2], psD[k % 2]
                nk = len(pts)
                mm([lambda e, pO=pO, pt=pt, ks=ks, j=j, i=i, nk=nk, Vb=Vb: e.matmul(pO[:], Vb[:, ks, j * 64:(j + 1) * 64], pt[:], start=(i == 0), stop=(i == nk - 1)) for i, (pt, ks) in enumerate(pts)],
                   reads=[Vb] + [p[0] for p in pts], writes=[pO])
                mm([lambda e, pD=pD, pt=pt, i=i: e.matmul(pD[:], cstbf[:, ONES, 0:64], pt[:], start=(i == 0), stop=False) for i, (pt, ks) in enumerate(pts)]
                   + [lambda e, pD=pD, j=j: e.matmul(pD[:], ones2[:], skrow[:, 4 * j:4 * j + 4, :], start=False, stop=True)],
                   reads=[cstbf, ones2, skrow] + [p[0] for p in pts], writes=[pD])
                ld_, zr_ = lnd[k % 2], zr[k % 2]
                op("scalar", lambda e, ld_=ld_, pD=pD: e.activation(out=ld_[:], in_=pD[:], func=AF.Ln), reads=[pD], writes=[ld_])
                op("scalar", lambda e, ld_=ld_: e.activation(out=ld_[:], in_=ld_[:], func=AF.Exp, scale=-1.0), reads=[ld_], writes=[ld_])
                op("vector", lambda e, zr_=zr_, ld_=ld_, j=j, nb=nb, Zb=Zb: e.tensor_tensor(out=zr_[:].rearrange("p (g q) -> p g q", g=4), in0=ld_[:].rearrange("p (g q) -> p g q", g=4), in1=Zb[:, 4 * j:4 * j + 4, nb * 128:(nb + 1) * 128], op=ALU.mult), reads=[ld_, Zb], writes=[zr_])
                op("vector", lambda e, zr_=zr_, pO=pO, j=j, nb=nb, Ab=Ab: e.tensor_tensor(out=Ab[:, 4 * j:4 * j + 4, nb * 128:(nb + 1) * 128], in0=pO[:].rearrange("p (g q) -> p g q", g=4), in1=zr_[:].rearrange("p (g q) -> p g q", g=4), op=ALU.mult), reads=[pO, zr_], writes=[Ab])

            NI = len(items2)
            s1(0)
            for k in range(NI):
                if k + 1 < NI:
                    s1(k + 1)
                s2(k)
            dma(AGT[:, t * TT:(t + 1) * TT].rearrange("(h d) t -> d h t", d=64), Ab[:], reads=[Ab], slot=Ab)
        sync_stores(AGs)
        phase_end(mark)

        mark = phase_scope()
        G = P.sb("G", [128, NBLK, 32], F32)
        SP_ = P.sb("SP", [128, 2, NBLK, 8], F32)
        EA = P.sb("EA", [128, 2, NBLK, 8], F32)
        EB = P.sb("EB", [128, 2, NBLK, 8], F32)
        EBT = P.sb("EBT", [128, 2, NBLK, 8], F32)
        WK = P.sb("WK", [128, 2, NBLK, 8], F32)
        dma(G[:], GTS.rearrange("(n p) c -> p n c", p=128), writes=[G], slot=G)
        NG = NBLK * 8
        GCH = 384 // 8
        psT = [P.ps("psT%d" % i, [128, 512]) for i in range(1)]
        psK = P.ps("psK", [128, 512])
        psG = [psT[0], psK]
        for d_ in range(2):
            fcol = 8 + 16 * d_
            icol = 16 * d_
            op("scalar", lambda e, d_=d_, fcol=fcol: e.activation(out=SP_[:, d_, :, :], in_=G[:, :, fcol:fcol + 8], func=AF.Exp, scale=-1.0), reads=[G], writes=[SP_])
            op("scalar", lambda e, d_=d_: e.activation(out=SP_[:, d_, :, :], in_=SP_[:, d_, :, :], func=AF.Ln, bias=1.0, scale=1.0), reads=[SP_], writes=[SP_])
            tri = TRIU if d_ == 0 else TRIL
            for c0 in range(0, NBLK, GCH):
                nbk = min(GCH, NBLK - c0)
                pg, pt_ = psG[0], psG[1]
                mm([lambda e, pg=pg, c0=c0, nbk=nbk, d_=d_, tri=tri: e.matmul(pg[:, 0:nbk * 8], cst32[:, tri, :], SP_[:, d_, c0:c0 + nbk, :], start=True, stop=True)], reads=[cst32, SP_], writes=[pg])
                mm([lambda e, pt_=pt_, c0=c0, nbk=nbk, d_=d_: e.matmul(pt_[:, 0:nbk * 8], cst32[:, ONES, :], SP_[:, d_, c0:c0 + nbk, :], start=True, stop=True)], reads=[cst32, SP_], writes=[pt_])
                op("vector", lambda e, pg=pg, c0=c0, nbk=nbk, d_=d_, icol=icol: e.tensor_tensor(out=EA[:, d_, c0:c0 + nbk, :], in0=G[:, c0:c0 + nbk, icol:icol + 8], in1=pg[:, 0:nbk * 8].rearrange("p (n h) -> p n h", h=8), op=ALU.add), reads=[G, pg], writes=[EA])
                op("scalar", lambda e, c0=c0, nbk=nbk, d_=d_: e.activation(out=EA[:, d_, c0:c0 + nbk, :], in_=EA[:, d_, c0:c0 + nbk, :], func=AF.Exp), reads=[EA], writes=[EA])
                op("scalar", lambda e, pg=pg, c0=c0, nbk=nbk, d_=d_: e.activation(out=EB[:, d_, c0:c0 + nbk, :], in_=pg[:, 0:nbk * 8].rearrange("p (n h) -> p n h", h=8), func=AF.Exp, scale=-1.0), reads=[pg], writes=[EB])
                op("scalar", lambda e, pt_=pt_, c0=c0, nbk=nbk, d_=d_: e.activation(out=EBT[:, d_, c0:c0 + nbk, :], in_=pt_[:, 0:nbk * 8].rearrange("p (n h) -> p n h", h=8), func=AF.Exp, scale=-1.0), reads=[pt_], writes=[EBT])
                op("vector", lambda e, c0=c0, nbk=nbk, d_=d_: e.tensor_tensor(out=WK[:, d_, c0:c0 + nbk, :], in0=EA[:, d_, c0:c0 + nbk, :], in1=EBT[:, d_, c0:c0 + nbk, :], op=ALU.mult), reads=[EA, EBT], writes=[WK])

        WDT = 132
        SW = 160
        GR = [(0, 3), (3, 3), (6, 2)]
        Cst = P.sb("Cst", [128, 8, 128], F32)
        Cbf2 = [P.sb("Cbf%d" % i, [128, 8, WDT], BF16) for i in range(2)]
        n8 = P.sb("n8", [128, 8], F32)
        Cg = [Buf(Cst.t) for _ in GR]
        TQ = [P.sb("TQ%d" % i, [128, 8, TT], BF16) for i in range(2)]
        TK = [P.sb("TK%d" % i, [128, 8, TT], BF16) for i in range(2)]
        TV = P.sb("TV", [128, TB, 8, 128], BF16)
        TV1 = [P.sb("TV1_%d" % i, [128, TB, 8, WDT], BF16) for i in range(2)]
        TO = [P.sb("TO%d" % i, [128, TB, 1024], BF16) for i in range(2)]
        TZ = [P.sb("TZ%d" % i, [128, TB, 1024], BF16) for i in range(2)]
        THB = [P.sb("THB%d" % i, [128, TB, 8, 128], F32) for i in range(2)]
        hs_ = [P.sb("hs%d" % i, [128, 8, 128], F32) for i in range(2)]
        hq_ = [P.sb("hq%d" % i, [128, 8, 128], F32) for i in range(2)]
        hn_ = [P.sb("hn%d" % i, [128, 1024], BF16) for i in range(2)]
        p8 = [P.sb("p8_%d" % i, [128, 8, 128], BF16) for i in range(3)]
        k8 = [P.sb("k8_%d" % i, [128, 8, 128], BF16) for i in range(3)]
        rr = [P.sb("rr%d" % i, [128, 8], F32) for i in range(2)]
        rt_ = [P.sb("rt%d" % i, [128, 8], F32) for i in range(2)]
        ss8 = [P.sb("ss8_%d" % i, [128, 8], F32) for i in range(2)]
        psX = [P.ps("psX%d" % i, [128, 512]) for i in range(3)]
        psC = [P.ps("psC%d" % i, [128, 512]) for i in range(3)]
        mqv = MQT.rearrange("(h d) t -> d h t", d=128)
        mkv = MKT.rearrange("(h d) t -> d h t", d=128)
        cur = [0]
        for i in range(2):
            op("gpsimd", lambda e, i=i: e.memset(TV1[i][:, :, :, 128:WDT], 1.0), writes=[TV1[i]])

        def grp(ps, nh, lo, hi):
            return ps[:, 0:nh * SW].rearrange("p (a b) -> p a b", b=SW)[:, :, lo:hi]

        def n_to_cbf(c):
            op("vector", lambda e, c=c: e.tensor_copy(out=Cbf2[c][:, :, 128:129], in_=bass.AP(n8.t, 0, [[8, 128], [1, 8], [1, 1]])), reads=[n8], writes=[Cbf2[c]])

        def reset_state():
            c = cur[0]
            for g in range(3):
                h0, nh = GR[g]
                op("gpsimd", lambda e, h0=h0, nh=nh: e.memset(Cst[:, h0:h0 + nh, :], 0.0), writes=[Cg[g]])
            op("gpsimd", lambda e, c=c: e.memset(Cbf2[c][:], 0.0), writes=[Cbf2[c]])
            op("vector", lambda e: e.memset(n8[:], 0.0), writes=[n8])

        def link_state(b):
            c = cur[0]
            fl = flg[:, b:b + 1]
            for g in range(3):
                h0, nh = GR[g]
                op("gpsimd", lambda e, h0=h0, nh=nh: e.tensor_scalar(out=Cst[:, h0:h0 + nh, :], in0=Cst[:, h0:h0 + nh, :], scalar1=fl, scalar2=None, op0=ALU.mult), reads=[flg, Cg[g]], writes=[Cg[g]])
                op("scalar", lambda e, h0=h0, nh=nh, c=c: e.activation(out=Cbf2[c][:, h0:h0 + nh, 0:128], in_=Cst[:, h0:h0 + nh, :], func=AF.Copy), reads=[Cg[g]], writes=[Cbf2[c]])
            op("vector", lambda e: e.tensor_scalar(out=n8[:], in0=n8[:], scalar1=fl, scalar2=None, op0=ALU.mult), reads=[flg, n8], writes=[n8])
            n_to_cbf(c)

        seq = [(1, n) for n in range(NBLK - 1, -1, -1)] + [(0, n) for n in range(NBLK)]

        def ptile(i):
            return i // TB

        def tile_blocks(pt):
            d_ = 1 if pt < NT else 0
            tt = (NT - 1 - pt) if d_ == 1 else (pt - NT)
            return d_, tt

        def loads_tile(pt):
            d_, tt = tile_blocks(pt)
            q_, k_, v1 = TQ[pt % 2], TK[pt % 2], TV1[pt % 2]
            dma(q_[:], mqv[:, :, tt * TT:(tt + 1) * TT], writes=[q_], slot=q_)
            dma(k_[:], mkv[:, :, tt * TT:(tt + 1) * TT], writes=[k_], slot=k_)
            dma(TV[:], MV[tt * TT:(tt + 1) * TT, :].rearrange("(b p) (h e) -> p b h e", p=128, h=8), writes=[TV], slot=TV)
            op("gpsimd", lambda e, v1=v1: e.tensor_copy(out=v1[:, :, :, 0:128], in_=TV[:]), reads=[TV], writes=[v1])
            if d_ == 0 and pt > NT:
                loadsB_tile(pt)

        def loadsB_tile(pt):
            d_, tt = tile_blocks(pt)
            o_b, z_b, h_b = TO[pt % 2], TZ[pt % 2], THB[pt % 2]
            dma(o_b[:], MO[tt * TT:(tt + 1) * TT, :].rearrange("(b p) c -> p b c", p=128), writes=[o_b], slot=o_b)
            dma(z_b[:], MZ[tt * TT:(tt + 1) * TT, :].rearrange("(b p) c -> p b c", p=128), writes=[z_b], slot=z_b)
            dma(h_b[:], HB[tt * TT:(tt + 1) * TT, :].rearrange("(b p) (h e) -> p b h e", p=128, h=8), writes=[h_b], slot=h_b)

        def stageA1(i):
            d_, n = seq[i]
            i3 = i % 3
            tri = TRIU if d_ == 0 else TRIL
            pt2 = ptile(i) % 2
            nbk = n % TB
            bs = slice(nbk * 128, (nbk + 1) * 128)
            q_, k_ = TQ[pt2], TK[pt2]
            pt_, kt_ = p8[i3], k8[i3]
            for hf in range(2):
                pst, psk = psT[0], psK
                for hh in range(4):
                    h = 4 * hf + hh
                    mm([lambda e, pst=pst, hh=hh, h=h, k_=k_, q_=q_, bs=bs: e.matmul(pst[:, hh * 128:(hh + 1) * 128], k_[:, h, bs], q_[:, h, bs], start=True, stop=True)], reads=[k_, q_], writes=[pst])
                for hh in range(4):
                    h = 4 * hf + hh
                    mm([lambda e, psk=psk, hh=hh, h=h, k_=k_, bs=bs: e.matmul(psk[:, hh * 128:(hh + 1) * 128], k_[:, h, bs], cstbf[:, IDENT, :], start=True, stop=True)], reads=[k_, cstbf], writes=[psk])
                for hh in range(4):
                    h = 4 * hf + hh
                    op("vector", lambda e, pt_=pt_, pst=pst, hh=hh, h=h, n=n, d_=d_, tri=tri: e.scalar_tensor_tensor(out=pt_[:, h, :], in0=pst[:, hh * 128:(hh + 1) * 128], scalar=EA[:, d_, n, h:h + 1], in1=cst32[:, tri, :], op0=ALU.mult, op1=ALU.mult), reads=[pst, EA, cst32], writes=[pt_])
                    op("scalar", lambda e, kt_=kt_, psk=psk, hh=hh, h=h, n=n, d_=d_: e.activation(out=kt_[:, h, :], in_=psk[:, hh * 128:(hh + 1) * 128], func=AF.Copy, scale=WK[:, d_, n, h:h + 1]), reads=[psk, WK], writes=[kt_])

        def stageA2(i):
            d_, n = seq[i]
            i3 = i % 3
            v1 = TV1[ptile(i) % 2]
            nbk = n % TB
            kt_ = k8[i3]
            for g in range(3):
                h0, nh = GR[g]
                pc = psC[g]
                for j in range(nh):
                    h = h0 + j
                    mm([lambda e, pc=pc, j=j, h=h, kt_=kt_, v1=v1, nbk=nbk: e.matmul(pc[:, j * SW:j * SW + 129], kt_[:, h, :], v1[:, nbk, h, 0:129], start=True, stop=True)], reads=[kt_, v1], writes=[pc])

        cidx = {}

        def stageB_update(i):
            d_, n = seq[i]
            i2, i3 = i % 2, i % 3
            seg, nis = n // SEGB, n % SEGB
            pt2 = ptile(i) % 2
            nbk = n % TB
            bs = slice(nbk * 128, (nbk + 1) * 128)
            q_, v1 = TQ[pt2], TV1[pt2]
            pt_ = p8[i3]
            first_in_pass = (n == NBLK - 1) if d_ == 1 else (n == 0)
            first_in_seg = (nis == SEGB - 1) if d_ == 1 else (nis == 0)
            if first_in_pass:
                reset_state()
            elif first_in_seg:
                link_state(seg if d_ == 1 else seg - 1)
            c = cur[0]
            nx = 1 - c
            Cb = Cbf2[c]
            op("vector", lambda e, n=n, d_=d_: e.tensor_tensor(out=n8[:], in0=n8[:], in1=EBT[:, d_, n, :], op=ALU.mult), reads=[n8, EBT], writes=[n8])
            for g in range(3):
                h0, nh = GR[g]
                pc = psC[g]
                for j in range(nh):
                    h = h0 + j
                    op("vector", lambda e, h=h, j=j, pc=pc, n=n, d_=d_: e.scalar_tensor_tensor(out=Cst[:, h, :], in0=Cst[:, h, :], scalar=EBT[:, d_, n, h:h + 1], in1=pc[:, j * SW:j * SW + 128], op0=ALU.mult, op1=ALU.add), reads=[pc, Cg[g], EBT], writes=[Cg[g]])
                op("vector", lambda e, h0=h0, nh=nh, pc=pc: e.tensor_tensor(out=bass.AP(n8.t, h0, [[8, 128], [1, nh], [1, 1]]), in0=bass.AP(n8.t, h0, [[8, 128], [1, nh], [1, 1]]), in1=grp(pc, nh, 128, 129), op=ALU.add), reads=[n8, pc], writes=[n8])
                op("scalar", lambda e, h0=h0, nh=nh, nx=nx: e.activation(out=Cbf2[nx][:, h0:h0 + nh, 0:128], in_=Cst[:, h0:h0 + nh, :], func=AF.Copy), reads=[Cg[g]], writes=[Cbf2[nx]])
            n_to_cbf(nx)
            cur[0] = nx
            cidx[i] = c

        def stageB(i):
            d_, n = seq[i]
            i2, i3 = i % 2, i % 3
            pt2 = ptile(i) % 2
            nbk = n % TB
            bs = slice(nbk * 128, (nbk + 1) * 128)
            q_, v1 = TQ[pt2], TV1[pt2]
            pt_ = p8[i3]
            c = cidx[i]
            Cb = Cbf2[c]
            for g in range(3):
                h0, nh = GR[g]
                psx = psX[g]
                for j in range(nh):
                    h = h0 + j
                    mm([lambda e, psx=psx, j=j, h=h, q_=q_, Cb=Cb, bs=bs: e.matmul(psx[:, j * SW:j * SW + 129], q_[:, h, bs], Cb[:, h, 0:129], start=True, stop=False),
                        lambda e, psx=psx, j=j, h=h, pt_=pt_, v1=v1, nbk=nbk: e.matmul(psx[:, j * SW:j * SW + 129], pt_[:, h, :], v1[:, nbk, h, 0:129], start=False, stop=True)],
                       reads=[q_, Cb, pt_, v1], writes=[psx])
            hq = hq_[i2]
            r_, t_ = rr[i2], rt_[i2]
            for g in range(3):
                h0, nh = GR[g]
                psx = psX[g]
                op("scalar", lambda e, h0=h0, nh=nh, psx=psx, t_=t_: e.activation(out=bass.AP(t_.t, h0, [[8, 128], [1, nh], [1, 1]]), in_=grp(psx, nh, 128, 129), func=AF.Abs), reads=[psx], writes=[t_])
                op("scalar", lambda e, h0=h0, nh=nh, psx=psx, hq=hq: e.activation(out=hq[:, h0:h0 + nh, :], in_=grp(psx, nh, 0, 128), func=AF.Copy), reads=[psx], writes=[hq])
            op("vector", lambda e, t_=t_, n=n, d_=d_: e.tensor_tensor(out=t_[:], in0=t_[:], in1=EB[:, d_, n, :], op=ALU.mult), reads=[t_, EB], writes=[t_])
            op("vector", lambda e, t_=t_: e.tensor_scalar_max(out=t_[:], in0=t_[:], scalar1=1.0), reads=[t_], writes=[t_])
            op("vector", lambda e, t_=t_, r_=r_: e.reciprocal(out=r_[:], in_=t_[:]), reads=[t_], writes=[r_])
            op("vector", lambda e, r_=r_, n=n, d_=d_: e.tensor_tensor(out=r_[:], in0=r_[:], in1=EB[:, d_, n, :], op=ALU.mult), reads=[r_, EB], writes=[r_])
            r_bc = bass.AP(r_.t, 0, [[8, 128], [1, 8], [0, 128]])
            hs = hs_[i2]
            if d_ == 1:
                op("gpsimd", lambda e, hs=hs, hq=hq, r_bc=r_bc: e.tensor_tensor(out=hs[:], in0=hq[:], in1=r_bc, op=ALU.mult), reads=[hq, r_], writes=[hs])
                dma(HB[n * 128:(n + 1) * 128, :].rearrange("p (h e) -> p h e", h=8), hs[:], reads=[hs], slot=hs)
            else:
                o_b, z_b, h_b = TO[pt2], TZ[pt2], THB[pt2]
                hn = hn_[i2]
                s8 = ss8[i2]
                for h in range(8):
                    op("vector", lambda e, hs=hs, hq=hq, h_b=h_b, nbk=nbk, h=h, r_=r_: e.scalar_tensor_tensor(out=hs[:, h, :], in0=hq[:, h, :], scalar=r_[:, h:h + 1], in1=h_b[:, nbk, h, :], op0=ALU.mult, op1=ALU.add), reads=[hq, r_, h_b], writes=[hs])
                op("gpsimd", lambda e, hs=hs, o_b=o_b, nbk=nbk: e.tensor_tensor(out=hs[:], in0=hs[:], in1=o_b[:, nbk, :].rearrange("p (h e) -> p h e", h=8), op=ALU.mult), reads=[hs, o_b], writes=[hs])
                op("vector", lambda e, s8=s8: e.memset(s8[:], 0.0), writes=[s8])
                for h in range(8):
                    op("scalar", lambda e, hs=hs, hq=hq, h=h, s8=s8: e.activation(out=hq[:, h, :], in_=hs[:, h, :], func=AF.Square, accum_out=s8[:, h:h + 1]), reads=[hs, s8], writes=[hq, s8])
                op("scalar", lambda e, s8=s8: e.activation(out=s8[:], in_=s8[:], func=AF.Ln, bias=EPS, scale=1.0 / 128), reads=[s8], writes=[s8])
                op("scalar", lambda e, s8=s8: e.activation(out=s8[:], in_=s8[:], func=AF.Exp, scale=-0.5), reads=[s8], writes=[s8])
                s_bc = bass.AP(s8.t, 0, [[8, 128], [1, 8], [0, 128]])
                op("gpsimd", lambda e, hs=hs, s_bc=s_bc: e.tensor_tensor(out=hs[:], in0=hs[:], in1=s_bc, op=ALU.mult), reads=[hs, s8], writes=[hs])
                op("gpsimd", lambda e, hs=hs, hn=hn, z_b=z_b, nbk=nbk: e.tensor_tensor(out=hn[:].rearrange("p (h e) -> p h e", h=8), in0=hs[:], in1=z_b[:, nbk, :].rearrange("p (h e) -> p h e", h=8), op=ALU.mult), reads=[hs, z_b], writes=[hn])
                dma(HN[n * 128:(n + 1) * 128, :], hn[:], reads=[hn], slot=hn)

        NS = len(seq)
        loads_tile(0)
        stageA1(0)
        stageA2(0)
        for i in range(NS):
            if i % TB == 0 and ptile(i) + 1 < 2 * NT:
                loads_tile(ptile(i) + 1)
            stageB_update(i)
            if i + 1 < NS:
                stageA1(i + 1)
            stageB(i)
            if i + 1 < NS:
                if seq[i + 1][0] == 0 and seq[i][0] == 1:
                    sync_stores(hs_)
                    loadsB_tile(NT)
                stageA2(i + 1)
        sync_stores(hn_)
        phase_end(mark)

        mark = phase_scope()
        T4 = min(256, TT)
        NT4 = T // T4
        B4 = T4 // 128
        Wa = P.sb("Wa", [128, 8, D], BF16)
        Wm = P.sb("Wm", [128, 8, D], BF16)
        Wo = P.sb("Wo", [128, 8, D], BF16)
        mgT = P.sb("mgT", [128, 8], F32)
        dma(mgT[:], mgT_d[l], writes=[mgT], slot=mgT)
        w4s = [P.sb("w4s%d" % i, [128, 8, 512], F32) for i in range(2)]
        ci = 0
        for (Wd, Wsb, scaled) in ((wa_d, Wa, False), (wm_d, Wm, True), (wo_d, Wo, False)):
            wvv = Wd[l].rearrange("(kc p) c -> p kc c", p=128)
            for c0 in (0, 512):
                ws_ = w4s[ci % 2]
                ci += 1
                dma(ws_[:], wvv[:, :, c0:c0 + 512], writes=[ws_], slot=ws_)
                for kc in range(8):
                    eng = ("gpsimd", "vector", "scalar")[kc % 3]
                    if scaled:
                        if eng == "scalar":
                            op(eng, lambda e, kc=kc, ws_=ws_, Wsb=Wsb, c0=c0: e.activation(out=Wsb[:, kc, c0:c0 + 512], in_=ws_[:, kc, :], func=AF.Copy, scale=mgT[:, kc:kc + 1]), reads=[ws_, mgT], writes=[Wsb])
                        else:
                            op(eng, lambda e, kc=kc, ws_=ws_, Wsb=Wsb, c0=c0: e.tensor_scalar(out=Wsb[:, kc, c0:c0 + 512], in0=ws_[:, kc, :], scalar1=mgT[:, kc:kc + 1], scalar2=None, op0=ALU.mult), reads=[ws_, mgT], writes=[Wsb])
                    else:
                        if eng == "scalar":
                            op(eng, lambda e, kc=kc, ws_=ws_, Wsb=Wsb, c0=c0: e.activation(out=Wsb[:, kc, c0:c0 + 512], in_=ws_[:, kc, :], func=AF.Copy), reads=[ws_], writes=[Wsb])
                        else:
                            op(eng, lambda e, kc=kc, ws_=ws_, Wsb=Wsb, c0=c0: e.tensor_copy(out=Wsb[:, kc, c0:c0 + 512], in_=ws_[:, kc, :]), reads=[ws_], writes=[Wsb])
        ag = [P.sb("ag%d" % i, [128, 8, T4], BF16) for i in range(2)]
        hnt = [P.sb("hnt%d" % i, [128, B4, D], BF16) for i in range(2)]
        hg = [P.sb("hg%d" % i, [128, 8, T4], BF16) for i in range(2)]
        ga = [P.sb("ga%d" % i, [128, 16, T4], BF16) for i in range(2)]
        x4 = [P.sb("x4_%d" % i, [128, 8, T4], F32) for i in range(2)]
        mgd = [P.sb("mgd%d" % i, [128, 8, T4], BF16) for i in range(2)]
        yo = [P.sb("yo%d" % i, [128, 8, T4], F32) for i in range(2)]
        ta_ = [P.sb("ta%d" % i, [128, T4], F32) for i in range(2)]
        tb_ = [P.sb("tb%d" % i, [128, T4], F32) for i in range(2)]
        psTr = [P.ps("psTr%d" % i, [128, T4]) for i in range(2)]
        psAo = [P.ps("psAo%d" % i, [128, T4]) for i in range(2)]
        psMo = [P.ps("psMo%d" % i, [128, T4]) for i in range(2)]
        psYo = [P.ps("psYo%d" % i, [128, T4]) for i in range(2)]
        xsv = x_src.rearrange("(c p) t -> p c t", p=128)
        xdv = x_dst.rearrange("(c p) t -> p c t", p=128)
        for t in range(NT4):
            i2 = t % 2
            sl = slice(t * T4, (t + 1) * T4)
            a_, hn4, hg_, ga_, x_, mg_, y_ = ag[i2], hnt[i2], hg[i2], ga[i2], x4[i2], mgd[i2], yo[i2]
            dma(a_[:], AGT[:, sl].rearrange("(c p) t -> p c t", p=128), writes=[a_], slot=a_)
            dma(hn4[:], HN[sl, :].rearrange("(b p) c -> p b c", p=128), writes=[hn4], slot=hn4)
            dma(ga_[:], GAT[:, sl].rearrange("(c p) t -> p c t", p=128), writes=[ga_], slot=ga_)
            dma(x_[:], xsv[:, :, sl], writes=[x_], slot=x_)
            for c in range(8):
                ptr = psTr[c % 2]
                mm([lambda e, ptr=ptr, b=b, c=c, hn4=hn4: e.matmul(ptr[:, b * 128:(b + 1) * 128], hn4[:, b, c * 128:(c + 1) * 128], cstbf[:, IDENT, :], start=True, stop=True) for b in range(B4)],
                   reads=[hn4, cstbf], writes=[ptr])
                eng = "scalar" if c % 2 == 0 else "vector"
                if eng == "scalar":
                    op(eng, lambda e, ptr=ptr, hg_=hg_, c=c: e.activation(out=hg_[:, c, :], in_=ptr[:], func=AF.Copy), reads=[ptr], writes=[hg_])
                else:
                    op(eng, lambda e, ptr=ptr, hg_=hg_, c=c: e.tensor_copy(out=hg_[:, c, :], in_=ptr[:]), reads=[ptr], writes=[hg_])
            for oc in range(8):
                pa, pm = psAo[oc % 2], psMo[oc % 2]
                mm([lambda e, pa=pa, kc=kc, oc=oc, a_=a_: e.matmul(pa[:], Wa[:, kc, oc * 128:(oc + 1) * 128], a_[:, kc, :], start=(kc == 0), stop=(kc == 7)) for kc in range(8)], reads=[Wa, a_], writes=[pa])
                mm([lambda e, pm=pm, kc=kc, oc=oc, hg_=hg_: e.matmul(pm[:], Wm[:, kc, oc * 128:(oc + 1) * 128], hg_[:, kc, :], start=(kc == 0), stop=(kc == 7)) for kc in range(8)], reads=[Wm, hg_], writes=[pm])
                ta, tb = ta_[oc % 2], tb_[oc % 2]
                op("vector", lambda e, ta=ta, pa=pa, ga_=ga_, oc=oc: e.tensor_tensor(out=ta[:], in0=pa[:], in1=ga_[:, oc, :], op=ALU.mult), reads=[pa, ga_], writes=[ta])
                op("vector", lambda e, tb=tb, pm=pm, ga_=ga_, oc=oc: e.tensor_tensor(out=tb[:], in0=pm[:], in1=ga_[:, 8 + oc, :], op=ALU.mult), reads=[pm, ga_], writes=[tb])
                op("gpsimd", lambda e, ta=ta, tb=tb, mg_=mg_, oc=oc: e.tensor_tensor(out=mg_[:, oc, :], in0=ta[:], in1=tb[:], op=ALU.add), reads=[ta, tb], writes=[mg_])
            for oc in range(8):
                py = psYo[oc % 2]
                mm([lambda e, py=py, kc=kc, oc=oc, mg_=mg_: e.matmul(py[:], Wo[:, kc, oc * 128:(oc + 1) * 128], mg_[:, kc, :], start=(kc == 0), stop=(kc == 7)) for kc in range(8)], reads=[Wo, mg_], writes=[py])
                op("vector", lambda e, py=py, y_=y_, x_=x_, oc=oc: e.tensor_tensor(out=y_[:, oc, :], in0=py[:], in1=x_[:, oc, :], op=ALU.add), reads=[py, x_], writes=[y_])
            dma(xdv[:, :, sl], y_[:], reads=[y_], slot=y_)
        sync_stores(yo)
        phase_end(mark)

    with nc.Block() as block:
        P.replay(block)
    P.close()
    return nc


def make_consts():
    c = np.zeros((128, 6, 128), np.float32)
    i = np.arange(128)
    c[:, 0, :] = np.eye(128)
    c[:, 1, :] = (i[:, None] <= i[None, :])
    c[:, 2, :] = (i[:, None] >= i[None, :])
    c[:, 3, :] = (i[:, None] // 64 == i[None, :] // 64)
    Pm = np.zeros((128, 128), np.float32)
    for m in range(128):
        d = m % 64
        if d < 8:
            Pm[m + 8, m] = -1.0
        elif d < 16:
            Pm[m - 8, m] = 1.0
    c[:, 4, :] = Pm
    c[:, 5, :] = 1.0
    return c


def rope_tables(pos):
    T = pos.shape[0]
    half = 8
    inv = np.power(np.float32(500000.0), -np.arange(half, dtype=np.float32) * np.float32(2.0) / np.float32(16)).astype(np.float32)
    ang = (pos.astype(np.float32)[:, None] * inv[None, :]).astype(np.float32)
    cos = np.cos(ang).astype(np.float32)
    sin = np.sin(ang).astype(np.float32)
    ct = np.ones((128, T), np.float32)
    st = np.zeros((128, T), np.float32)
    for p in range(128):
        d = p % 64
        if d < 16:
            ct[p] = cos[:, d % 8]
            st[p] = sin[:, d % 8]
    rk = np.stack([ct, st]).astype(np.float32)
    rq = (rk * np.float32(0.125)).astype(np.float32)
    return rq, rk


def prep_weights(inp, L):
    f = lambda a: np.ascontiguousarray(np.asarray(a, dtype=np.float32))
    b_in = f(inp["b_in"])[:L]
    starts = ([O_AQ + 128 * i for i in range(8)] + [O_AK + 128 * i for i in range(2)] + [O_AZ + 128 * i for i in range(8)]
              + [O_MQ + 128 * i for i in range(16)] + [O_MG + 128 * i for i in range(16)])
    biasT = np.ascontiguousarray(np.stack([b_in[:, st:st + 128] for st in starts], axis=-1))
    gT = np.ascontiguousarray(f(inp["norm_g"])[:L].reshape(L, 8, 128).transpose(0, 2, 1))
    mgT = np.ascontiguousarray(f(inp["m_norm_g"])[:L].reshape(L, 8, 128).transpose(0, 2, 1))
    gq = np.tile(f(inp["q_norm_g"])[:L], (1, 2))
    gk = np.tile(f(inp["k_norm_g"])[:L], (1, 2))
    gqk = np.ascontiguousarray(np.stack([gq, gk], axis=-1))
    cwT = np.ascontiguousarray(f(inp["conv_w"])[:L].reshape(L, 3, 16, 128).transpose(0, 3, 2, 1))
    return {"w_in": f(inp["w_in"])[:L], "b_in": b_in, "biasT": biasT, "gT": gT, "mgT": mgT, "gqk": gqk,
            "sink": f(inp["sink"])[:L], "cwT": cwT, "w_att_out": f(inp["w_att_out"])[:L],
            "w_m_out": f(inp["w_m_out"])[:L], "w_out": f(inp["w_out"])[:L], "consts": make_consts()}


_NC_CACHE = {}


def kernel(x_prompt, x_sample, norm_g, w_in, b_in, q_norm_g, k_norm_g, sink, conv_w, m_norm_g, w_att_out, w_m_out, w_out):
    inp = dict(norm_g=norm_g, w_in=w_in, b_in=b_in, q_norm_g=q_norm_g, k_norm_g=k_norm_g, sink=sink, conv_w=conv_w,
               m_norm_g=m_norm_g, w_att_out=w_att_out, w_m_out=w_m_out, w_out=w_out)
    cfg = Cfg(3, 16, 4, 4)
    xp = np.asarray(x_prompt, np.float32)
    xs = np.asarray(x_sample, np.float32)
    wts = prep_weights(inp, 4)
    SEG = 2048
    in_maps = []
    for c in range(8):
        if c < 4:
            segs = [xp[c, :SEG], xp[c, SEG:], xs[c]]
            pos = np.concatenate([np.arange(4096), np.arange(2048)]).astype(np.float32)
            fl = [1.0, 0.0]
        else:
            j = 4 + 3 * (c - 4)
            segs = [xs[j], xs[j + 1], xs[j + 2]]
            pos = np.concatenate([np.arange(2048)] * 3).astype(np.float32)
            fl = [0.0, 0.0]
        xT = np.ascontiguousarray(np.concatenate(segs, axis=0).T)
        rq, rk = rope_tables(pos)
        m = dict(wts)
        m.update({"xT": xT, "flags": np.tile(np.array(fl, np.float32)[None, :], (128, 1)), "ropeq": rq, "ropek": rk})
        in_maps.append(m)
    if "nc" not in _NC_CACHE:
        _NC_CACHE["nc"] = build(cfg)
    res = run_bass_kernel_spmd(_NC_CACHE["nc"], in_maps, core_ids=list(range(8)))
    yp = np.empty_like(xp)
    ys = np.empty_like(xs)
    for c in range(8):
        y = np.asarray(res.results[c]["yT"], np.float32).T
        if c < 4:
            yp[c, :SEG] = y[:SEG]
            yp[c, SEG:] = y[SEG:2 * SEG]
            ys[c] = y[2 * SEG:]
        else:
            j = 4 + 3 * (c - 4)
            for i in range(3):
                ys[j + i] = y[i * SEG:(i + 1) * SEG]
    return (yp, ys)
```

```python
import numpy as np
import concourse.bass as bass
import concourse.mybir as mybir
from concourse.bass_utils import run_bass_kernel_spmd

F32 = mybir.dt.float32
BF16 = mybir.dt.bfloat16
ALU = mybir.AluOpType
AF = mybir.ActivationFunctionType
AX = mybir.AxisListType
ENGS = ("sync", "scalar", "vector", "gpsimd", "tensor")

D = 1024
IN_DIM = 9760
EPS = 1e-6
O_AQ, O_AK, O_AV, O_AZ = 0, 1024, 1280, 1536
O_MQ, O_MK, O_MV, O_MO, O_MZ = 2560, 3584, 4608, 5632, 6656
O_G4, O_MG = 7680, 7712


class Tok:
    __slots__ = ("sem", "val")

    def __init__(self, sem, val):
        self.sem = sem
        self.val = val


class Buf:
    __slots__ = ("t", "wtok", "rtoks", "dsem")

    def __init__(self, t):
        self.t = t
        self.wtok = None
        self.rtoks = []
        self.dsem = None

    def __getitem__(self, idx):
        return self.t[idx]


class Prog:
    def __init__(self, nc):
        self.nc = nc
        self.q = {e: [] for e in ENGS}
        self.esem = {}
        self.ecnt = {e: 0 for e in ENGS}
        self.waited = {e: {} for e in ENGS}
        self.cms = []
        self.sem_cms = []
        self.sem_pool = []
        self.phase_slots = []
        self.pending = []
        self.nsem = 0
        for e in ENGS:
            self.esem[e] = self.new_sem("p_" + e)

    def enter(self, cm):
        v = cm.__enter__()
        self.cms.append(cm)
        return v

    def new_sem(self, name):
        self.nsem += 1
        cm = self.nc.semaphore(name)
        v = cm.__enter__()
        self.sem_cms.append(cm)
        return v

    def sb(self, name, shape, dt):
        self.nsem += 1
        return Buf(self.enter(self.nc.sbuf_tensor("s%d_%s" % (self.nsem, name), shape, dt)))

    def ps(self, name, shape, dt=F32):
        self.nsem += 1
        return Buf(self.enter(self.nc.psum_tensor("p%d_%s" % (self.nsem, name), shape, dt)))

    def wait(self, eng, tok):
        if tok is None:
            return
        w = self.waited[eng]
        k = id(tok.sem)
        if w.get(k, 0) >= tok.val:
            return
        w[k] = tok.val
        self.q[eng].append(lambda e, s=tok.sem, v=tok.val: e.wait_ge(s, v))

    def _deps(self, eng, reads, writes):
        for b in reads:
            self.wait(eng, b.wtok)
        for b in writes:
            self.wait(eng, b.wtok)
            for t in b.rtoks:
                self.wait(eng, t)

    def _post(self, tok, reads, writes):
        for b in reads:
            b.rtoks.append(tok)
            if len(b.rtoks) > 24:
                b.rtoks = b.rtoks[-24:]
        for b in writes:
            b.wtok = tok
            b.rtoks = []

    def op(self, eng, fn, reads=(), writes=()):
        self._deps(eng, reads, writes)
        self.ecnt[eng] += 1
        sem = self.esem[eng]
        tok = Tok(sem, self.ecnt[eng])
        self.q[eng].append(lambda e, fn=fn, sem=sem: fn(e).then_inc(sem, 1))
        self._post(tok, reads, writes)
        return tok

    def mm(self, fns, reads=(), writes=()):
        eng = "tensor"
        self._deps(eng, reads, writes)
        for fn in fns[:-1]:
            self.q[eng].append(lambda e, fn=fn: fn(e))
        self.ecnt[eng] += 1
        sem = self.esem[eng]
        tok = Tok(sem, self.ecnt[eng])
        self.q[eng].append(lambda e, fn=fns[-1], sem=sem: fn(e).then_inc(sem, 1))
        self._post(tok, reads, writes)
        return tok

    def dma(self, out_ap, in_ap, reads=(), writes=(), slot=None, eng="sync"):
        self._deps(eng, reads, writes)
        if slot.dsem is None:
            slot.dsem = self.sem_pool.pop() if self.sem_pool else [self.new_sem("d%d" % self.nsem), 0]
            self.phase_slots.append(slot)
        slot.dsem[1] += 16
        sem, val = slot.dsem
        tok = Tok(sem, val)
        self.q[eng].append(lambda e, o=out_ap, i=in_ap, s=sem: e.dma_start(out=o, in_=i).then_inc(s, 16))
        self._post(tok, reads, writes)
        return tok

    def replay(self, block):
        q = self.q

        @block.sync
        def _(e):
            for f in q["sync"]:
                f(e)

        @block.scalar
        def _(e):
            for f in q["scalar"]:
                f(e)

        @block.vector
        def _(e):
            for f in q["vector"]:
                f(e)

        @block.gpsimd
        def _(e):
            for f in q["gpsimd"]:
                f(e)

        @block.tensor
        def _(e):
            for f in q["tensor"]:
                f(e)

    def close(self):
        for cm in reversed(self.cms):
            cm.__exit__(None, None, None)
        self.cms = []
        for cm in reversed(self.sem_cms):
            cm.__exit__(None, None, None)
        self.sem_cms = []


def bcast_ap(t, off, dims):
    return bass.AP(t, off, dims)


class Cfg:
    def __init__(self, nseg=3, segb=16, tb=4, depth=4):
        self.NSEG, self.SEGB, self.TB, self.DEPTH = nseg, segb, tb, depth
        self.NBLK = nseg * segb
        self.T = self.NBLK * 128
        self.TT = tb * 128
        self.NT = self.NBLK // tb
        self.SEGT = segb * 128
        self.TPS = segb // tb


def build(cfg, debug=False):
    nc = bass.Bass("TRN2", target_bir_lowering=False)
    T, TT, NT, NBLK, NSEG, SEGB, TB, SEGT, L = cfg.T, cfg.TT, cfg.NT, cfg.NBLK, cfg.NSEG, cfg.SEGB, cfg.TB, cfg.SEGT, cfg.DEPTH
    NB = max(NSEG - 1, 1)

    def din(name, shape, dt=F32):
        return nc.dram_tensor(name, list(shape), dt, kind="ExternalInput").ap()

    def dscr(name, shape, dt):
        return nc.dram_tensor(name, list(shape), dt, kind=("ExternalOutput" if debug else "Internal")).ap()

    xT_in = din("xT", [D, T])
    flags_in = din("flags", [128, NB])
    ropeq_in = din("ropeq", [2, 128, T])
    ropek_in = din("ropek", [2, 128, T])
    consts_in = din("consts", [128, 6, 128])
    w_in_d = din("w_in", [L, D, IN_DIM])
    b_in_d = din("b_in", [L, IN_DIM])
    biasT_d = din("biasT", [L, 128, 50])
    gT_d = din("gT", [L, 128, 8])
    mgT_d = din("mgT", [L, 128, 8])
    gqk_d = din("gqk", [L, 128, 2])
    sink_d = din("sink", [L, 16])
    cwT_d = din("cwT", [L, 128, 16, 3])
    wa_d = din("w_att_out", [L, D, D])
    wm_d = din("w_m_out", [L, D, D])
    wo_d = din("w_out", [L, D, D])
    yT_out = nc.dram_tensor("yT", [D, T], F32, kind="ExternalOutput").ap()

    xs_d = dscr("xs", [D, T], F32)
    QT = dscr("QT", [D, T], BF16)
    KT = dscr("KT", [256, T], BF16)
    VA = dscr("VA", [T, 256], BF16)
    ZAT = dscr("ZAT", [D, T], BF16)
    MQT = dscr("MQT", [D, T], BF16)
    MKT = dscr("MKT", [D, T], BF16)
    MV = dscr("MV", [T, D], BF16)
    MO = dscr("MO", [T, D], BF16)
    MZ = dscr("MZ", [T, D], BF16)
    GTS = dscr("GTS", [T, 32], F32)
    GAT = dscr("GAT", [2 * D, T], BF16)
    HB = dscr("HB", [T, D], F32)
    HN = dscr("HN", [T, D], BF16)
    AGT = dscr("AGT", [D, T], BF16)

    P = Prog(nc)
    op, mm, dma = P.op, P.mm, P.dma

    res_bufs = []
    for l in range(L):
        src = xT_in if l == 0 else res_bufs[-1][1]
        dst = yT_out if (L - 1 - l) % 2 == 0 else xs_d
        res_bufs.append((src, dst))

    cst32 = P.sb("cst32", [128, 6, 128], F32)
    cstbf = P.sb("cstbf", [128, 6, 128], BF16)
    flg = P.sb("flg", [128, NB], F32)
    amask = P.sb("amask", [128, 2 + 2 * NB, 512], BF16)
    dma(cst32[:], consts_in, writes=[cst32], slot=cst32)
    dma(flg[:], flags_in, writes=[flg], slot=flg)
    op("vector", lambda e: e.tensor_copy(out=cstbf[:], in_=cst32[:]), reads=[cst32], writes=[cstbf])
    IDENT, TRIU, TRIL, BD64, RPERM, ONES = range(6)
    for g in range(4):
        op("vector", lambda e, g=g: e.tensor_copy(out=amask[:, 0, g * 128:(g + 1) * 128], in_=cst32[:, TRIL, :]), reads=[cst32], writes=[amask])
        op("vector", lambda e, g=g: e.tensor_copy(out=amask[:, 1, g * 128:(g + 1) * 128], in_=cst32[:, TRIU, :]), reads=[cst32], writes=[amask])
    for b in range(NSEG - 1):
        op("vector", lambda e, b=b: e.tensor_scalar(out=amask[:, 2 + 2 * b, :], in0=amask[:, 0, :], scalar1=flg[:, b:b + 1], scalar2=None, op0=ALU.mult), reads=[flg, amask], writes=[amask])
        op("vector", lambda e, b=b: e.tensor_scalar(out=amask[:, 3 + 2 * b, :], in0=amask[:, 1, :], scalar1=flg[:, b:b + 1], scalar2=None, op0=ALU.mult), reads=[flg, amask], writes=[amask])

    def sync_stores(bufs):
        for b in bufs:
            for t in list(b.rtoks) + [b.wtok]:
                if t is not None:
                    P.wait("sync", t)
                    P.pending.append(t)

    def phase_scope():
        return len(P.cms)

    def phase_end(mark):
        toks = list(P.pending)
        P.pending = []
        for e in ("scalar", "vector", "gpsimd", "tensor"):
            if P.ecnt[e] > 0:
                toks.append(Tok(P.esem[e], P.ecnt[e]))
        for e in ENGS:
            for t in toks:
                P.wait(e, t)
        for sl in P.phase_slots:
            P.sem_pool.append(sl.dsem)
            sl.dsem = None
        P.phase_slots = []
        while len(P.cms) > mark:
            cm = P.cms.pop()
            cm.__exit__(None, None, None)

    for l in range(L):
        x_src, x_dst = res_bufs[l]
        mark = phase_scope()
        xn = P.sb("xn", [128, 8, T], BF16)
        xn_tiles = [Buf(xn.t) for _ in range(NT)]
        gT = P.sb("gT", [128, 8], F32)
        biasT = P.sb("biasT", [128, 50], F32)
        hbias = P.sb("hbias", [128, 50], F32)
        gqk = P.sb("gqk", [128, 2], F32)
        cw = P.sb("cw", [128, 16, 3], F32)
        dma(gT[:], gT_d[l], writes=[gT], slot=gT)
        dma(biasT[:], biasT_d[l], writes=[biasT], slot=biasT)
        dma(gqk[:], gqk_d[l], writes=[gqk], slot=gqk)
        dma(cw[:], cwT_d[l], writes=[cw], slot=cw)
        op("vector", lambda e: e.tensor_scalar(out=hbias[:], in0=biasT[:], scalar1=0.5, scalar2=None, op0=ALU.mult), reads=[biasT], writes=[hbias])
        KS = 128.0 ** -0.5
        op("vector", lambda e: e.tensor_scalar(out=cw[:, 0:8, :], in0=cw[:, 0:8, :], scalar1=0.5, scalar2=None, op0=ALU.mult), reads=[cw], writes=[cw])
        op("vector", lambda e: e.tensor_scalar(out=cw[:, 8:16, :], in0=cw[:, 8:16, :], scalar1=0.5 * KS, scalar2=None, op0=ALU.mult), reads=[cw], writes=[cw])

        mark1 = phase_scope()
        xst = [P.sb("xst%d" % i, [128, 8, TT], F32) for i in range(2)]
        sqb = [P.sb("sqb%d" % i, [128, 8, TT], BF16) for i in range(2)]
        rsb = [P.sb("rsb%d" % i, [128, TT], F32) for i in range(2)]
        pss = [P.ps("pss%d" % i, [128, TT]) for i in range(2)]
        xv = x_src.rearrange("(c p) t -> p c t", p=128)
        for t in range(NT):
            xb, sq, rs, pp = xst[t % 2], sqb[t % 2], rsb[t % 2], pss[t % 2]
            dma(xb[:], xv[:, :, t * TT:(t + 1) * TT], writes=[xb], slot=xb)
            op("scalar", lambda e, xb=xb, sq=sq: e.activation(out=sq[:], in_=xb[:], func=AF.Square), reads=[xb], writes=[sq])
            mm([lambda e, c=c, sq=sq, pp=pp: e.matmul(pp[:], cstbf[:, ONES, :], sq[:, c, :], start=(c == 0), stop=(c == 7)) for c in range(8)], reads=[sq, cstbf], writes=[pp])
            op("scalar", lambda e, rs=rs, pp=pp: e.activation(out=rs[:], in_=pp[:], func=AF.Ln, bias=EPS, scale=1.0 / D), reads=[pp], writes=[rs])
            op("scalar", lambda e, rs=rs: e.activation(out=rs[:], in_=rs[:], func=AF.Exp, scale=-0.5), reads=[rs], writes=[rs])
            rs_bc = bass.AP(rs.t, 0, [[TT, 128], [0, 8], [1, TT]])
            op("vector", lambda e, xb=xb, rs_bc=rs_bc, t=t: e.tensor_tensor(out=xn[:, :, t * TT:(t + 1) * TT], in0=xb[:], in1=rs_bc, op=ALU.mult), reads=[xb, rs], writes=[xn_tiles[t]])

        phase_end(mark1)
        WS = 512
        wst = [P.sb("wst%d" % i, [128, 8, WS], F32) for i in range(1)]
        wbf = [P.sb("wbf%d" % i, [128, 8, WS], BF16) for i in range(2)]
        slab_ctr = [0]
        wv = w_in_d[l].rearrange("(kc p) c -> p kc c", p=128)

        def load_slab(c0, ncols):
            i = slab_ctr[0] % 2
            slab_ctr[0] += 1
            ws_, wb_ = wst[0], wbf[i]
            dma(ws_[:, :, 0:ncols], wv[:, :, c0:c0 + ncols], writes=[ws_], slot=ws_)
            for kc in range(8):
                eng = "gpsimd" if kc % 2 == 0 else "vector"
                op(eng, lambda e, kc=kc, ws_=ws_, wb_=wb_: e.tensor_scalar(out=wb_[:, kc, 0:ncols], in0=ws_[:, kc, 0:ncols], scalar1=gT[:, kc:kc + 1], scalar2=None, op0=ALU.mult), reads=[ws_, gT], writes=[wb_])
            return wb_

        psA = [P.ps("psA%d" % i, [128, 512]) for i in range(3)]
        psB = [P.ps("psB%d" % i, [128, 512]) for i in range(4)]
        pa_ctr = [0]

        def next_psA():
            pa_ctr[0] += 1
            return psA[pa_ctr[0] % 3]

        def fm_matmul(pp, wb_, cc, t):
            mm([lambda e, kc=kc: e.matmul(pp[:, 0:TT], wb_[:, kc, cc * 128:(cc + 1) * 128], xn[:, kc, t * TT:(t + 1) * TT], start=(kc == 0), stop=(kc == 7)) for kc in range(8)],
               reads=[wb_, xn_tiles[t]], writes=[pp])

        stg = [P.sb("stg%d" % i, [128, 512], BF16) for i in range(4)]
        stg_ctr = [0]

        def next_stg():
            stg_ctr[0] += 1
            return stg[stg_ctr[0] % 4]

        markA = phase_scope()
        wqk = P.sb("wqk", [128, 8, 1280], BF16)
        for (c0, ncols) in ((0, 512), (512, 512), (1024, 256)):
            wb_ = load_slab(c0, ncols)
            op("gpsimd", lambda e, wb_=wb_, c0=c0, ncols=ncols: e.tensor_copy(out=wqk[:, :, c0:c0 + ncols], in_=wb_[:, :, 0:ncols]), reads=[wb_], writes=[wqk])
        rq = [P.sb("rq%d" % i, [128, 2, TT], F32) for i in range(2)]
        rk = [P.sb("rk%d" % i, [128, 2, TT], F32) for i in range(2)]
        q0 = [P.sb("q0_%d" % i, [128, TT], F32) for i in range(3)]
        sq1 = [P.sb("sq1_%d" % i, [128, TT], BF16) for i in range(3)]
        rs1 = [P.sb("rs1_%d" % i, [128, TT], F32) for i in range(3)]
        qn = [P.sb("qn_%d" % i, [128, TT], BF16) for i in range(3)]
        t1b = [P.sb("t1b_%d" % i, [128, TT], F32) for i in range(2)]
        t2b = [P.sb("t2b_%d" % i, [128, TT], F32) for i in range(2)]
        itemsA = [(t, cc) for t in range(NT) for cc in range(10)]
        ppA = {}

        def a_rope_loads(t):
            dma(rq[t % 2][:], ropeq_in[:, :, t * TT:(t + 1) * TT].rearrange("a p t -> p a t"), writes=[rq[t % 2]], slot=rq[t % 2])
            dma(rk[t % 2][:], ropek_in[:, :, t * TT:(t + 1) * TT].rearrange("a p t -> p a t"), writes=[rk[t % 2]], slot=rk[t % 2])

        def a_stage1(i):
            t, cc = itemsA[i]
            if cc == 0:
                a_rope_loads(t)
            pp = next_psA()
            fm_matmul(pp, wqk, cc, t)
            a_q0, a_sq = q0[i % 3], sq1[i % 3]
            op("scalar", lambda e, a=a_q0, pp=pp, cc=cc: e.activation(out=a[:], in_=pp[:, 0:TT], func=AF.Identity, bias=biasT[:, cc:cc + 1], scale=1.0), reads=[pp, biasT], writes=[a_q0])
            op("scalar", lambda e, a=a_sq, pp=pp, cc=cc: e.activation(out=a[:], in_=pp[:, 0:TT], func=AF.Square, bias=biasT[:, cc:cc + 1], scale=1.0), reads=[pp, biasT], writes=[a_sq])

        def a_stage2(i):
            t, cc = itemsA[i]
            gcol = 0 if cc < 8 else 1
            p2 = psB[i % 2]
            a_q0, a_sq, a_rs, a_qn = q0[i % 3], sq1[i % 3], rs1[i % 3], qn[i % 3]
            mm([lambda e, a=a_sq, p2=p2: e.matmul(p2[:, 0:TT], cstbf[:, BD64, :], a[:], start=True, stop=True)], reads=[a_sq, cstbf], writes=[p2])
            op("scalar", lambda e, a=a_rs, p2=p2: e.activation(out=a[:], in_=p2[:, 0:TT], func=AF.Ln, bias=EPS, scale=1.0 / 64), reads=[p2], writes=[a_rs])
            op("scalar", lambda e, a=a_rs: e.activation(out=a[:], in_=a[:], func=AF.Exp, scale=-0.5), reads=[a_rs], writes=[a_rs])
            op("vector", lambda e, a=a_qn, b=a_q0, c=a_rs, gcol=gcol: e.scalar_tensor_tensor(out=a[:], in0=b[:], scalar=gqk[:, gcol:gcol + 1], in1=c[:], op0=ALU.mult, op1=ALU.mult), reads=[a_q0, a_rs, gqk], writes=[a_qn])

        def a_stage3(i):
            t, cc = itemsA[i]
            isq = cc < 8
            rt = rq[t % 2] if isq else rk[t % 2]
            p3 = psB[2 + i % 2]
            a_qn, a_t1, a_t2 = qn[i % 3], t1b[i % 2], t2b[i % 2]
            mm([lambda e, a=a_qn, p3=p3: e.matmul(p3[:, 0:TT], cstbf[:, RPERM, :], a[:], start=True, stop=True)], reads=[a_qn, cstbf], writes=[p3])
            op("gpsimd", lambda e, a=a_t1, b=a_qn, rt=rt: e.tensor_tensor(out=a[:], in0=b[:], in1=rt[:, 0, :], op=ALU.mult), reads=[a_qn, rt], writes=[a_t1])
            op("vector", lambda e, a=a_t2, p3=p3, rt=rt: e.tensor_tensor(out=a[:], in0=p3[:, 0:TT], in1=rt[:, 1, :], op=ALU.mult), reads=[p3, rt], writes=[a_t2])
            sg = next_stg()
            op("gpsimd", lambda e, sg=sg, a=a_t1, b=a_t2: e.tensor_tensor(out=sg[:, 0:TT], in0=a[:], in1=b[:], op=ALU.add), reads=[a_t1, a_t2], writes=[sg])
            if isq:
                dst = QT[cc * 128:(cc + 1) * 128, t * TT:(t + 1) * TT]
            else:
                dst = KT[(cc - 8) * 128:(cc - 7) * 128, t * TT:(t + 1) * TT]
            dma(dst, sg[:, 0:TT], reads=[sg], slot=sg)

        NA = len(itemsA)
        for i in range(NA + 2):
            if i < NA:
                a_stage1(i)
            if 0 <= i - 1 < NA:
                a_stage2(i - 1)
            if 0 <= i - 2 < NA:
                a_stage3(i - 2)

        sync_stores(stg)
        phase_end(markA)
        markB = phase_scope()
        sg32 = [P.sb("sg32_%d" % i, [128, 512], F32) for i in range(2)]
        fmB = [("az", O_AZ, 8, ZAT, 10), ("mg", O_MG, 16, GAT, 34)]
        it = 0
        for (kind, cbase, nch, dstT, bidx) in fmB:
            for s0 in range(0, nch, 4):
                nchs = min(4, nch - s0)
                wb_ = load_slab(cbase + s0 * 128, nchs * 128)
                for cc in range(nchs):
                    gch = bidx + s0 + cc
                    for t in range(NT):
                        pp = next_psA()
                        fm_matmul(pp, wb_, cc, t)
                        sg = next_stg()
                        if kind == "az":
                            op("scalar", lambda e, sg=sg, pp=pp, gch=gch: e.activation(out=sg[:, 0:TT], in_=pp[:, 0:TT], func=AF.Silu, bias=biasT[:, gch:gch + 1], scale=1.0), reads=[pp, biasT], writes=[sg])
                        else:
                            tmp = sg32[it % 2]
                            it += 1
                            op("scalar", lambda e, tmp=tmp, pp=pp, gch=gch: e.activation(out=tmp[:, 0:TT], in_=pp[:, 0:TT], func=AF.Tanh, bias=hbias[:, gch:gch + 1], scale=0.5), reads=[pp, hbias], writes=[tmp])
                            op("vector", lambda e, sg=sg, tmp=tmp: e.tensor_scalar(out=sg[:, 0:TT], in0=tmp[:, 0:TT], scalar1=0.5, scalar2=0.5, op0=ALU.mult, op1=ALU.add), reads=[tmp], writes=[sg])
                        r0 = (s0 + cc) * 128
                        dma(dstT[r0:r0 + 128, t * TT:(t + 1) * TT], sg[:, 0:TT], reads=[sg], slot=sg)

        sync_stores(stg)
        phase_end(markB)
        markC = phase_scope()
        rawb = [[P.sb("raw%d_%d" % (r_, s), [128, SEGT + 2], BF16) for s in range(NSEG)] for r_ in range(2)]
        dg = P.sb("dg", [128, 16, 3, 128], BF16)
        cvu = [P.sb("cvu%d" % i, [128, TT], F32) for i in range(3)]
        cvo = [P.sb("cvo%d" % i, [128, TT], BF16) for i in range(4)]
        for ch in range(16):
            for j in range(3):
                eng = "vector" if (ch * 3 + j) % 2 == 0 else "gpsimd"
                op(eng, lambda e, ch=ch, j=j: e.tensor_scalar(out=dg[:, ch, j, :], in0=cst32[:, IDENT, :], scalar1=cw[:, ch, j:j + 1], scalar2=None, op0=ALU.mult), reads=[cst32, cw], writes=[dg])
        for r_ in range(2):
            for s in range(NSEG):
                op("gpsimd", lambda e, s=s, r_=r_: e.memset(rawb[r_][s][:, 0:1], 0.0), writes=[rawb[r_][s]])
                op("gpsimd", lambda e, s=s, r_=r_: e.memset(rawb[r_][s][:, SEGT + 1:SEGT + 2], 0.0), writes=[rawb[r_][s]])
        slabC = {}

        def c_main(ch):
            s0, cc = (ch // 4) * 4, ch % 4
            if cc == 0:
                slabC[s0] = load_slab(O_MQ + s0 * 128, 512)
            wb_ = slabC[s0]
            raw = rawb[ch % 2]
            gch = 18 + ch
            for t in range(NT):
                s = t // cfg.TPS
                tl = t % cfg.TPS
                pp = next_psA()
                fm_matmul(pp, wb_, cc, t)
                op("scalar", lambda e, s=s, tl=tl, pp=pp, gch=gch, raw=raw: e.activation(out=raw[s][:, 1 + tl * TT:1 + (tl + 1) * TT], in_=pp[:, 0:TT], func=AF.Identity, bias=biasT[:, gch:gch + 1], scale=1.0), reads=[pp, biasT], writes=[raw[s]])
            for b in range(NSEG - 1):
                op("vector", lambda e, b=b, raw=raw: e.tensor_scalar(out=raw[b][:, SEGT + 1:SEGT + 2], in0=raw[b + 1][:, 1:2], scalar1=flg[:, b:b + 1], scalar2=None, op0=ALU.mult), reads=[raw[b + 1], flg], writes=[raw[b]])
                op("vector", lambda e, b=b, raw=raw: e.tensor_scalar(out=raw[b + 1][:, 0:1], in0=raw[b][:, SEGT:SEGT + 1], scalar1=flg[:, b:b + 1], scalar2=None, op0=ALU.mult), reads=[raw[b], flg], writes=[raw[b + 1]])

        cctr = [0]

        def c_conv(ch):
            raw = rawb[ch % 2]
            dstT = MQT if ch < 8 else MKT
            r0 = (ch % 8) * 128
            inv_s = 1.0 if ch < 8 else 1.0 / KS
            for t in range(NT):
                s = t // cfg.TPS
                tl = t % cfg.TPS
                cctr[0] += 1
                pc = psB[cctr[0] % 2]
                u = cvu[cctr[0] % 3]
                o = cvo[cctr[0] % 4]
                rw = raw[s]
                mm([lambda e, pc=pc, ch=ch, j=j, rw=rw, tl=tl: e.matmul(pc[:, 0:TT], dg[:, ch, j, :], rw[:, tl * TT + j:tl * TT + j + TT], start=(j == 0), stop=(j == 2)) for j in range(3)],
                   reads=[rw, dg], writes=[pc])
                op("scalar", lambda e, u=u, pc=pc, inv_s=inv_s: e.activation(out=u[:], in_=pc[:, 0:TT], func=AF.Tanh, scale=inv_s), reads=[pc], writes=[u])
                op("vector", lambda e, u=u, pc=pc, o=o: e.scalar_tensor_tensor(out=o[:], in0=u[:], scalar=1.0, in1=pc[:, 0:TT], op0=ALU.add, op1=ALU.mult), reads=[u, pc], writes=[o])
                dma(dstT[r0:r0 + 128, t * TT:(t + 1) * TT], o[:], reads=[o], slot=o)

        c_main(0)
        for ch in range(16):
            if ch + 1 < 16:
                c_main(ch + 1)
            c_conv(ch)
        sync_stores(cvo)
        phase_end(markC)
        bbc = [P.sb("bbc%d" % i, [128, 512], F32) for i in range(2)]
        tm32 = [P.sb("tm32_%d" % i, [128, 512], F32) for i in range(2)]
        tmo = [P.sb("tmo_%d" % i, [128, 512], BF16) for i in range(3)]
        g32 = [P.sb("g32_%d" % i, [128, 32], F32) for i in range(2)]
        tmD = [("v", O_AV, 256, VA, 0), ("v", O_MV, 512, MV, 0), ("v", O_MV + 512, 512, MV, 512),
               ("o", O_MO, 512, MO, 0), ("o", O_MO + 512, 512, MO, 512),
               ("z", O_MZ, 512, MZ, 0), ("z", O_MZ + 512, 512, MZ, 512), ("g", O_G4, 32, GTS, 0)]
        it = 0
        for si, (kind, c0, ncols, dstD, dc0) in enumerate(tmD):
            wb_ = load_slab(c0, ncols)
            bb = bbc[si % 2]
            dma(bb[:, 0:ncols], bass.AP(b_in_d.tensor, l * IN_DIM + c0, [[0, 128], [1, ncols]]), writes=[bb], slot=bb)
            for blk in range(NBLK):
                t = blk // TB
                pp = next_psA()
                mm([lambda e, kc=kc, pp=pp, blk=blk, ncols=ncols, wb_=wb_: e.matmul(pp[:, 0:ncols], xn[:, kc, blk * 128:(blk + 1) * 128], wb_[:, kc, 0:ncols], start=(kc == 0), stop=(kc == 7)) for kc in range(8)],
                   reads=[wb_, xn_tiles[t]], writes=[pp])
                r0 = blk * 128
                if kind == "v":
                    o_ = tmo[it % 3]
                    it += 1
                    op("vector", lambda e, o_=o_, pp=pp, bb=bb, ncols=ncols: e.tensor_tensor(out=o_[:, 0:ncols], in0=pp[:, 0:ncols], in1=bb[:, 0:ncols], op=ALU.add), reads=[pp, bb], writes=[o_])
                    dma(dstD[r0:r0 + 128, dc0:dc0 + ncols], o_[:, 0:ncols], reads=[o_], slot=o_)
                elif kind == "o":
                    o_, tmp = tmo[it % 3], tm32[it % 2]
                    it += 1
                    op("vector", lambda e, tmp=tmp, pp=pp, bb=bb, ncols=ncols: e.tensor_tensor(out=tmp[:, 0:ncols], in0=pp[:, 0:ncols], in1=bb[:, 0:ncols], op=ALU.add), reads=[pp, bb], writes=[tmp])
                    op("scalar", lambda e, tmp=tmp, ncols=ncols: e.activation(out=tmp[:, 0:ncols], in_=tmp[:, 0:ncols], func=AF.Tanh, scale=0.5), reads=[tmp], writes=[tmp])
                    op("gpsimd", lambda e, tmp=tmp, o_=o_, ncols=ncols: e.tensor_scalar(out=o_[:, 0:ncols], in0=tmp[:, 0:ncols], scalar1=0.5, scalar2=0.5, op0=ALU.mult, op1=ALU.add), reads=[tmp], writes=[o_])
                    dma(dstD[r0:r0 + 128, dc0:dc0 + ncols], o_[:, 0:ncols], reads=[o_], slot=o_)
                elif kind == "z":
                    o_, tmp = tmo[it % 3], tm32[it % 2]
                    it += 1
                    op("vector", lambda e, tmp=tmp, pp=pp, bb=bb, ncols=ncols: e.tensor_tensor(out=tmp[:, 0:ncols], in0=pp[:, 0:ncols], in1=bb[:, 0:ncols], op=ALU.add), reads=[pp, bb], writes=[tmp])
                    op("scalar", lambda e, tmp=tmp, o_=o_, ncols=ncols: e.activation(out=o_[:, 0:ncols], in_=tmp[:, 0:ncols], func=AF.Silu), reads=[tmp], writes=[o_])
                    dma(dstD[r0:r0 + 128, dc0:dc0 + ncols], o_[:, 0:ncols], reads=[o_], slot=o_)
                else:
                    o_ = g32[it % 2]
                    it += 1
                    op("vector", lambda e, o_=o_, pp=pp, bb=bb: e.tensor_tensor(out=o_[:], in0=pp[:, 0:32], in1=bb[:, 0:32], op=ALU.add), reads=[pp, bb], writes=[o_])
                    dma(dstD[r0:r0 + 128, 0:32], o_[:], reads=[o_], slot=o_)
        sync_stores(tmo + g32)
        phase_end(mark)

        mark = phase_scope()
        HW_ = TT + 256
        Qs = [P.sb("Qs%d" % i, [64, 16, TT], BF16) for i in range(2)]
        Ks = [P.sb("Ks%d" % i, [64, 4, HW_], BF16) for i in range(2)]
        Vs = [P.sb("Vs%d" % i, [128, TB + 2, 256], BF16) for i in range(2)]
        Zs = [P.sb("Zs%d" % i, [64, 16, TT], BF16) for i in range(2)]
        AGs = [P.sb("AGs%d" % i, [64, 16, TT], BF16) for i in range(2)]
        pT = [P.sb("pT%d" % i, [128, 512], BF16) for i in range(9)]
        lnd = [P.sb("lnd%d" % i, [64, 512], F32) for i in range(2)]
        zr = [P.sb("zr%d" % i, [64, 512], F32) for i in range(2)]
        skr = P.sb("skr", [2, 16], F32)
        ske = P.sb("ske", [2, 16], F32)
        skhl = P.sb("skhl", [2, 16], BF16)
        sktmp = P.sb("sktmp", [2, 16], F32)
        skrow = P.sb("skrow", [2, 16, 128], BF16)
        ones2 = P.sb("ones2", [2, 64], BF16)
        psS = [P.ps("psS%d" % i, [128, 512]) for i in range(4)]
        psO = [P.ps("psO%d" % i, [64, 512]) for i in range(2)]
        psD = [P.ps("psD%d" % i, [64, 512]) for i in range(2)]
        dma(skr[:], bass.AP(sink_d.tensor, l * 16, [[0, 2], [1, 16]]), writes=[skr], slot=skr)
        op("scalar", lambda e: e.activation(out=ske[:], in_=skr[:], func=AF.Exp), reads=[skr], writes=[ske])
        op("vector", lambda e: e.tensor_copy(out=skhl[:], in_=ske[:]), reads=[ske], writes=[skhl])
        op("vector", lambda e: e.tensor_tensor(out=sktmp[:], in0=ske[:], in1=skhl[:], op=ALU.subtract), reads=[ske, skhl], writes=[sktmp])
        op("vector", lambda e: e.tensor_copy(out=skhl[:], in_=sktmp[:]), reads=[sktmp], writes=[skhl])
        op("vector", lambda e: e.tensor_copy(out=skhl[0:1, :], in_=ske[0:1, :]), reads=[ske], writes=[skhl])
        op("vector", lambda e: e.tensor_copy(out=skrow[:], in_=bass.AP(skhl.t, 0, [[16, 2], [1, 16], [0, 128]])), reads=[skhl], writes=[skrow])
        op("vector", lambda e: e.memset(ones2[:], 1.0), writes=[ones2])
        sctr = [0]
        pctr = [0]
        for t in range(NT):
            i2 = t % 2
            Qb, Kb, Vb, Zb, Ab = Qs[i2], Ks[i2], Vs[i2], Zs[i2], AGs[i2]
            b0 = t * TB
            lo_blk = max(b0 - 1, 0)
            hi_blk = min(b0 + TB + 1, NBLK)
            dma(Qb[:], QT[:, t * TT:(t + 1) * TT].rearrange("(h d) t -> d h t", d=64), writes=[Qb], slot=Qb)
            dma(Zb[:], ZAT[:, t * TT:(t + 1) * TT].rearrange("(h d) t -> d h t", d=64), writes=[Zb], slot=Zb)
            ko = (lo_blk - (b0 - 1)) * 128
            dma(Kb[:, :, ko:ko + (hi_blk - lo_blk) * 128], KT[:, lo_blk * 128:hi_blk * 128].rearrange("(j d) t -> d j t", d=64), writes=[Kb], slot=Kb)
            vo = lo_blk - (b0 - 1)
            dma(Vb[:, vo:vo + (hi_blk - lo_blk), :], VA[lo_blk * 128:hi_blk * 128, :].rearrange("(n p) c -> p n c", p=128), writes=[Vb], slot=Vb)
            items2 = []
            for nb in range(TB):
                n = b0 + nb
                seg, nis = n // SEGB, n % SEGB
                kbs = []
                if n > 0:
                    if nis > 0:
                        kbs.append((nb, 0))
                    else:
                        kbs.append((nb, 2 + 2 * (seg - 1)))
                kbs.append((nb + 1, None))
                if n < NBLK - 1:
                    if nis < SEGB - 1:
                        kbs.append((nb + 2, 1))
                    else:
                        kbs.append((nb + 2, 3 + 2 * seg))
                for j in range(4):
                    items2.append((nb, j, kbs))
            ptsd = {}

            def s1(k, Kb=Kb, Qb=Qb):
                nb, j, kbs = items2[k]
                pts = []
                for (ks, mi) in kbs:
                    sctr[0] += 1
                    pS = psS[sctr[0] % 4]
                    pctr[0] += 1
                    pt = pT[pctr[0] % 9]
                    mm([lambda e, pS=pS, ks=ks, j=j, nb=nb, Kb=Kb, Qb=Qb: e.matmul(pS[:], Kb[:, j, ks * 128:(ks + 1) * 128], Qb[:, 4 * j:4 * j + 4, nb * 128:(nb + 1) * 128], start=True, stop=True)],
                       reads=[Kb, Qb], writes=[pS])
                    op("scalar", lambda e, pS=pS, pt=pt: e.activation(out=pt[:], in_=pS[:], func=AF.Exp), reads=[pS], writes=[pt])
                    if mi is not None:
                        eng = "gpsimd" if mi % 2 == 0 else "vector"
                        op(eng, lambda e, pt=pt, mi=mi: e.tensor_tensor(out=pt[:], in0=pt[:], in1=amask[:, mi, :], op=ALU.mult), reads=[pt, amask], writes=[pt])
                    pts.append((pt, ks))
                ptsd[k] = pts

            def s2(k, Vb=Vb, Zb=Zb, Ab=Ab):
                nb, j, kbs = items2[k]
                pts = ptsd.pop(k)
                pO, pD = psO[k % 2], psD[k % 2]
                nk = len(pts)
                mm([lambda e, pO=pO, pt=pt, ks=ks, j=j, i=i, nk=nk, Vb=Vb: e.matmul(pO[:], Vb[:, ks, j * 64:(j + 1) * 64], pt[:], start=(i == 0), stop=(i == nk - 1)) for i, (pt, ks) in enumerate(pts)],
                   reads=[Vb] + [p[0] for p in pts], writes=[pO])
                mm([lambda e, pD=pD, pt=pt, i=i: e.matmul(pD[:], cstbf[:, ONES, 0:64], pt[:], start=(i == 0), stop=False) for i, (pt, ks) in enumerate(pts)]
                   + [lambda e, pD=pD, j=j: e.matmul(pD[:], ones2[:], skrow[:, 4 * j:4 * j + 4, :], start=False, stop=True)],
                   reads=[cstbf, ones2, skrow] + [p[0] for p in pts], writes=[pD])
                ld_, zr_ = lnd[k % 2], zr[k % 2]
                op("scalar", lambda e, ld_=ld_, pD=pD: e.activation(out=ld_[:], in_=pD[:], func=AF.Ln), reads=[pD], writes=[ld_])
                op("scalar", lambda e, ld_=ld_: e.activation(out=ld_[:], in_=ld_[:], func=AF.Exp, scale=-1.0), reads=[ld_], writes=[ld_])
                op("vector", lambda e, zr_=zr_, ld_=ld_, j=j, nb=nb, Zb=Zb: e.tensor_tensor(out=zr_[:].rearrange("p (g q) -> p g q", g=4), in0=ld_[:].rearrange("p (g q) -> p g q", g=4), in1=Zb[:, 4 * j:4 * j + 4, nb * 128:(nb + 1) * 128], op=ALU.mult), reads=[ld_, Zb], writes=[zr_])
                op("vector", lambda e, zr_=zr_, pO=pO, j=j, nb=nb, Ab=Ab: e.tensor_tensor(out=Ab[:, 4 * j:4 * j + 4, nb * 128:(nb + 1) * 128], in0=pO[:].rearrange("p (g q) -> p g q", g=4), in1=zr_[:].rearrange("p (g q) -> p g q", g=4), op=ALU.mult), reads=[pO, zr_], writes=[Ab])

            NI = len(items2)
            s1(0)
            for k in range(NI):
                if k + 1 < NI:
                    s1(k + 1)
                s2(k)
            dma(AGT[:, t * TT:(t + 1) * TT].rearrange("(h d) t -> d h t", d=64), Ab[:], reads=[Ab], slot=Ab)
        sync_stores(AGs)
        phase_end(mark)

        mark = phase_scope()
        G = P.sb("G", [128, NBLK, 32], F32)
        SP_ = P.sb("SP", [128, 2, NBLK, 8], F32)
        EA = P.sb("EA", [128, 2, NBLK, 8], F32)
        EB = P.sb("EB", [128, 2, NBLK, 8], F32)
        EBT = P.sb("EBT", [128, 2, NBLK, 8], F32)
        WK = P.sb("WK", [128, 2, NBLK, 8], F32)
        dma(G[:], GTS.rearrange("(n p) c -> p n c", p=128), writes=[G], slot=G)
        NG = NBLK * 8
        GCH = 384 // 8
        psT = [P.ps("psT%d" % i, [128, 512]) for i in range(1)]
        psK = P.ps("psK", [128, 512])
        psG = [psT[0], psK]
        for d_ in range(2):
            fcol = 8 + 16 * d_
            icol = 16 * d_
            op("scalar", lambda e, d_=d_, fcol=fcol: e.activation(out=SP_[:, d_, :, :], in_=G[:, :, fcol:fcol + 8], func=AF.Exp, scale=-1.0), reads=[G], writes=[SP_])
            op("scalar", lambda e, d_=d_: e.activation(out=SP_[:, d_, :, :], in_=SP_[:, d_, :, :], func=AF.Ln, bias=1.0, scale=1.0), reads=[SP_], writes=[SP_])
            tri = TRIU if d_ == 0 else TRIL
            for c0 in range(0, NBLK, GCH):
                nbk = min(GCH, NBLK - c0)
                pg, pt_ = psG[0], psG[1]
                mm([lambda e, pg=pg, c0=c0, nbk=nbk, d_=d_, tri=tri: e.matmul(pg[:, 0:nbk * 8], cst32[:, tri, :], SP_[:, d_, c0:c0 + nbk, :], start=True, stop=True)], reads=[cst32, SP_], writes=[pg])
                mm([lambda e, pt_=pt_, c0=c0, nbk=nbk, d_=d_: e.matmul(pt_[:, 0:nbk * 8], cst32[:, ONES, :], SP_[:, d_, c0:c0 + nbk, :], start=True, stop=True)], reads=[cst32, SP_], writes=[pt_])
                op("vector", lambda e, pg=pg, c0=c0, nbk=nbk, d_=d_, icol=icol: e.tensor_tensor(out=EA[:, d_, c0:c0 + nbk, :], in0=G[:, c0:c0 + nbk, icol:icol + 8], in1=pg[:, 0:nbk * 8].rearrange("p (n h) -> p n h", h=8), op=ALU.add), reads=[G, pg], writes=[EA])
                op("scalar", lambda e, c0=c0, nbk=nbk, d_=d_: e.activation(out=EA[:, d_, c0:c0 + nbk, :], in_=EA[:, d_, c0:c0 + nbk, :], func=AF.Exp), reads=[EA], writes=[EA])
                op("scalar", lambda e, pg=pg, c0=c0, nbk=nbk, d_=d_: e.activation(out=EB[:, d_, c0:c0 + nbk, :], in_=pg[:, 0:nbk * 8].rearrange("p (n h) -> p n h", h=8), func=AF.Exp, scale=-1.0), reads=[pg], writes=[EB])
                op("scalar", lambda e, pt_=pt_, c0=c0, nbk=nbk, d_=d_: e.activation(out=EBT[:, d_, c0:c0 + nbk, :], in_=pt_[:, 0:nbk * 8].rearrange("p (n h) -> p n h", h=8), func=AF.Exp, scale=-1.0), reads=[pt_], writes=[EBT])
                op("vector", lambda e, c0=c0, nbk=nbk, d_=d_: e.tensor_tensor(out=WK[:, d_, c0:c0 + nbk, :], in0=EA[:, d_, c0:c0 + nbk, :], in1=EBT[:, d_, c0:c0 + nbk, :], op=ALU.mult), reads=[EA, EBT], writes=[WK])

        WDT = 132
        SW = 160
        GR = [(0, 3), (3, 3), (6, 2)]
        Cst = P.sb("Cst", [128, 8, 128], F32)
        Cbf2 = [P.sb("Cbf%d" % i, [128, 8, WDT], BF16) for i in range(2)]
        n8 = P.sb("n8", [128, 8], F32)
        Cg = [Buf(Cst.t) for _ in GR]
        TQ = [P.sb("TQ%d" % i, [128, 8, TT], BF16) for i in range(2)]
        TK = [P.sb("TK%d" % i, [128, 8, TT], BF16) for i in range(2)]
        TV = P.sb("TV", [128, TB, 8, 128], BF16)
        TV1 = [P.sb("TV1_%d" % i, [128, TB, 8, WDT], BF16) for i in range(2)]
        TO = [P.sb("TO%d" % i, [128, TB, 1024], BF16) for i in range(2)]
        TZ = [P.sb("TZ%d" % i, [128, TB, 1024], BF16) for i in range(2)]
        THB = [P.sb("THB%d" % i, [128, TB, 8, 128], F32) for i in range(2)]
        hs_ = [P.sb("hs%d" % i, [128, 8, 128], F32) for i in range(2)]
        hq_ = [P.sb("hq%d" % i, [128, 8, 128], F32) for i in range(2)]
        hn_ = [P.sb("hn%d" % i, [128, 1024], BF16) for i in range(2)]
        p8 = [P.sb("p8_%d" % i, [128, 8, 128], BF16) for i in range(3)]
        k8 = [P.sb("k8_%d" % i, [128, 8, 128], BF16) for i in range(3)]
        rr = [P.sb("rr%d" % i, [128, 8], F32) for i in range(2)]
        rt_ = [P.sb("rt%d" % i, [128, 8], F32) for i in range(2)]
        ss8 = [P.sb("ss8_%d" % i, [128, 8], F32) for i in range(2)]
        psX = [P.ps("psX%d" % i, [128, 512]) for i in range(3)]
        psC = [P.ps("psC%d" % i, [128, 512]) for i in range(3)]
        mqv = MQT.rearrange("(h d) t -> d h t", d=128)
        mkv = MKT.rearrange("(h d) t -> d h t", d=128)
        cur = [0]
        for i in range(2):
            op("gpsimd", lambda e, i=i: e.memset(TV1[i][:, :, :, 128:WDT], 1.0), writes=[TV1[i]])

        def grp(ps, nh, lo, hi):
            return ps[:, 0:nh * SW].rearrange("p (a b) -> p a b", b=SW)[:, :, lo:hi]

        def n_to_cbf(c):
            op("vector", lambda e, c=c: e.tensor_copy(out=Cbf2[c][:, :, 128:129], in_=bass.AP(n8.t, 0, [[8, 128], [1, 8], [1, 1]])), reads=[n8], writes=[Cbf2[c]])

        def reset_state():
            c = cur[0]
            for g in range(3):
                h0, nh = GR[g]
                op("gpsimd", lambda e, h0=h0, nh=nh: e.memset(Cst[:, h0:h0 + nh, :], 0.0), writes=[Cg[g]])
            op("gpsimd", lambda e, c=c: e.memset(Cbf2[c][:], 0.0), writes=[Cbf2[c]])
            op("vector", lambda e: e.memset(n8[:], 0.0), writes=[n8])

        def link_state(b):
            c = cur[0]
            fl = flg[:, b:b + 1]
            for g in range(3):
                h0, nh = GR[g]
                op("gpsimd", lambda e, h0=h0, nh=nh: e.tensor_scalar(out=Cst[:, h0:h0 + nh, :], in0=Cst[:, h0:h0 + nh, :], scalar1=fl, scalar2=None, op0=ALU.mult), reads=[flg, Cg[g]], writes=[Cg[g]])
                op("scalar", lambda e, h0=h0, nh=nh, c=c: e.activation(out=Cbf2[c][:, h0:h0 + nh, 0:128], in_=Cst[:, h0:h0 + nh, :], func=AF.Copy), reads=[Cg[g]], writes=[Cbf2[c]])
            op("vector", lambda e: e.tensor_scalar(out=n8[:], in0=n8[:], scalar1=fl, scalar2=None, op0=ALU.mult), reads=[flg, n8], writes=[n8])
            n_to_cbf(c)

        seq = [(1, n) for n in range(NBLK - 1, -1, -1)] + [(0, n) for n in range(NBLK)]

        def ptile(i):
            return i // TB

        def tile_blocks(pt):
            d_ = 1 if pt < NT else 0
            tt = (NT - 1 - pt) if d_ == 1 else (pt - NT)
            return d_, tt

        def tv_copy(pt):
            v1 = TV1[pt % 2]
            op("gpsimd", lambda e, v1=v1: e.tensor_copy(out=v1[:, :, :, 0:128], in_=TV[:]), reads=[TV], writes=[v1])

        def loads_tile(pt):
            d_, tt = tile_blocks(pt)
            q_, k_, v1 = TQ[pt % 2], TK[pt % 2], TV1[pt % 2]
            dma(q_[:], mqv[:, :, tt * TT:(tt + 1) * TT], writes=[q_], slot=q_)
            dma(k_[:], mkv[:, :, tt * TT:(tt + 1) * TT], writes=[k_], slot=k_)
            dma(TV[:], MV[tt * TT:(tt + 1) * TT, :].rearrange("(b p) (h e) -> p b h e", p=128, h=8), writes=[TV], slot=TV)
            if pt == 0:
                tv_copy(pt)
            if d_ == 0 and pt > NT:
                loadsB_tile(pt)

        def loadsB_tile(pt):
            d_, tt = tile_blocks(pt)
            o_b, z_b, h_b = TO[pt % 2], TZ[pt % 2], THB[pt % 2]
            dma(o_b[:], MO[tt * TT:(tt + 1) * TT, :].rearrange("(b p) c -> p b c", p=128), writes=[o_b], slot=o_b)
            dma(z_b[:], MZ[tt * TT:(tt + 1) * TT, :].rearrange("(b p) c -> p b c", p=128), writes=[z_b], slot=z_b)
            dma(h_b[:], HB[tt * TT:(tt + 1) * TT, :].rearrange("(b p) (h e) -> p b h e", p=128, h=8), writes=[h_b], slot=h_b)

        def stageA1(i):
            d_, n = seq[i]
            i3 = i % 3
            tri = TRIU if d_ == 0 else TRIL
            pt2 = ptile(i) % 2
            nbk = n % TB
            bs = slice(nbk * 128, (nbk + 1) * 128)
            q_, k_ = TQ[pt2], TK[pt2]
            pt_, kt_ = p8[i3], k8[i3]
            for hf in range(2):
                pst, psk = psT[0], psK
                for hh in range(4):
                    h = 4 * hf + hh
                    mm([lambda e, pst=pst, hh=hh, h=h, k_=k_, q_=q_, bs=bs: e.matmul(pst[:, hh * 128:(hh + 1) * 128], k_[:, h, bs], q_[:, h, bs], start=True, stop=True)], reads=[k_, q_], writes=[pst])
                for hh in range(4):
                    h = 4 * hf + hh
                    mm([lambda e, psk=psk, hh=hh, h=h, k_=k_, bs=bs: e.matmul(psk[:, hh * 128:(hh + 1) * 128], k_[:, h, bs], cstbf[:, IDENT, :], start=True, stop=True)], reads=[k_, cstbf], writes=[psk])
                for hh in range(4):
                    h = 4 * hf + hh
                    op("vector", lambda e, pt_=pt_, pst=pst, hh=hh, h=h, n=n, d_=d_, tri=tri: e.scalar_tensor_tensor(out=pt_[:, h, :], in0=pst[:, hh * 128:(hh + 1) * 128], scalar=EA[:, d_, n, h:h + 1], in1=cst32[:, tri, :], op0=ALU.mult, op1=ALU.mult), reads=[pst, EA, cst32], writes=[pt_])
                    op("scalar", lambda e, kt_=kt_, psk=psk, hh=hh, h=h, n=n, d_=d_: e.activation(out=kt_[:, h, :], in_=psk[:, hh * 128:(hh + 1) * 128], func=AF.Copy, scale=WK[:, d_, n, h:h + 1]), reads=[psk, WK], writes=[kt_])

        def stageA2(i):
            d_, n = seq[i]
            i3 = i % 3
            v1 = TV1[ptile(i) % 2]
            nbk = n % TB
            kt_ = k8[i3]
            for g in range(3):
                h0, nh = GR[g]
                pc = psC[g]
                for j in range(nh):
                    h = h0 + j
                    mm([lambda e, pc=pc, j=j, h=h, kt_=kt_, v1=v1, nbk=nbk: e.matmul(pc[:, j * SW:j * SW + 129], kt_[:, h, :], v1[:, nbk, h, 0:129], start=True, stop=True)], reads=[kt_, v1], writes=[pc])

        cidx = {}

        def stageB_update(i):
            d_, n = seq[i]
            i2, i3 = i % 2, i % 3
            seg, nis = n // SEGB, n % SEGB
            pt2 = ptile(i) % 2
            nbk = n % TB
            bs = slice(nbk * 128, (nbk + 1) * 128)
            q_, v1 = TQ[pt2], TV1[pt2]
            pt_ = p8[i3]
            first_in_pass = (n == NBLK - 1) if d_ == 1 else (n == 0)
            first_in_seg = (nis == SEGB - 1) if d_ == 1 else (nis == 0)
            if first_in_pass:
                reset_state()
            elif first_in_seg:
                link_state(seg if d_ == 1 else seg - 1)
            c = cur[0]
            nx = 1 - c
            Cb = Cbf2[c]
            op("vector", lambda e, n=n, d_=d_: e.tensor_tensor(out=n8[:], in0=n8[:], in1=EBT[:, d_, n, :], op=ALU.mult), reads=[n8, EBT], writes=[n8])
            for g in range(3):
                h0, nh = GR[g]
                pc = psC[g]
                for j in range(nh):
                    h = h0 + j
                    op("vector", lambda e, h=h, j=j, pc=pc, n=n, d_=d_: e.scalar_tensor_tensor(out=Cst[:, h, :], in0=Cst[:, h, :], scalar=EBT[:, d_, n, h:h + 1], in1=pc[:, j * SW:j * SW + 128], op0=ALU.mult, op1=ALU.add), reads=[pc, Cg[g], EBT], writes=[Cg[g]])
                op("vector", lambda e, h0=h0, nh=nh, pc=pc: e.tensor_tensor(out=bass.AP(n8.t, h0, [[8, 128], [1, nh], [1, 1]]), in0=bass.AP(n8.t, h0, [[8, 128], [1, nh], [1, 1]]), in1=grp(pc, nh, 128, 129), op=ALU.add), reads=[n8, pc], writes=[n8])
                op("scalar", lambda e, h0=h0, nh=nh, nx=nx: e.activation(out=Cbf2[nx][:, h0:h0 + nh, 0:128], in_=Cst[:, h0:h0 + nh, :], func=AF.Copy), reads=[Cg[g]], writes=[Cbf2[nx]])
            n_to_cbf(nx)
            cur[0] = nx
            cidx[i] = c

        def stageB(i):
            d_, n = seq[i]
            i2, i3 = i % 2, i % 3
            pt2 = ptile(i) % 2
            nbk = n % TB
            bs = slice(nbk * 128, (nbk + 1) * 128)
            q_, v1 = TQ[pt2], TV1[pt2]
            pt_ = p8[i3]
            c = cidx[i]
            Cb = Cbf2[c]
            for g in range(3):
                h0, nh = GR[g]
                psx = psX[g]
                for j in range(nh):
                    h = h0 + j
                    mm([lambda e, psx=psx, j=j, h=h, q_=q_, Cb=Cb, bs=bs: e.matmul(psx[:, j * SW:j * SW + 129], q_[:, h, bs], Cb[:, h, 0:129], start=True, stop=False),
                        lambda e, psx=psx, j=j, h=h, pt_=pt_, v1=v1, nbk=nbk: e.matmul(psx[:, j * SW:j * SW + 129], pt_[:, h, :], v1[:, nbk, h, 0:129], start=False, stop=True)],
                       reads=[q_, Cb, pt_, v1], writes=[psx])
            hq = hq_[i2]
            r_, t_ = rr[i2], rt_[i2]
            for g in range(3):
                h0, nh = GR[g]
                psx = psX[g]
                op("scalar", lambda e, h0=h0, nh=nh, psx=psx, t_=t_: e.activation(out=bass.AP(t_.t, h0, [[8, 128], [1, nh], [1, 1]]), in_=grp(psx, nh, 128, 129), func=AF.Abs), reads=[psx], writes=[t_])
                op("scalar", lambda e, h0=h0, nh=nh, psx=psx, hq=hq: e.activation(out=hq[:, h0:h0 + nh, :], in_=grp(psx, nh, 0, 128), func=AF.Copy), reads=[psx], writes=[hq])
            op("vector", lambda e, t_=t_, n=n, d_=d_: e.tensor_tensor(out=t_[:], in0=t_[:], in1=EB[:, d_, n, :], op=ALU.mult), reads=[t_, EB], writes=[t_])
            op("vector", lambda e, t_=t_: e.tensor_scalar_max(out=t_[:], in0=t_[:], scalar1=1.0), reads=[t_], writes=[t_])
            op("vector", lambda e, t_=t_, r_=r_: e.reciprocal(out=r_[:], in_=t_[:]), reads=[t_], writes=[r_])
            op("vector", lambda e, r_=r_, n=n, d_=d_: e.tensor_tensor(out=r_[:], in0=r_[:], in1=EB[:, d_, n, :], op=ALU.mult), reads=[r_, EB], writes=[r_])
            r_bc = bass.AP(r_.t, 0, [[8, 128], [1, 8], [0, 128]])
            hs = hs_[i2]
            if d_ == 1:
                op("gpsimd", lambda e, hs=hs, hq=hq, r_bc=r_bc: e.tensor_tensor(out=hs[:], in0=hq[:], in1=r_bc, op=ALU.mult), reads=[hq, r_], writes=[hs])
                dma(HB[n * 128:(n + 1) * 128, :].rearrange("p (h e) -> p h e", h=8), hs[:], reads=[hs], slot=hs)
            else:
                o_b, z_b, h_b = TO[pt2], TZ[pt2], THB[pt2]
                hn = hn_[i2]
                s8 = ss8[i2]
                for h in range(8):
                    op("vector", lambda e, hs=hs, hq=hq, h_b=h_b, nbk=nbk, h=h, r_=r_: e.scalar_tensor_tensor(out=hs[:, h, :], in0=hq[:, h, :], scalar=r_[:, h:h + 1], in1=h_b[:, nbk, h, :], op0=ALU.mult, op1=ALU.add), reads=[hq, r_, h_b], writes=[hs])
                op("gpsimd", lambda e, hs=hs, o_b=o_b, nbk=nbk: e.tensor_tensor(out=hs[:], in0=hs[:], in1=o_b[:, nbk, :].rearrange("p (h e) -> p h e", h=8), op=ALU.mult), reads=[hs, o_b], writes=[hs])
                op("vector", lambda e, s8=s8: e.memset(s8[:], 0.0), writes=[s8])
                for h in range(8):
                    op("scalar", lambda e, hs=hs, hq=hq, h=h, s8=s8: e.activation(out=hq[:, h, :], in_=hs[:, h, :], func=AF.Square, accum_out=s8[:, h:h + 1]), reads=[hs, s8], writes=[hq, s8])
                op("scalar", lambda e, s8=s8: e.activation(out=s8[:], in_=s8[:], func=AF.Ln, bias=EPS, scale=1.0 / 128), reads=[s8], writes=[s8])
                op("scalar", lambda e, s8=s8: e.activation(out=s8[:], in_=s8[:], func=AF.Exp, scale=-0.5), reads=[s8], writes=[s8])
                s_bc = bass.AP(s8.t, 0, [[8, 128], [1, 8], [0, 128]])
                op("gpsimd", lambda e, hs=hs, s_bc=s_bc: e.tensor_tensor(out=hs[:], in0=hs[:], in1=s_bc, op=ALU.mult), reads=[hs, s8], writes=[hs])
                op("gpsimd", lambda e, hs=hs, hn=hn, z_b=z_b, nbk=nbk: e.tensor_tensor(out=hn[:].rearrange("p (h e) -> p h e", h=8), in0=hs[:], in1=z_b[:, nbk, :].rearrange("p (h e) -> p h e", h=8), op=ALU.mult), reads=[hs, z_b], writes=[hn])
                dma(HN[n * 128:(n + 1) * 128, :], hn[:], reads=[hn], slot=hn)

        NS = len(seq)
        loads_tile(0)
        stageA1(0)
        stageA2(0)
        for i in range(NS):
            if i % TB == 0 and ptile(i) + 1 < 2 * NT:
                loads_tile(ptile(i) + 1)
            stageB_update(i)
            if i + 1 < NS:
                stageA1(i + 1)
            stageB(i)
            if i % TB == TB - 2 and ptile(i) + 1 < 2 * NT:
                tv_copy(ptile(i) + 1)
            if i + 1 < NS:
                if seq[i + 1][0] == 0 and seq[i][0] == 1:
                    sync_stores(hs_)
                    loadsB_tile(NT)
                stageA2(i + 1)
        sync_stores(hn_)
        phase_end(mark)

        mark = phase_scope()
        T4 = min(256, TT)
        NT4 = T // T4
        B4 = T4 // 128
        Wa = P.sb("Wa", [128, 8, D], BF16)
        Wm = P.sb("Wm", [128, 8, D], BF16)
        Wo = P.sb("Wo", [128, 8, D], BF16)
        mgT = P.sb("mgT", [128, 8], F32)
        dma(mgT[:], mgT_d[l], writes=[mgT], slot=mgT)
        w4s = [P.sb("w4s%d" % i, [128, 8, 512], F32) for i in range(2)]
        ci = 0
        for (Wd, Wsb, scaled) in ((wa_d, Wa, False), (wm_d, Wm, True), (wo_d, Wo, False)):
            wvv = Wd[l].rearrange("(kc p) c -> p kc c", p=128)
            for c0 in (0, 512):
                ws_ = w4s[ci % 2]
                ci += 1
                dma(ws_[:], wvv[:, :, c0:c0 + 512], writes=[ws_], slot=ws_)
                for kc in range(8):
                    eng = ("gpsimd", "vector", "scalar")[kc % 3]
                    if scaled:
                        if eng == "scalar":
                            op(eng, lambda e, kc=kc, ws_=ws_, Wsb=Wsb, c0=c0: e.activation(out=Wsb[:, kc, c0:c0 + 512], in_=ws_[:, kc, :], func=AF.Copy, scale=mgT[:, kc:kc + 1]), reads=[ws_, mgT], writes=[Wsb])
                        else:
                            op(eng, lambda e, kc=kc, ws_=ws_, Wsb=Wsb, c0=c0: e.tensor_scalar(out=Wsb[:, kc, c0:c0 + 512], in0=ws_[:, kc, :], scalar1=mgT[:, kc:kc + 1], scalar2=None, op0=ALU.mult), reads=[ws_, mgT], writes=[Wsb])
                    else:
                        if eng == "scalar":
                            op(eng, lambda e, kc=kc, ws_=ws_, Wsb=Wsb, c0=c0: e.activation(out=Wsb[:, kc, c0:c0 + 512], in_=ws_[:, kc, :], func=AF.Copy), reads=[ws_], writes=[Wsb])
                        else:
                            op(eng, lambda e, kc=kc, ws_=ws_, Wsb=Wsb, c0=c0: e.tensor_copy(out=Wsb[:, kc, c0:c0 + 512], in_=ws_[:, kc, :]), reads=[ws_], writes=[Wsb])
        ag = [P.sb("ag%d" % i, [128, 8, T4], BF16) for i in range(2)]
        hnt = [P.sb("hnt%d" % i, [128, B4, D], BF16) for i in range(2)]
        hg = [P.sb("hg%d" % i, [128, 8, T4], BF16) for i in range(2)]
        ga = [P.sb("ga%d" % i, [128, 16, T4], BF16) for i in range(2)]
        x4 = [P.sb("x4_%d" % i, [128, 8, T4], F32) for i in range(2)]
        mgd = [P.sb("mgd%d" % i, [128, 8, T4], BF16) for i in range(2)]
        yo = [P.sb("yo%d" % i, [128, 8, T4], F32) for i in range(2)]
        ta_ = [P.sb("ta%d" % i, [128, T4], F32) for i in range(2)]
        tb_ = [P.sb("tb%d" % i, [128, T4], F32) for i in range(2)]
        psTr = [P.ps("psTr%d" % i, [128, T4]) for i in range(2)]
        psAo = [P.ps("psAo%d" % i, [128, T4]) for i in range(2)]
        psMo = [P.ps("psMo%d" % i, [128, T4]) for i in range(2)]
        psYo = [P.ps("psYo%d" % i, [128, T4]) for i in range(2)]
        xsv = x_src.rearrange("(c p) t -> p c t", p=128)
        xdv = x_dst.rearrange("(c p) t -> p c t", p=128)
        for t in range(NT4):
            i2 = t % 2
            sl = slice(t * T4, (t + 1) * T4)
            a_, hn4, hg_, ga_, x_, mg_, y_ = ag[i2], hnt[i2], hg[i2], ga[i2], x4[i2], mgd[i2], yo[i2]
            dma(a_[:], AGT[:, sl].rearrange("(c p) t -> p c t", p=128), writes=[a_], slot=a_)
            dma(hn4[:], HN[sl, :].rearrange("(b p) c -> p b c", p=128), writes=[hn4], slot=hn4)
            dma(ga_[:], GAT[:, sl].rearrange("(c p) t -> p c t", p=128), writes=[ga_], slot=ga_)
            dma(x_[:], xsv[:, :, sl], writes=[x_], slot=x_)
            for c in range(8):
                ptr = psTr[c % 2]
                mm([lambda e, ptr=ptr, b=b, c=c, hn4=hn4: e.matmul(ptr[:, b * 128:(b + 1) * 128], hn4[:, b, c * 128:(c + 1) * 128], cstbf[:, IDENT, :], start=True, stop=True) for b in range(B4)],
                   reads=[hn4, cstbf], writes=[ptr])
                eng = "scalar" if c % 2 == 0 else "vector"
                if eng == "scalar":
                    op(eng, lambda e, ptr=ptr, hg_=hg_, c=c: e.activation(out=hg_[:, c, :], in_=ptr[:], func=AF.Copy), reads=[ptr], writes=[hg_])
                else:
                    op(eng, lambda e, ptr=ptr, hg_=hg_, c=c: e.tensor_copy(out=hg_[:, c, :], in_=ptr[:]), reads=[ptr], writes=[hg_])
            for oc in range(8):
                pa, pm = psAo[oc % 2], psMo[oc % 2]
                mm([lambda e, pa=pa, kc=kc, oc=oc, a_=a_: e.matmul(pa[:], Wa[:, kc, oc * 128:(oc + 1) * 128], a_[:, kc, :], start=(kc == 0), stop=(kc == 7)) for kc in range(8)], reads=[Wa, a_], writes=[pa])
                mm([lambda e, pm=pm, kc=kc, oc=oc, hg_=hg_: e.matmul(pm[:], Wm[:, kc, oc * 128:(oc + 1) * 128], hg_[:, kc, :], start=(kc == 0), stop=(kc == 7)) for kc in range(8)], reads=[Wm, hg_], writes=[pm])
                ta, tb = ta_[oc % 2], tb_[oc % 2]
                op("vector", lambda e, ta=ta, pa=pa, ga_=ga_, oc=oc: e.tensor_tensor(out=ta[:], in0=pa[:], in1=ga_[:, oc, :], op=ALU.mult), reads=[pa, ga_], writes=[ta])
                op("vector", lambda e, tb=tb, pm=pm, ga_=ga_, oc=oc: e.tensor_tensor(out=tb[:], in0=pm[:], in1=ga_[:, 8 + oc, :], op=ALU.mult), reads=[pm, ga_], writes=[tb])
                op("gpsimd", lambda e, ta=ta, tb=tb, mg_=mg_, oc=oc: e.tensor_tensor(out=mg_[:, oc, :], in0=ta[:], in1=tb[:], op=ALU.add), reads=[ta, tb], writes=[mg_])
            for oc in range(8):
                py = psYo[oc % 2]
                mm([lambda e, py=py, kc=kc, oc=oc, mg_=mg_: e.matmul(py[:], Wo[:, kc, oc * 128:(oc + 1) * 128], mg_[:, kc, :], start=(kc == 0), stop=(kc == 7)) for kc in range(8)], reads=[Wo, mg_], writes=[py])
                op("vector", lambda e, py=py, y_=y_, x_=x_, oc=oc: e.tensor_tensor(out=y_[:, oc, :], in0=py[:], in1=x_[:, oc, :], op=ALU.add), reads=[py, x_], writes=[y_])
            dma(xdv[:, :, sl], y_[:], reads=[y_], slot=y_)
        sync_stores(yo)
        phase_end(mark)

    with nc.Block() as block:
        P.replay(block)
    P.close()
    return nc


def make_consts():
    c = np.zeros((128, 6, 128), np.float32)
    i = np.arange(128)
    c[:, 0, :] = np.eye(128)
    c[:, 1, :] = (i[:, None] <= i[None, :])
    c[:, 2, :] = (i[:, None] >= i[None, :])
    c[:, 3, :] = (i[:, None] // 64 == i[None, :] // 64)
    Pm = np.zeros((128, 128), np.float32)
    for m in range(128):
        d = m % 64
        if d < 8:
            Pm[m + 8, m] = -1.0
        elif d < 16:
            Pm[m - 8, m] = 1.0
    c[:, 4, :] = Pm
    c[:, 5, :] = 1.0
    return c


def rope_tables(pos):
    T = pos.shape[0]
    half = 8
    inv = np.power(np.float32(500000.0), -np.arange(half, dtype=np.float32) * np.float32(2.0) / np.float32(16)).astype(np.float32)
    ang = (pos.astype(np.float32)[:, None] * inv[None, :]).astype(np.float32)
    cos = np.cos(ang).astype(np.float32)
    sin = np.sin(ang).astype(np.float32)
    ct = np.ones((128, T), np.float32)
    st = np.zeros((128, T), np.float32)
    for p in range(128):
        d = p % 64
        if d < 16:
            ct[p] = cos[:, d % 8]
            st[p] = sin[:, d % 8]
    rk = np.stack([ct, st]).astype(np.float32)
    rq = (rk * np.float32(0.125)).astype(np.float32)
    return rq, rk


def prep_weights(inp, L):
    f = lambda a: np.ascontiguousarray(np.asarray(a, dtype=np.float32))
    b_in = f(inp["b_in"])[:L]
    starts = ([O_AQ + 128 * i for i in range(8)] + [O_AK + 128 * i for i in range(2)] + [O_AZ + 128 * i for i in range(8)]
              + [O_MQ + 128 * i for i in range(16)] + [O_MG + 128 * i for i in range(16)])
    biasT = np.ascontiguousarray(np.stack([b_in[:, st:st + 128] for st in starts], axis=-1))
    gT = np.ascontiguousarray(f(inp["norm_g"])[:L].reshape(L, 8, 128).transpose(0, 2, 1))
    mgT = np.ascontiguousarray(f(inp["m_norm_g"])[:L].reshape(L, 8, 128).transpose(0, 2, 1))
    gq = np.tile(f(inp["q_norm_g"])[:L], (1, 2))
    gk = np.tile(f(inp["k_norm_g"])[:L], (1, 2))
    gqk = np.ascontiguousarray(np.stack([gq, gk], axis=-1))
    cwT = np.ascontiguousarray(f(inp["conv_w"])[:L].reshape(L, 3, 16, 128).transpose(0, 3, 2, 1))
    return {"w_in": f(inp["w_in"])[:L], "b_in": b_in, "biasT": biasT, "gT": gT, "mgT": mgT, "gqk": gqk,
            "sink": f(inp["sink"])[:L], "cwT": cwT, "w_att_out": f(inp["w_att_out"])[:L],
            "w_m_out": f(inp["w_m_out"])[:L], "w_out": f(inp["w_out"])[:L], "consts": make_consts()}


_NC_CACHE = {}


def kernel(x_prompt, x_sample, norm_g, w_in, b_in, q_norm_g, k_norm_g, sink, conv_w, m_norm_g, w_att_out, w_m_out, w_out):
    inp = dict(norm_g=norm_g, w_in=w_in, b_in=b_in, q_norm_g=q_norm_g, k_norm_g=k_norm_g, sink=sink, conv_w=conv_w,
               m_norm_g=m_norm_g, w_att_out=w_att_out, w_m_out=w_m_out, w_out=w_out)
    cfg = Cfg(3, 16, 4, 4)
    xp = np.asarray(x_prompt, np.float32)
    xs = np.asarray(x_sample, np.float32)
    wts = prep_weights(inp, 4)
    SEG = 2048
    in_maps = []
    for c in range(8):
        if c < 4:
            segs = [xp[c, :SEG], xp[c, SEG:], xs[c]]
            pos = np.concatenate([np.arange(4096), np.arange(2048)]).astype(np.float32)
            fl = [1.0, 0.0]
        else:
            j = 4 + 3 * (c - 4)
            segs = [xs[j], xs[j + 1], xs[j + 2]]
            pos = np.concatenate([np.arange(2048)] * 3).astype(np.float32)
            fl = [0.0, 0.0]
        xT = np.ascontiguousarray(np.concatenate(segs, axis=0).T)
        rq, rk = rope_tables(pos)
        m = dict(wts)
        m.update({"xT": xT, "flags": np.tile(np.array(fl, np.float32)[None, :], (128, 1)), "ropeq": rq, "ropek": rk})
        in_maps.append(m)
    if "nc" not in _NC_CACHE:
        _NC_CACHE["nc"] = build(cfg)
    res = run_bass_kernel_spmd(_NC_CACHE["nc"], in_maps, core_ids=list(range(8)))
    yp = np.empty_like(xp)
    ys = np.empty_like(xs)
    for c in range(8):
        y = np.asarray(res.results[c]["yT"], np.float32).T
        if c < 4:
            yp[c, :SEG] = y[:SEG]
            yp[c, SEG:] = y[SEG:2 * SEG]
            ys[c] = y[2 * SEG:]
        else:
            j = 4 + 3 * (c - 4)
            for i in range(3):
                ys[j + i] = y[i * SEG:(i + 1) * SEG]
    return (yp, ys)
```

```python
import numpy as np
import concourse.bass as bass
import concourse.mybir as mybir
from concourse.bass_utils import run_bass_kernel_spmd

F32 = mybir.dt.float32
BF16 = mybir.dt.bfloat16
ALU = mybir.AluOpType
AF = mybir.ActivationFunctionType
AX = mybir.AxisListType
ENGS = ("sync", "scalar", "vector", "gpsimd", "tensor")

D = 1024
IN_DIM = 9760
EPS = 1e-6
O_AQ, O_AK, O_AV, O_AZ = 0, 1024, 1280, 1536
O_MQ, O_MK, O_MV, O_MO, O_MZ = 2560, 3584, 4608, 5632, 6656
O_G4, O_MG = 7680, 7712


class Tok:
    __slots__ = ("sem", "val")

    def __init__(self, sem, val):
        self.sem = sem
        self.val = val


class Buf:
    __slots__ = ("t", "wtok", "rtoks", "dsem")

    def __init__(self, t):
        self.t = t
        self.wtok = None
        self.rtoks = []
        self.dsem = None

    def __getitem__(self, idx):
        return self.t[idx]


class Prog:
    def __init__(self, nc):
        self.nc = nc
        self.q = {e: [] for e in ENGS}
        self.esem = {}
        self.ecnt = {e: 0 for e in ENGS}
        self.waited = {e: {} for e in ENGS}
        self.cms = []
        self.sem_cms = []
        self.sem_pool = []
        self.phase_slots = []
        self.pending = []
        self.nsem = 0
        for e in ENGS:
            self.esem[e] = self.new_sem("p_" + e)

    def enter(self, cm):
        v = cm.__enter__()
        self.cms.append(cm)
        return v

    def new_sem(self, name):
        self.nsem += 1
        cm = self.nc.semaphore(name)
        v = cm.__enter__()
        self.sem_cms.append(cm)
        return v

    def sb(self, name, shape, dt):
        self.nsem += 1
        return Buf(self.enter(self.nc.sbuf_tensor("s%d_%s" % (self.nsem, name), shape, dt)))

    def ps(self, name, shape, dt=F32):
        self.nsem += 1
        return Buf(self.enter(self.nc.psum_tensor("p%d_%s" % (self.nsem, name), shape, dt)))

    def wait(self, eng, tok):
        if tok is None:
            return
        w = self.waited[eng]
        k = id(tok.sem)
        if w.get(k, 0) >= tok.val:
            return
        w[k] = tok.val
        self.q[eng].append(lambda e, s=tok.sem, v=tok.val: e.wait_ge(s, v))

    def _deps(self, eng, reads, writes):
        for b in reads:
            self.wait(eng, b.wtok)
        for b in writes:
            self.wait(eng, b.wtok)
            for t in b.rtoks:
                self.wait(eng, t)

    def _post(self, tok, reads, writes):
        for b in reads:
            b.rtoks.append(tok)
            if len(b.rtoks) > 24:
                b.rtoks = b.rtoks[-24:]
        for b in writes:
            b.wtok = tok
            b.rtoks = []

    def op(self, eng, fn, reads=(), writes=()):
        self._deps(eng, reads, writes)
        self.ecnt[eng] += 1
        sem = self.esem[eng]
        tok = Tok(sem, self.ecnt[eng])
        self.q[eng].append(lambda e, fn=fn, sem=sem: fn(e).then_inc(sem, 1))
        self._post(tok, reads, writes)
        return tok

    def mm(self, fns, reads=(), writes=()):
        eng = "tensor"
        self._deps(eng, reads, writes)
        for fn in fns[:-1]:
            self.q[eng].append(lambda e, fn=fn: fn(e))
        self.ecnt[eng] += 1
        sem = self.esem[eng]
        tok = Tok(sem, self.ecnt[eng])
        self.q[eng].append(lambda e, fn=fns[-1], sem=sem: fn(e).then_inc(sem, 1))
        self._post(tok, reads, writes)
        return tok

    def dma(self, out_ap, in_ap, reads=(), writes=(), slot=None, eng="sync"):
        self._deps(eng, reads, writes)
        if slot.dsem is None:
            slot.dsem = self.sem_pool.pop() if self.sem_pool else [self.new_sem("d%d" % self.nsem), 0]
            self.phase_slots.append(slot)
        slot.dsem[1] += 16
        sem, val = slot.dsem
        tok = Tok(sem, val)
        self.q[eng].append(lambda e, o=out_ap, i=in_ap, s=sem: e.dma_start(out=o, in_=i).then_inc(s, 16))
        self._post(tok, reads, writes)
        return tok

    def replay(self, block):
        q = self.q

        @block.sync
        def _(e):
            for f in q["sync"]:
                f(e)

        @block.scalar
        def _(e):
            for f in q["scalar"]:
                f(e)

        @block.vector
        def _(e):
            for f in q["vector"]:
                f(e)

        @block.gpsimd
        def _(e):
            for f in q["gpsimd"]:
                f(e)

        @block.tensor
        def _(e):
            for f in q["tensor"]:
                f(e)

    def close(self):
        for cm in reversed(self.cms):
            cm.__exit__(None, None, None)
        self.cms = []
        for cm in reversed(self.sem_cms):
            cm.__exit__(None, None, None)
        self.sem_cms = []


def bcast_ap(t, off, dims):
    return bass.AP(t, off, dims)


class Cfg:
    def __init__(self, nseg=3, segb=16, tb=4, depth=4):
        self.NSEG, self.SEGB, self.TB, self.DEPTH = nseg, segb, tb, depth
        self.NBLK = nseg * segb
        self.T = self.NBLK * 128
        self.TT = tb * 128
        self.NT = self.NBLK // tb
        self.SEGT = segb * 128
        self.TPS = segb // tb


def build(cfg, debug=False):
    nc = bass.Bass("TRN2", target_bir_lowering=False)
    T, TT, NT, NBLK, NSEG, SEGB, TB, SEGT, L = cfg.T, cfg.TT, cfg.NT, cfg.NBLK, cfg.NSEG, cfg.SEGB, cfg.TB, cfg.SEGT, cfg.DEPTH
    NB = max(NSEG - 1, 1)

    def din(name, shape, dt=F32):
        return nc.dram_tensor(name, list(shape), dt, kind="ExternalInput").ap()

    def dscr(name, shape, dt):
        return nc.dram_tensor(name, list(shape), dt, kind=("ExternalOutput" if debug else "Internal")).ap()

    xT_in = din("xT", [D, T])
    flags_in = din("flags", [128, NB])
    ropeq_in = din("ropeq", [2, 128, T])
    ropek_in = din("ropek", [2, 128, T])
    consts_in = din("consts", [128, 6, 128])
    w_in_d = din("w_in", [L, D, IN_DIM])
    b_in_d = din("b_in", [L, IN_DIM])
    biasT_d = din("biasT", [L, 128, 50])
    gT_d = din("gT", [L, 128, 8])
    mgT_d = din("mgT", [L, 128, 8])
    gqk_d = din("gqk", [L, 128, 2])
    sink_d = din("sink", [L, 16])
    cwT_d = din("cwT", [L, 128, 16, 3])
    wa_d = din("w_att_out", [L, D, D])
    wm_d = din("w_m_out", [L, D, D])
    wo_d = din("w_out", [L, D, D])
    yT_out = nc.dram_tensor("yT", [D, T], F32, kind="ExternalOutput").ap()

    xs_d = dscr("xs", [D, T], F32)
    QT = dscr("QT", [D, T], BF16)
    KT = dscr("KT", [256, T], BF16)
    VA = dscr("VA", [T, 256], BF16)
    ZAT = dscr("ZAT", [D, T], BF16)
    MQT = dscr("MQT", [D, T], BF16)
    MKT = dscr("MKT", [D, T], BF16)
    MV = dscr("MV", [T, D], BF16)
    MO = dscr("MO", [T, D], BF16)
    MZ = dscr("MZ", [T, D], BF16)
    GTS = dscr("GTS", [T, 32], F32)
    GAT = dscr("GAT", [2 * D, T], BF16)
    HB = dscr("HB", [T, D], F32)
    HN = dscr("HN", [T, D], BF16)
    AGT = dscr("AGT", [D, T], BF16)

    P = Prog(nc)
    op, mm, dma = P.op, P.mm, P.dma

    res_bufs = []
    for l in range(L):
        src = xT_in if l == 0 else res_bufs[-1][1]
        dst = yT_out if (L - 1 - l) % 2 == 0 else xs_d
        res_bufs.append((src, dst))

    cst32 = P.sb("cst32", [128, 6, 128], F32)
    cstbf = P.sb("cstbf", [128, 6, 128], BF16)
    flg = P.sb("flg", [128, NB], F32)
    amask = P.sb("amask", [128, 2 + 2 * NB, 512], BF16)
    dma(cst32[:], consts_in, writes=[cst32], slot=cst32)
    dma(flg[:], flags_in, writes=[flg], slot=flg)
    op("vector", lambda e: e.tensor_copy(out=cstbf[:], in_=cst32[:]), reads=[cst32], writes=[cstbf])
    IDENT, TRIU, TRIL, BD64, RPERM, ONES = range(6)
    for g in range(4):
        op("vector", lambda e, g=g: e.tensor_copy(out=amask[:, 0, g * 128:(g + 1) * 128], in_=cst32[:, TRIL, :]), reads=[cst32], writes=[amask])
        op("vector", lambda e, g=g: e.tensor_copy(out=amask[:, 1, g * 128:(g + 1) * 128], in_=cst32[:, TRIU, :]), reads=[cst32], writes=[amask])
    for b in range(NSEG - 1):
        op("vector", lambda e, b=b: e.tensor_scalar(out=amask[:, 2 + 2 * b, :], in0=amask[:, 0, :], scalar1=flg[:, b:b + 1], scalar2=None, op0=ALU.mult), reads=[flg, amask], writes=[amask])
        op("vector", lambda e, b=b: e.tensor_scalar(out=amask[:, 3 + 2 * b, :], in0=amask[:, 1, :], scalar1=flg[:, b:b + 1], scalar2=None, op0=ALU.mult), reads=[flg, amask], writes=[amask])

    def sync_stores(bufs):
        for b in bufs:
            for t in list(b.rtoks) + [b.wtok]:
                if t is not None:
                    P.wait("sync", t)
                    P.pending.append(t)

    def phase_scope():
        return len(P.cms)

    def phase_end(mark):
        toks = list(P.pending)
        P.pending = []
        for e in ("scalar", "vector", "gpsimd", "tensor"):
            if P.ecnt[e] > 0:
                toks.append(Tok(P.esem[e], P.ecnt[e]))
        for e in ENGS:
            for t in toks:
                P.wait(e, t)
        for sl in P.phase_slots:
            P.wait("sync", Tok(sl.dsem[0], sl.dsem[1]))
            P.sem_pool.append(sl.dsem)
            sl.dsem = None
        P.phase_slots = []
        while len(P.cms) > mark:
            cm = P.cms.pop()
            cm.__exit__(None, None, None)

    for l in range(L):
        x_src, x_dst = res_bufs[l]
        mark = phase_scope()
        xn = P.sb("xn", [128, 8, T], BF16)
        xn_tiles = [Buf(xn.t) for _ in range(NT)]
        gT = P.sb("gT", [128, 8], F32)
        biasT = P.sb("biasT", [128, 50], F32)
        hbias = P.sb("hbias", [128, 50], F32)
        gqk = P.sb("gqk", [128, 2], F32)
        cw = P.sb("cw", [128, 16, 3], F32)
        dma(gT[:], gT_d[l], writes=[gT], slot=gT)
        dma(biasT[:], biasT_d[l], writes=[biasT], slot=biasT)
        dma(gqk[:], gqk_d[l], writes=[gqk], slot=gqk)
        dma(cw[:], cwT_d[l], writes=[cw], slot=cw)
        op("vector", lambda e: e.tensor_scalar(out=hbias[:], in0=biasT[:], scalar1=0.5, scalar2=None, op0=ALU.mult), reads=[biasT], writes=[hbias])
        KS = 128.0 ** -0.5
        op("vector", lambda e: e.tensor_scalar(out=cw[:, 0:8, :], in0=cw[:, 0:8, :], scalar1=0.5, scalar2=None, op0=ALU.mult), reads=[cw], writes=[cw])
        op("vector", lambda e: e.tensor_scalar(out=cw[:, 8:16, :], in0=cw[:, 8:16, :], scalar1=0.5 * KS, scalar2=None, op0=ALU.mult), reads=[cw], writes=[cw])

        mark1 = phase_scope()
        xst = [P.sb("xst%d" % i, [128, 8, TT], F32) for i in range(2)]
        sqb = [P.sb("sqb%d" % i, [128, 8, TT], BF16) for i in range(2)]
        rsb = [P.sb("rsb%d" % i, [128, TT], F32) for i in range(2)]
        pss = [P.ps("pss%d" % i, [128, TT]) for i in range(2)]
        xv = x_src.rearrange("(c p) t -> p c t", p=128)
        for t in range(NT):
            xb, sq, rs, pp = xst[t % 2], sqb[t % 2], rsb[t % 2], pss[t % 2]
            dma(xb[:], xv[:, :, t * TT:(t + 1) * TT], writes=[xb], slot=xb)
            op("scalar", lambda e, xb=xb, sq=sq: e.activation(out=sq[:], in_=xb[:], func=AF.Square), reads=[xb], writes=[sq])
            mm([lambda e, c=c, sq=sq, pp=pp: e.matmul(pp[:], cstbf[:, ONES, :], sq[:, c, :], start=(c == 0), stop=(c == 7)) for c in range(8)], reads=[sq, cstbf], writes=[pp])
            op("scalar", lambda e, rs=rs, pp=pp: e.activation(out=rs[:], in_=pp[:], func=AF.Ln, bias=EPS, scale=1.0 / D), reads=[pp], writes=[rs])
            op("scalar", lambda e, rs=rs: e.activation(out=rs[:], in_=rs[:], func=AF.Exp, scale=-0.5), reads=[rs], writes=[rs])
            rs_bc = bass.AP(rs.t, 0, [[TT, 128], [0, 8], [1, TT]])
            op("vector", lambda e, xb=xb, rs_bc=rs_bc, t=t: e.tensor_tensor(out=xn[:, :, t * TT:(t + 1) * TT], in0=xb[:], in1=rs_bc, op=ALU.mult), reads=[xb, rs], writes=[xn_tiles[t]])

        phase_end(mark1)
        WS = 512
        wst = [P.sb("wst%d" % i, [128, 8, WS], F32) for i in range(1)]
        wbf = [P.sb("wbf%d" % i, [128, 8, WS], BF16) for i in range(2)]
        slab_ctr = [0]
        wv = w_in_d[l].rearrange("(kc p) c -> p kc c", p=128)

        slab_list = []
        slab_loaded = {}

        def get_slab(c0, ncols):
            k = slab_list.index((c0, ncols))
            if k not in slab_loaded:
                slab_loaded[k] = load_slab(*slab_list[k])
            if k + 1 < len(slab_list) and (k + 1) not in slab_loaded:
                slab_loaded[k + 1] = load_slab(*slab_list[k + 1])
            return slab_loaded[k]

        def load_slab(c0, ncols):
            i = slab_ctr[0] % 2
            slab_ctr[0] += 1
            ws_, wb_ = wst[0], wbf[i]
            dma(ws_[:, :, 0:ncols], wv[:, :, c0:c0 + ncols], writes=[ws_], slot=ws_)
            for kc in range(8):
                eng = "gpsimd" if kc % 2 == 0 else "vector"
                op(eng, lambda e, kc=kc, ws_=ws_, wb_=wb_: e.tensor_scalar(out=wb_[:, kc, 0:ncols], in0=ws_[:, kc, 0:ncols], scalar1=gT[:, kc:kc + 1], scalar2=None, op0=ALU.mult), reads=[ws_, gT], writes=[wb_])
            return wb_

        psA = [P.ps("psA%d" % i, [128, 512]) for i in range(3)]
        psB = [P.ps("psB%d" % i, [128, 512]) for i in range(4)]
        pa_ctr = [0]

        def next_psA():
            pa_ctr[0] += 1
            return psA[pa_ctr[0] % 3]

        def fm_matmul(pp, wb_, cc, t):
            mm([lambda e, kc=kc: e.matmul(pp[:, 0:TT], wb_[:, kc, cc * 128:(cc + 1) * 128], xn[:, kc, t * TT:(t + 1) * TT], start=(kc == 0), stop=(kc == 7)) for kc in range(8)],
               reads=[wb_, xn_tiles[t]], writes=[pp])

        stg = [P.sb("stg%d" % i, [128, 512], BF16) for i in range(4)]
        stg_ctr = [0]

        def next_stg():
            stg_ctr[0] += 1
            return stg[stg_ctr[0] % 4]

        markA = phase_scope()
        wqk = P.sb("wqk", [128, 8, 1280], BF16)
        for (c0, ncols) in ((0, 512), (512, 512), (1024, 256)):
            wb_ = load_slab(c0, ncols)
            op("gpsimd", lambda e, wb_=wb_, c0=c0, ncols=ncols: e.tensor_copy(out=wqk[:, :, c0:c0 + ncols], in_=wb_[:, :, 0:ncols]), reads=[wb_], writes=[wqk])
        rq = [P.sb("rq%d" % i, [128, 2, TT], F32) for i in range(2)]
        rk = [P.sb("rk%d" % i, [128, 2, TT], F32) for i in range(2)]
        q0 = [P.sb("q0_%d" % i, [128, TT], F32) for i in range(3)]
        sq1 = [P.sb("sq1_%d" % i, [128, TT], BF16) for i in range(3)]
        rs1 = [P.sb("rs1_%d" % i, [128, TT], F32) for i in range(3)]
        qn = [P.sb("qn_%d" % i, [128, TT], BF16) for i in range(3)]
        t1b = [P.sb("t1b_%d" % i, [128, TT], F32) for i in range(2)]
        t2b = [P.sb("t2b_%d" % i, [128, TT], F32) for i in range(2)]
        itemsA = [(t, cc) for t in range(NT) for cc in range(10)]
        ppA = {}

        def a_rope_loads(t):
            dma(rq[t % 2][:], ropeq_in[:, :, t * TT:(t + 1) * TT].rearrange("a p t -> p a t"), writes=[rq[t % 2]], slot=rq[t % 2])
            dma(rk[t % 2][:], ropek_in[:, :, t * TT:(t + 1) * TT].rearrange("a p t -> p a t"), writes=[rk[t % 2]], slot=rk[t % 2])

        def a_stage1(i):
            t, cc = itemsA[i]
            if cc == 0:
                a_rope_loads(t)
            pp = next_psA()
            fm_matmul(pp, wqk, cc, t)
            a_q0, a_sq = q0[i % 3], sq1[i % 3]
            op("scalar", lambda e, a=a_q0, pp=pp, cc=cc: e.activation(out=a[:], in_=pp[:, 0:TT], func=AF.Identity, bias=biasT[:, cc:cc + 1], scale=1.0), reads=[pp, biasT], writes=[a_q0])
            op("scalar", lambda e, a=a_sq, pp=pp, cc=cc: e.activation(out=a[:], in_=pp[:, 0:TT], func=AF.Square, bias=biasT[:, cc:cc + 1], scale=1.0), reads=[pp, biasT], writes=[a_sq])

        def a_stage2(i):
            t, cc = itemsA[i]
            gcol = 0 if cc < 8 else 1
            p2 = psB[i % 2]
            a_q0, a_sq, a_rs, a_qn = q0[i % 3], sq1[i % 3], rs1[i % 3], qn[i % 3]
            mm([lambda e, a=a_sq, p2=p2: e.matmul(p2[:, 0:TT], cstbf[:, BD64, :], a[:], start=True, stop=True)], reads=[a_sq, cstbf], writes=[p2])
            op("scalar", lambda e, a=a_rs, p2=p2: e.activation(out=a[:], in_=p2[:, 0:TT], func=AF.Ln, bias=EPS, scale=1.0 / 64), reads=[p2], writes=[a_rs])
            op("scalar", lambda e, a=a_rs: e.activation(out=a[:], in_=a[:], func=AF.Exp, scale=-0.5), reads=[a_rs], writes=[a_rs])
            op("vector", lambda e, a=a_qn, b=a_q0, c=a_rs, gcol=gcol: e.scalar_tensor_tensor(out=a[:], in0=b[:], scalar=gqk[:, gcol:gcol + 1], in1=c[:], op0=ALU.mult, op1=ALU.mult), reads=[a_q0, a_rs, gqk], writes=[a_qn])

        def a_stage3(i):
            t, cc = itemsA[i]
            isq = cc < 8
            rt = rq[t % 2] if isq else rk[t % 2]
            p3 = psB[2 + i % 2]
            a_qn, a_t1, a_t2 = qn[i % 3], t1b[i % 2], t2b[i % 2]
            mm([lambda e, a=a_qn, p3=p3: e.matmul(p3[:, 0:TT], cstbf[:, RPERM, :], a[:], start=True, stop=True)], reads=[a_qn, cstbf], writes=[p3])
            op("gpsimd", lambda e, a=a_t1, b=a_qn, rt=rt: e.tensor_tensor(out=a[:], in0=b[:], in1=rt[:, 0, :], op=ALU.mult), reads=[a_qn, rt], writes=[a_t1])
            op("vector", lambda e, a=a_t2, p3=p3, rt=rt: e.tensor_tensor(out=a[:], in0=p3[:, 0:TT], in1=rt[:, 1, :], op=ALU.mult), reads=[p3, rt], writes=[a_t2])
            sg = next_stg()
            op("gpsimd", lambda e, sg=sg, a=a_t1, b=a_t2: e.tensor_tensor(out=sg[:, 0:TT], in0=a[:], in1=b[:], op=ALU.add), reads=[a_t1, a_t2], writes=[sg])
            if isq:
                dst = QT[cc * 128:(cc + 1) * 128, t * TT:(t + 1) * TT]
            else:
                dst = KT[(cc - 8) * 128:(cc - 7) * 128, t * TT:(t + 1) * TT]
            dma(dst, sg[:, 0:TT], reads=[sg], slot=sg)

        NA = len(itemsA)
        for i in range(NA + 2):
            if i < NA:
                a_stage1(i)
            if 0 <= i - 1 < NA:
                a_stage2(i - 1)
            if 0 <= i - 2 < NA:
                a_stage3(i - 2)

        sync_stores(stg)
        phase_end(markA)
        markB = phase_scope()
        sg32 = [P.sb("sg32_%d" % i, [128, 512], F32) for i in range(2)]
        fmB = [("az", O_AZ, 8, ZAT, 10), ("mg", O_MG, 16, GAT, 34)]
        for (_k, cbase, nch, _d, _b) in fmB:
            for s0 in range(0, nch, 4):
                slab_list.append((cbase + s0 * 128, min(4, nch - s0) * 128))
        for s0 in range(0, 16, 4):
            slab_list.append((O_MQ + s0 * 128, 512))
        for (_c0, _nc) in ((O_AV, 256), (O_MV, 512), (O_MV + 512, 512), (O_MO, 512), (O_MO + 512, 512), (O_MZ, 512), (O_MZ + 512, 512), (O_G4, 32)):
            slab_list.append((_c0, _nc))
        it = 0
        for (kind, cbase, nch, dstT, bidx) in fmB:
            for s0 in range(0, nch, 4):
                nchs = min(4, nch - s0)
                wb_ = get_slab(cbase + s0 * 128, nchs * 128)
                for cc in range(nchs):
                    gch = bidx + s0 + cc
                    for t in range(NT):
                        pp = next_psA()
                        fm_matmul(pp, wb_, cc, t)
                        sg = next_stg()
                        if kind == "az":
                            op("scalar", lambda e, sg=sg, pp=pp, gch=gch: e.activation(out=sg[:, 0:TT], in_=pp[:, 0:TT], func=AF.Silu, bias=biasT[:, gch:gch + 1], scale=1.0), reads=[pp, biasT], writes=[sg])
                        else:
                            tmp = sg32[it % 2]
                            it += 1
                            op("scalar", lambda e, tmp=tmp, pp=pp, gch=gch: e.activation(out=tmp[:, 0:TT], in_=pp[:, 0:TT], func=AF.Tanh, bias=hbias[:, gch:gch + 1], scale=0.5), reads=[pp, hbias], writes=[tmp])
                            op("vector", lambda e, sg=sg, tmp=tmp: e.tensor_scalar(out=sg[:, 0:TT], in0=tmp[:, 0:TT], scalar1=0.5, scalar2=0.5, op0=ALU.mult, op1=ALU.add), reads=[tmp], writes=[sg])
                        r0 = (s0 + cc) * 128
                        dma(dstT[r0:r0 + 128, t * TT:(t + 1) * TT], sg[:, 0:TT], reads=[sg], slot=sg)

        sync_stores(stg)
        phase_end(markB)
        markC = phase_scope()
        rawb = [[P.sb("raw%d_%d" % (r_, s), [128, SEGT + 2], BF16) for s in range(NSEG)] for r_ in range(2)]
        dg = P.sb("dg", [128, 16, 3, 128], BF16)
        cvu = [P.sb("cvu%d" % i, [128, TT], F32) for i in range(3)]
        cvo = [P.sb("cvo%d" % i, [128, TT], BF16) for i in range(4)]
        for ch in range(16):
            for j in range(3):
                eng = "vector" if (ch * 3 + j) % 2 == 0 else "gpsimd"
                op(eng, lambda e, ch=ch, j=j: e.tensor_scalar(out=dg[:, ch, j, :], in0=cst32[:, IDENT, :], scalar1=cw[:, ch, j:j + 1], scalar2=None, op0=ALU.mult), reads=[cst32, cw], writes=[dg])
        for r_ in range(2):
            for s in range(NSEG):
                op("gpsimd", lambda e, s=s, r_=r_: e.memset(rawb[r_][s][:, 0:1], 0.0), writes=[rawb[r_][s]])
                op("gpsimd", lambda e, s=s, r_=r_: e.memset(rawb[r_][s][:, SEGT + 1:SEGT + 2], 0.0), writes=[rawb[r_][s]])
        slabC = {}

        def c_main(ch):
            s0, cc = (ch // 4) * 4, ch % 4
            if cc == 0:
                slabC[s0] = get_slab(O_MQ + s0 * 128, 512)
            wb_ = slabC[s0]
            raw = rawb[ch % 2]
            gch = 18 + ch
            for t in range(NT):
                s = t // cfg.TPS
                tl = t % cfg.TPS
                pp = next_psA()
                fm_matmul(pp, wb_, cc, t)
                op("scalar", lambda e, s=s, tl=tl, pp=pp, gch=gch, raw=raw: e.activation(out=raw[s][:, 1 + tl * TT:1 + (tl + 1) * TT], in_=pp[:, 0:TT], func=AF.Identity, bias=biasT[:, gch:gch + 1], scale=1.0), reads=[pp, biasT], writes=[raw[s]])
            for b in range(NSEG - 1):
                op("vector", lambda e, b=b, raw=raw: e.tensor_scalar(out=raw[b][:, SEGT + 1:SEGT + 2], in0=raw[b + 1][:, 1:2], scalar1=flg[:, b:b + 1], scalar2=None, op0=ALU.mult), reads=[raw[b + 1], flg], writes=[raw[b]])
                op("vector", lambda e, b=b, raw=raw: e.tensor_scalar(out=raw[b + 1][:, 0:1], in0=raw[b][:, SEGT:SEGT + 1], scalar1=flg[:, b:b + 1], scalar2=None, op0=ALU.mult), reads=[raw[b], flg], writes=[raw[b + 1]])

        cctr = [0]

        def c_conv(ch):
            raw = rawb[ch % 2]
            dstT = MQT if ch < 8 else MKT
            r0 = (ch % 8) * 128
            inv_s = 1.0 if ch < 8 else 1.0 / KS
            for t in range(NT):
                s = t // cfg.TPS
                tl = t % cfg.TPS
                cctr[0] += 1
                pc = psB[cctr[0] % 2]
                u = cvu[cctr[0] % 3]
                o = cvo[cctr[0] % 4]
                rw = raw[s]
                mm([lambda e, pc=pc, ch=ch, j=j, rw=rw, tl=tl: e.matmul(pc[:, 0:TT], dg[:, ch, j, :], rw[:, tl * TT + j:tl * TT + j + TT], start=(j == 0), stop=(j == 2)) for j in range(3)],
                   reads=[rw, dg], writes=[pc])
                op("scalar", lambda e, u=u, pc=pc, inv_s=inv_s: e.activation(out=u[:], in_=pc[:, 0:TT], func=AF.Tanh, scale=inv_s), reads=[pc], writes=[u])
                op("vector", lambda e, u=u, pc=pc, o=o: e.scalar_tensor_tensor(out=o[:], in0=u[:], scalar=1.0, in1=pc[:, 0:TT], op0=ALU.add, op1=ALU.mult), reads=[u, pc], writes=[o])
                dma(dstT[r0:r0 + 128, t * TT:(t + 1) * TT], o[:], reads=[o], slot=o)

        c_main(0)
        for ch in range(16):
            if ch + 1 < 16:
                c_main(ch + 1)
            c_conv(ch)
        sync_stores(cvo)
        phase_end(markC)
        bbc = [P.sb("bbc%d" % i, [128, 512], F32) for i in range(2)]
        tm32 = [P.sb("tm32_%d" % i, [128, 512], F32) for i in range(2)]
        tmo = [P.sb("tmo_%d" % i, [128, 512], BF16) for i in range(3)]
        g32 = [P.sb("g32_%d" % i, [128, 32], F32) for i in range(2)]
        tmD = [("v", O_AV, 256, VA, 0), ("v", O_MV, 512, MV, 0), ("v", O_MV + 512, 512, MV, 512),
               ("o", O_MO, 512, MO, 0), ("o", O_MO + 512, 512, MO, 512),
               ("z", O_MZ, 512, MZ, 0), ("z", O_MZ + 512, 512, MZ, 512), ("g", O_G4, 32, GTS, 0)]
        it = 0
        for si, (kind, c0, ncols, dstD, dc0) in enumerate(tmD):
            wb_ = get_slab(c0, ncols)
            bb = bbc[si % 2]
            dma(bb[:, 0:ncols], bass.AP(b_in_d.tensor, l * IN_DIM + c0, [[0, 128], [1, ncols]]), writes=[bb], slot=bb)
            for blk in range(NBLK):
                t = blk // TB
                pp = next_psA()
                mm([lambda e, kc=kc, pp=pp, blk=blk, ncols=ncols, wb_=wb_: e.matmul(pp[:, 0:ncols], xn[:, kc, blk * 128:(blk + 1) * 128], wb_[:, kc, 0:ncols], start=(kc == 0), stop=(kc == 7)) for kc in range(8)],
                   reads=[wb_, xn_tiles[t]], writes=[pp])
                r0 = blk * 128
                if kind == "v":
                    o_ = tmo[it % 3]
                    it += 1
                    op("vector", lambda e, o_=o_, pp=pp, bb=bb, ncols=ncols: e.tensor_tensor(out=o_[:, 0:ncols], in0=pp[:, 0:ncols], in1=bb[:, 0:ncols], op=ALU.add), reads=[pp, bb], writes=[o_])
                    dma(dstD[r0:r0 + 128, dc0:dc0 + ncols], o_[:, 0:ncols], reads=[o_], slot=o_)
                elif kind == "o":
                    o_, tmp = tmo[it % 3], tm32[it % 2]
                    it += 1
                    op("vector", lambda e, tmp=tmp, pp=pp, bb=bb, ncols=ncols: e.tensor_tensor(out=tmp[:, 0:ncols], in0=pp[:, 0:ncols], in1=bb[:, 0:ncols], op=ALU.add), reads=[pp, bb], writes=[tmp])
                    op("scalar", lambda e, tmp=tmp, ncols=ncols: e.activation(out=tmp[:, 0:ncols], in_=tmp[:, 0:ncols], func=AF.Tanh, scale=0.5), reads=[tmp], writes=[tmp])
                    op("gpsimd", lambda e, tmp=tmp, o_=o_, ncols=ncols: e.tensor_scalar(out=o_[:, 0:ncols], in0=tmp[:, 0:ncols], scalar1=0.5, scalar2=0.5, op0=ALU.mult, op1=ALU.add), reads=[tmp], writes=[o_])
                    dma(dstD[r0:r0 + 128, dc0:dc0 + ncols], o_[:, 0:ncols], reads=[o_], slot=o_)
                elif kind == "z":
                    o_, tmp = tmo[it % 3], tm32[it % 2]
                    it += 1
                    op("vector", lambda e, tmp=tmp, pp=pp, bb=bb, ncols=ncols: e.tensor_tensor(out=tmp[:, 0:ncols], in0=pp[:, 0:ncols], in1=bb[:, 0:ncols], op=ALU.add), reads=[pp, bb], writes=[tmp])
                    op("scalar", lambda e, tmp=tmp, o_=o_, ncols=ncols: e.activation(out=o_[:, 0:ncols], in_=tmp[:, 0:ncols], func=AF.Silu), reads=[tmp], writes=[o_])
                    dma(dstD[r0:r0 + 128, dc0:dc0 + ncols], o_[:, 0:ncols], reads=[o_], slot=o_)
                else:
                    o_ = g32[it % 2]
                    it += 1
                    op("vector", lambda e, o_=o_, pp=pp, bb=bb: e.tensor_tensor(out=o_[:], in0=pp[:, 0:32], in1=bb[:, 0:32], op=ALU.add), reads=[pp, bb], writes=[o_])
                    dma(dstD[r0:r0 + 128, 0:32], o_[:], reads=[o_], slot=o_)
        sync_stores(tmo + g32)
        phase_end(mark)

        mark = phase_scope()
        HW_ = TT + 256
        Qs = [P.sb("Qs%d" % i, [64, 16, TT], BF16) for i in range(2)]
        Ks = [P.sb("Ks%d" % i, [64, 4, HW_], BF16) for i in range(2)]
        Vs = [P.sb("Vs%d" % i, [128, TB + 2, 256], BF16) for i in range(2)]
        Zs = [P.sb("Zs%d" % i, [64, 16, TT], BF16) for i in range(2)]
        AGs = [P.sb("AGs%d" % i, [64, 16, TT], BF16) for i in range(2)]
        pT = [P.sb("pT%d" % i, [128, 512], BF16) for i in range(9)]
        lnd = [P.sb("lnd%d" % i, [64, 512], F32) for i in range(2)]
        zr = [P.sb("zr%d" % i, [64, 512], F32) for i in range(2)]
        skr = P.sb("skr", [2, 16], F32)
        ske = P.sb("ske", [2, 16], F32)
        skhl = P.sb("skhl", [2, 16], BF16)
        sktmp = P.sb("sktmp", [2, 16], F32)
        skrow = P.sb("skrow", [2, 16, 128], BF16)
        ones2 = P.sb("ones2", [2, 64], BF16)
        psS = [P.ps("psS%d" % i, [128, 512]) for i in range(4)]
        psO = [P.ps("psO%d" % i, [64, 512]) for i in range(2)]
        psD = [P.ps("psD%d" % i, [64, 512]) for i in range(2)]
        dma(skr[:], bass.AP(sink_d.tensor, l * 16, [[0, 2], [1, 16]]), writes=[skr], slot=skr)
        op("scalar", lambda e: e.activation(out=ske[:], in_=skr[:], func=AF.Exp), reads=[skr], writes=[ske])
        op("vector", lambda e: e.tensor_copy(out=skhl[:], in_=ske[:]), reads=[ske], writes=[skhl])
        op("vector", lambda e: e.tensor_tensor(out=sktmp[:], in0=ske[:], in1=skhl[:], op=ALU.subtract), reads=[ske, skhl], writes=[sktmp])
        op("vector", lambda e: e.tensor_copy(out=skhl[:], in_=sktmp[:]), reads=[sktmp], writes=[skhl])
        op("vector", lambda e: e.tensor_copy(out=skhl[0:1, :], in_=ske[0:1, :]), reads=[ske], writes=[skhl])
        op("vector", lambda e: e.tensor_copy(out=skrow[:], in_=bass.AP(skhl.t, 0, [[16, 2], [1, 16], [0, 128]])), reads=[skhl], writes=[skrow])
        op("vector", lambda e: e.memset(ones2[:], 1.0), writes=[ones2])
        sctr = [0]
        pctr = [0]
        for t in range(NT):
            i2 = t % 2
            Qb, Kb, Vb, Zb, Ab = Qs[i2], Ks[i2], Vs[i2], Zs[i2], AGs[i2]
            b0 = t * TB
            lo_blk = max(b0 - 1, 0)
            hi_blk = min(b0 + TB + 1, NBLK)
            dma(Qb[:], QT[:, t * TT:(t + 1) * TT].rearrange("(h d) t -> d h t", d=64), writes=[Qb], slot=Qb)
            dma(Zb[:], ZAT[:, t * TT:(t + 1) * TT].rearrange("(h d) t -> d h t", d=64), writes=[Zb], slot=Zb)
            ko = (lo_blk - (b0 - 1)) * 128
            dma(Kb[:, :, ko:ko + (hi_blk - lo_blk) * 128], KT[:, lo_blk * 128:hi_blk * 128].rearrange("(j d) t -> d j t", d=64), writes=[Kb], slot=Kb)
            vo = lo_blk - (b0 - 1)
            dma(Vb[:, vo:vo + (hi_blk - lo_blk), :], VA[lo_blk * 128:hi_blk * 128, :].rearrange("(n p) c -> p n c", p=128), writes=[Vb], slot=Vb)
            items2 = []
            for nb in range(TB):
                n = b0 + nb
                seg, nis = n // SEGB, n % SEGB
                kbs = []
                if n > 0:
                    if nis > 0:
                        kbs.append((nb, 0))
                    else:
                        kbs.append((nb, 2 + 2 * (seg - 1)))
                kbs.append((nb + 1, None))
                if n < NBLK - 1:
                    if nis < SEGB - 1:
                        kbs.append((nb + 2, 1))
                    else:
                        kbs.append((nb + 2, 3 + 2 * seg))
                for j in range(4):
                    items2.append((nb, j, kbs))
            ptsd = {}

            def s1(k, Kb=Kb, Qb=Qb):
                nb, j, kbs = items2[k]
                pts = []
                for (ks, mi) in kbs:
                    sctr[0] += 1
                    pS = psS[sctr[0] % 4]
                    pctr[0] += 1
                    pt = pT[pctr[0] % 9]
                    mm([lambda e, pS=pS, ks=ks, j=j, nb=nb, Kb=Kb, Qb=Qb: e.matmul(pS[:], Kb[:, j, ks * 128:(ks + 1) * 128], Qb[:, 4 * j:4 * j + 4, nb * 128:(nb + 1) * 128], start=True, stop=True)],
                       reads=[Kb, Qb], writes=[pS])
                    op("scalar", lambda e, pS=pS, pt=pt: e.activation(out=pt[:], in_=pS[:], func=AF.Exp), reads=[pS], writes=[pt])
                    if mi is not None:
                        eng = "gpsimd" if mi % 2 == 0 else "vector"
                        op(eng, lambda e, pt=pt, mi=mi: e.tensor_tensor(out=pt[:], in0=pt[:], in1=amask[:, mi, :], op=ALU.mult), reads=[pt, amask], writes=[pt])
                    pts.append((pt, ks))
                ptsd[k] = pts

            def s2(k, Vb=Vb, Zb=Zb, Ab=Ab):
                nb, j, kbs = items2[k]
                pts = ptsd.pop(k)
                pO, pD = psO[k % 2], psD[k % 2]
                nk = len(pts)
                mm([lambda e, pO=pO, pt=pt, ks=ks, j=j, i=i, nk=nk, Vb=Vb: e.matmul(pO[:], Vb[:, ks, j * 64:(j + 1) * 64], pt[:], start=(i == 0), stop=(i == nk - 1)) for i, (pt, ks) in enumerate(pts)],
                   reads=[Vb] + [p[0] for p in pts], writes=[pO])
                mm([lambda e, pD=pD, pt=pt, i=i: e.matmul(pD[:], cstbf[:, ONES, 0:64], pt[:], start=(i == 0), stop=False) for i, (pt, ks) in enumerate(pts)]
                   + [lambda e, pD=pD, j=j: e.matmul(pD[:], ones2[:], skrow[:, 4 * j:4 * j + 4, :], start=False, stop=True)],
                   reads=[cstbf, ones2, skrow] + [p[0] for p in pts], writes=[pD])
                ld_, zr_ = lnd[k % 2], zr[k % 2]
                op("scalar", lambda e, ld_=ld_, pD=pD: e.activation(out=ld_[:], in_=pD[:], func=AF.Ln), reads=[pD], writes=[ld_])
                op("scalar", lambda e, ld_=ld_: e.activation(out=ld_[:], in_=ld_[:], func=AF.Exp, scale=-1.0), reads=[ld_], writes=[ld_])
                op("vector", lambda e, zr_=zr_, ld_=ld_, j=j, nb=nb, Zb=Zb: e.tensor_tensor(out=zr_[:].rearrange("p (g q) -> p g q", g=4), in0=ld_[:].rearrange("p (g q) -> p g q", g=4), in1=Zb[:, 4 * j:4 * j + 4, nb * 128:(nb + 1) * 128], op=ALU.mult), reads=[ld_, Zb], writes=[zr_])
                op("vector", lambda e, zr_=zr_, pO=pO, j=j, nb=nb, Ab=Ab: e.tensor_tensor(out=Ab[:, 4 * j:4 * j + 4, nb * 128:(nb + 1) * 128], in0=pO[:].rearrange("p (g q) -> p g q", g=4), in1=zr_[:].rearrange("p (g q) -> p g q", g=4), op=ALU.mult), reads=[pO, zr_], writes=[Ab])

            NI = len(items2)
            s1(0)
            for k in range(NI):
                if k + 1 < NI:
                    s1(k + 1)
                s2(k)
            dma(AGT[:, t * TT:(t + 1) * TT].rearrange("(h d) t -> d h t", d=64), Ab[:], reads=[Ab], slot=Ab)
        sync_stores(AGs)
        phase_end(mark)

        mark = phase_scope()
        G = P.sb("G", [128, NBLK, 32], F32)
        SP_ = P.sb("SP", [128, 2, NBLK, 8], F32)
        EA = P.sb("EA", [128, 2, NBLK, 8], F32)
        EB = P.sb("EB", [128, 2, NBLK, 8], F32)
        EBT = P.sb("EBT", [128, 2, NBLK, 8], F32)
        WK = P.sb("WK", [128, 2, NBLK, 8], F32)
        dma(G[:], GTS.rearrange("(n p) c -> p n c", p=128), writes=[G], slot=G)
        NG = NBLK * 8
        GCH = 384 // 8
        psT = [P.ps("psT%d" % i, [128, 512]) for i in range(2)]
        psG = psT
        for d_ in range(2):
            fcol = 8 + 16 * d_
            icol = 16 * d_
            op("scalar", lambda e, d_=d_, fcol=fcol: e.activation(out=SP_[:, d_, :, :], in_=G[:, :, fcol:fcol + 8], func=AF.Exp, scale=-1.0), reads=[G], writes=[SP_])
            op("scalar", lambda e, d_=d_: e.activation(out=SP_[:, d_, :, :], in_=SP_[:, d_, :, :], func=AF.Ln, bias=1.0, scale=1.0), reads=[SP_], writes=[SP_])
            tri = TRIU if d_ == 0 else TRIL
            for c0 in range(0, NBLK, GCH):
                nbk = min(GCH, NBLK - c0)
                pg, pt_ = psG[0], psG[1]
                mm([lambda e, pg=pg, c0=c0, nbk=nbk, d_=d_, tri=tri: e.matmul(pg[:, 0:nbk * 8], cst32[:, tri, :], SP_[:, d_, c0:c0 + nbk, :], start=True, stop=True)], reads=[cst32, SP_], writes=[pg])
                mm([lambda e, pt_=pt_, c0=c0, nbk=nbk, d_=d_: e.matmul(pt_[:, 0:nbk * 8], cst32[:, ONES, :], SP_[:, d_, c0:c0 + nbk, :], start=True, stop=True)], reads=[cst32, SP_], writes=[pt_])
                op("vector", lambda e, pg=pg, c0=c0, nbk=nbk, d_=d_, icol=icol: e.tensor_tensor(out=EA[:, d_, c0:c0 + nbk, :], in0=G[:, c0:c0 + nbk, icol:icol + 8], in1=pg[:, 0:nbk * 8].rearrange("p (n h) -> p n h", h=8), op=ALU.add), reads=[G, pg], writes=[EA])
                op("scalar", lambda e, c0=c0, nbk=nbk, d_=d_: e.activation(out=EA[:, d_, c0:c0 + nbk, :], in_=EA[:, d_, c0:c0 + nbk, :], func=AF.Exp), reads=[EA], writes=[EA])
                op("scalar", lambda e, pg=pg, c0=c0, nbk=nbk, d_=d_: e.activation(out=EB[:, d_, c0:c0 + nbk, :], in_=pg[:, 0:nbk * 8].rearrange("p (n h) -> p n h", h=8), func=AF.Exp, scale=-1.0), reads=[pg], writes=[EB])
                op("scalar", lambda e, pt_=pt_, c0=c0, nbk=nbk, d_=d_: e.activation(out=EBT[:, d_, c0:c0 + nbk, :], in_=pt_[:, 0:nbk * 8].rearrange("p (n h) -> p n h", h=8), func=AF.Exp, scale=-1.0), reads=[pt_], writes=[EBT])
                op("vector", lambda e, c0=c0, nbk=nbk, d_=d_: e.tensor_tensor(out=WK[:, d_, c0:c0 + nbk, :], in0=EA[:, d_, c0:c0 + nbk, :], in1=EBT[:, d_, c0:c0 + nbk, :], op=ALU.mult), reads=[EA, EBT], writes=[WK])

        Cst = P.sb("Cst", [128, 8, 128], F32)
        Cbf2 = [P.sb("Cbf%d" % i, [128, 8, 128], BF16) for i in range(2)]
        n8 = P.sb("n8", [128, 8], F32)
        nbf2 = [P.sb("nbf%d" % i, [128, 8], BF16) for i in range(2)]
        Chalf = [Buf(Cst.t), Buf(Cst.t)]
        Cbfh = [[Buf(Cbf2[i].t), Buf(Cbf2[i].t)] for i in range(2)]
        TQ = [P.sb("TQ%d" % i, [128, 8, TT], BF16) for i in range(2)]
        TK = [P.sb("TK%d" % i, [128, 8, TT], BF16) for i in range(2)]
        TV = [P.sb("TV%d" % i, [128, TB, 8, 128], BF16) for i in range(2)]
        TO = [P.sb("TO%d" % i, [128, TB, 1024], BF16) for i in range(2)]
        TZ = [P.sb("TZ%d" % i, [128, TB, 1024], BF16) for i in range(2)]
        THB = [P.sb("THB%d" % i, [128, TB, 8, 128], F32) for i in range(2)]
        hs_ = [P.sb("hs%d" % i, [128, 8, 128], F32) for i in range(2)]
        hq_ = [P.sb("hq%d" % i, [128, 8, 128], F32) for i in range(2)]
        hn_ = [P.sb("hn%d" % i, [128, 1024], BF16) for i in range(2)]
        p8 = [P.sb("p8_%d" % i, [128, 8, 128], BF16) for i in range(3)]
        k8 = [P.sb("k8_%d" % i, [128, 8, 128], BF16) for i in range(3)]
        p8h = [[Buf(p8[i].t), Buf(p8[i].t)] for i in range(3)]
        k8h = [[Buf(k8[i].t), Buf(k8[i].t)] for i in range(3)]
        rr = [P.sb("rr%d" % i, [128, 8], F32) for i in range(2)]
        rt_ = [P.sb("rt%d" % i, [128, 8], F32) for i in range(2)]
        ss8 = [P.sb("ss8_%d" % i, [128, 8], F32) for i in range(2)]
        psK = [P.ps("psK%d" % i, [128, 512]) for i in range(1)]
        psX = [P.ps("psX%d" % i, [128, 512]) for i in range(2)]
        psC = [P.ps("psC%d" % i, [128, 512]) for i in range(2)]
        psY = P.ps("psYN", [128, 16])
        psYb, psNb = Buf(psY.t), Buf(psY.t)
        mqv = MQT.rearrange("(h d) t -> d h t", d=128)
        mkv = MKT.rearrange("(h d) t -> d h t", d=128)
        cur = [0]

        def reset_state():
            c = cur[0]
            for hf in range(2):
                op("gpsimd", lambda e, hf=hf: e.memset(Cst[:, 4 * hf:4 * hf + 4, :], 0.0), writes=[Chalf[hf]])
                op("gpsimd", lambda e, hf=hf, c=c: e.memset(Cbf2[c][:, 4 * hf:4 * hf + 4, :], 0.0), writes=[Cbfh[c][hf]])
            op("vector", lambda e: e.memset(n8[:], 0.0), writes=[n8])
            op("vector", lambda e, c=c: e.memset(nbf2[c][:], 0.0), writes=[nbf2[c]])

        def link_state(b):
            c = cur[0]
            fl = flg[:, b:b + 1]
            for hf in range(2):
                op("gpsimd", lambda e, hf=hf: e.tensor_scalar(out=Cst[:, 4 * hf:4 * hf + 4, :], in0=Cst[:, 4 * hf:4 * hf + 4, :], scalar1=fl, scalar2=None, op0=ALU.mult), reads=[flg, Chalf[hf]], writes=[Chalf[hf]])
                op("scalar", lambda e, hf=hf, c=c: e.activation(out=Cbf2[c][:, 4 * hf:4 * hf + 4, :], in_=Cst[:, 4 * hf:4 * hf + 4, :], func=AF.Copy), reads=[Chalf[hf]], writes=[Cbfh[c][hf]])
            op("vector", lambda e: e.tensor_scalar(out=n8[:], in0=n8[:], scalar1=fl, scalar2=None, op0=ALU.mult), reads=[flg, n8], writes=[n8])
            op("vector", lambda e, c=c: e.tensor_copy(out=nbf2[c][:], in_=n8[:]), reads=[n8], writes=[nbf2[c]])

        seq = [(1, n) for n in range(NBLK - 1, -1, -1)] + [(0, n) for n in range(NBLK)]

        def stageA1(i):
            d_, n = seq[i]
            i3 = i % 3
            tri = TRIU if d_ == 0 else TRIL
            pt2 = ptile(i) % 2
            nbk = n % TB
            bs = slice(nbk * 128, (nbk + 1) * 128)
            q_, k_, v_ = TQ[pt2], TK[pt2], TV[pt2]
            for hf in range(2):
                pst, psk = psT[hf], psK[0]
                p4, k4 = p8h[i3][hf], k8h[i3][hf]
                pt_, kt_ = p8[i3], k8[i3]
                for hh in range(4):
                    h = 4 * hf + hh
                    mm([lambda e, pst=pst, hh=hh, h=h, k_=k_, q_=q_, bs=bs: e.matmul(pst[:, hh * 128:(hh + 1) * 128], k_[:, h, bs], q_[:, h, bs], start=True, stop=True)], reads=[k_, q_], writes=[pst])
                for hh in range(4):
                    h = 4 * hf + hh
                    mm([lambda e, psk=psk, hh=hh, h=h, k_=k_, bs=bs: e.matmul(psk[:, hh * 128:(hh + 1) * 128], k_[:, h, bs], cstbf[:, IDENT, :], start=True, stop=True)], reads=[k_, cstbf], writes=[psk])
                for hh in range(4):
                    h = 4 * hf + hh
                    op("vector", lambda e, pt_=pt_, pst=pst, hh=hh, h=h, n=n, d_=d_, tri=tri: e.scalar_tensor_tensor(out=pt_[:, h, :], in0=pst[:, hh * 128:(hh + 1) * 128], scalar=EA[:, d_, n, h:h + 1], in1=cst32[:, tri, :], op0=ALU.mult, op1=ALU.mult), reads=[pst, EA, cst32], writes=[p4])
                    op("scalar", lambda e, kt_=kt_, psk=psk, hh=hh, h=h, n=n, d_=d_: e.activation(out=kt_[:, h, :], in_=psk[:, hh * 128:(hh + 1) * 128], func=AF.Copy, scale=WK[:, d_, n, h:h + 1]), reads=[psk, WK], writes=[k4])

        def stageA2(i):
            d_, n = seq[i]
            i3 = i % 3
            v_ = TV[ptile(i) % 2]
            nbk = n % TB
            kt_ = k8[i3]
            for hf in range(2):
                pc = psC[hf]
                k4 = k8h[i3][hf]
                for hh in range(4):
                    h = 4 * hf + hh
                    mm([lambda e, pc=pc, hh=hh, h=h, kt_=kt_, v_=v_, nbk=nbk: e.matmul(pc[:, hh * 128:(hh + 1) * 128], kt_[:, h, :], v_[:, nbk, h, :], start=True, stop=True)], reads=[k4, v_], writes=[pc])
            for h in range(8):
                hf = h // 4
                mm([lambda e, h=h, kt_=kt_: e.matmul(psY[:, 8 + h:9 + h], kt_[:, h, :], cstbf[:, ONES, 0:1], start=True, stop=True)], reads=[k8h[i3][hf], cstbf], writes=[psNb])

        def ptile(i):
            return i // TB

        def tile_blocks(pt):
            d_ = 1 if pt < NT else 0
            tt = (NT - 1 - pt) if d_ == 1 else (pt - NT)
            return d_, tt

        def loads_tile(pt):
            d_, tt = tile_blocks(pt)
            q_, k_, v_ = TQ[pt % 2], TK[pt % 2], TV[pt % 2]
            dma(q_[:], mqv[:, :, tt * TT:(tt + 1) * TT], writes=[q_], slot=q_)
            dma(k_[:], mkv[:, :, tt * TT:(tt + 1) * TT], writes=[k_], slot=k_)
            dma(v_[:], MV[tt * TT:(tt + 1) * TT, :].rearrange("(b p) (h e) -> p b h e", p=128, h=8), writes=[v_], slot=v_)
            if d_ == 0 and pt > NT:
                loadsB_tile(pt)

        def loadsB_tile(pt):
            d_, tt = tile_blocks(pt)
            o_b, z_b, h_b = TO[pt % 2], TZ[pt % 2], THB[pt % 2]
            dma(o_b[:], MO[tt * TT:(tt + 1) * TT, :].rearrange("(b p) c -> p b c", p=128), writes=[o_b], slot=o_b)
            dma(z_b[:], MZ[tt * TT:(tt + 1) * TT, :].rearrange("(b p) c -> p b c", p=128), writes=[z_b], slot=z_b)
            dma(h_b[:], HB[tt * TT:(tt + 1) * TT, :].rearrange("(b p) (h e) -> p b h e", p=128, h=8), writes=[h_b], slot=h_b)

        def stageB(i):
            d_, n = seq[i]
            i2, i3 = i % 2, i % 3
            seg, nis = n // SEGB, n % SEGB
            pt2 = ptile(i) % 2
            nbk = n % TB
            bs = slice(nbk * 128, (nbk + 1) * 128)
            q_, v_ = TQ[pt2], TV[pt2]
            pt_ = p8[i3]
            first_in_pass = (n == NBLK - 1) if d_ == 1 else (n == 0)
            first_in_seg = (nis == SEGB - 1) if d_ == 1 else (nis == 0)
            if first_in_pass:
                reset_state()
            elif first_in_seg:
                link_state(seg if d_ == 1 else seg - 1)
            c = cur[0]
            nx = 1 - c
            Cb, nb_ = Cbf2[c], nbf2[c]
            for hf in range(2):
                psx = psX[hf]
                for hh in range(4):
                    h = 4 * hf + hh
                    mm([lambda e, psx=psx, hh=hh, h=h, q_=q_, Cb=Cb, bs=bs: e.matmul(psx[:, hh * 128:(hh + 1) * 128], q_[:, h, bs], Cb[:, h, :], start=True, stop=False),
                        lambda e, psx=psx, hh=hh, h=h, pt_=pt_, v_=v_, nbk=nbk: e.matmul(psx[:, hh * 128:(hh + 1) * 128], pt_[:, h, :], v_[:, nbk, h, :], start=False, stop=True)],
                       reads=[q_, Cbfh[c][hf], p8h[i3][hf], v_], writes=[psx])
            for h in range(8):
                hf = h // 4
                mm([lambda e, h=h, q_=q_, nb_=nb_, bs=bs: e.matmul(psY[:, h:h + 1], q_[:, h, bs], nb_[:, h:h + 1], start=True, stop=False),
                    lambda e, h=h, pt_=pt_: e.matmul(psY[:, h:h + 1], pt_[:, h, :], cstbf[:, ONES, 0:1], start=False, stop=True)],
                   reads=[q_, nb_, p8h[i3][hf], cstbf], writes=[psYb])
            for hf in range(2):
                ebt_bc = bass.AP(EBT.t, (d_ * NBLK + n) * 8 + 4 * hf, [[2 * NBLK * 8, 128], [1, 4], [0, 128]])
                pc = psC[hf]
                for hh in range(4):
                    h = 4 * hf + hh
                    op("vector", lambda e, h=h, hh=hh, pc=pc, n=n, d_=d_: e.scalar_tensor_tensor(out=Cst[:, h, :], in0=Cst[:, h, :], scalar=EBT[:, d_, n, h:h + 1], in1=pc[:, hh * 128:(hh + 1) * 128], op0=ALU.mult, op1=ALU.add), reads=[pc, Chalf[hf], EBT], writes=[Chalf[hf]])
                op("scalar", lambda e, hf=hf, nx=nx: e.activation(out=Cbf2[nx][:, 4 * hf:4 * hf + 4, :], in_=Cst[:, 4 * hf:4 * hf + 4, :], func=AF.Copy), reads=[Chalf[hf]], writes=[Cbfh[nx][hf]])
            op("vector", lambda e, n=n, d_=d_: e.tensor_tensor(out=n8[:], in0=n8[:], in1=EBT[:, d_, n, :], op=ALU.mult), reads=[n8, EBT], writes=[n8])
            op("vector", lambda e: e.tensor_tensor(out=n8[:], in0=n8[:], in1=psY[:, 8:16], op=ALU.add), reads=[n8, psNb], writes=[n8])
            op("vector", lambda e, nx=nx: e.tensor_copy(out=nbf2[nx][:], in_=n8[:]), reads=[n8], writes=[nbf2[nx]])
            cur[0] = nx
            hq = hq_[i2]
            hq_half = [Buf(hq.t), Buf(hq.t)]
            for hf in range(2):
                psx = psX[hf]
                op("scalar", lambda e, hf=hf, psx=psx, hq=hq: e.activation(out=hq[:, 4 * hf:4 * hf + 4, :], in_=psx[:].rearrange("p (a b) -> p a b", a=4), func=AF.Copy), reads=[psx], writes=[hq_half[hf], hq])
            r_, t_ = rr[i2], rt_[i2]
            op("scalar", lambda e, t_=t_: e.activation(out=t_[:], in_=psY[:, 0:8], func=AF.Abs), reads=[psYb], writes=[t_])
            op("vector", lambda e, t_=t_, n=n, d_=d_: e.tensor_tensor(out=t_[:], in0=t_[:], in1=EB[:, d_, n, :], op=ALU.mult), reads=[t_, EB], writes=[t_])
            op("vector", lambda e, t_=t_: e.tensor_scalar_max(out=t_[:], in0=t_[:], scalar1=1.0), reads=[t_], writes=[t_])
            op("vector", lambda e, t_=t_, r_=r_: e.reciprocal(out=r_[:], in_=t_[:]), reads=[t_], writes=[r_])
            op("vector", lambda e, r_=r_, n=n, d_=d_: e.tensor_tensor(out=r_[:], in0=r_[:], in1=EB[:, d_, n, :], op=ALU.mult), reads=[r_, EB], writes=[r_])
            r_bc = bass.AP(r_.t, 0, [[8, 128], [1, 8], [0, 128]])
            hs = hs_[i2]
            if d_ == 1:
                op("gpsimd", lambda e, hs=hs, hq=hq, r_bc=r_bc: e.tensor_tensor(out=hs[:], in0=hq[:], in1=r_bc, op=ALU.mult), reads=[hq, r_], writes=[hs])
                dma(HB[n * 128:(n + 1) * 128, :].rearrange("p (h e) -> p h e", h=8), hs[:], reads=[hs], slot=hs)
            else:
                o_b, z_b, h_b = TO[pt2], TZ[pt2], THB[pt2]
                hn = hn_[i2]
                s8 = ss8[i2]
                for h in range(8):
                    op("vector", lambda e, hs=hs, hq=hq, h_b=h_b, nbk=nbk, h=h, r_=r_: e.scalar_tensor_tensor(out=hs[:, h, :], in0=hq[:, h, :], scalar=r_[:, h:h + 1], in1=h_b[:, nbk, h, :], op0=ALU.mult, op1=ALU.add), reads=[hq, r_, h_b], writes=[hs])
                op("gpsimd", lambda e, hs=hs, o_b=o_b, nbk=nbk: e.tensor_tensor(out=hs[:], in0=hs[:], in1=o_b[:, nbk, :].rearrange("p (h e) -> p h e", h=8), op=ALU.mult), reads=[hs, o_b], writes=[hs])
                op("vector", lambda e, s8=s8: e.memset(s8[:], 0.0), writes=[s8])
                for h in range(8):
                    op("scalar", lambda e, hs=hs, hq=hq, h=h, s8=s8: e.activation(out=hq[:, h, :], in_=hs[:, h, :], func=AF.Square, accum_out=s8[:, h:h + 1]), reads=[hs, s8], writes=[hq, s8])
                op("scalar", lambda e, s8=s8: e.activation(out=s8[:], in_=s8[:], func=AF.Ln, bias=EPS, scale=1.0 / 128), reads=[s8], writes=[s8])
                op("scalar", lambda e, s8=s8: e.activation(out=s8[:], in_=s8[:], func=AF.Exp, scale=-0.5), reads=[s8], writes=[s8])
                s_bc = bass.AP(s8.t, 0, [[8, 128], [1, 8], [0, 128]])
                op("gpsimd", lambda e, hs=hs, s_bc=s_bc: e.tensor_tensor(out=hs[:], in0=hs[:], in1=s_bc, op=ALU.mult), reads=[hs, s8], writes=[hs])
                op("gpsimd", lambda e, hs=hs, hn=hn, z_b=z_b, nbk=nbk: e.tensor_tensor(out=hn[:].rearrange("p (h e) -> p h e", h=8), in0=hs[:], in1=z_b[:, nbk, :].rearrange("p (h e) -> p h e", h=8), op=ALU.mult), reads=[hs, z_b], writes=[hn])
                dma(HN[n * 128:(n + 1) * 128, :], hn[:], reads=[hn], slot=hn)

        NS = len(seq)
        loads_tile(0)
        stageA1(0)
        stageA2(0)
        for i in range(NS):
            if i % TB == 0 and ptile(i) + 1 < 2 * NT:
                loads_tile(ptile(i) + 1)
            if i + 1 < NS:
                stageA1(i + 1)
            stageB(i)
            if i + 1 < NS:
                if seq[i + 1][0] == 0 and seq[i][0] == 1:
                    sync_stores(hs_)
                    loadsB_tile(NT)
                stageA2(i + 1)
        sync_stores(hn_)
        phase_end(mark)

        mark = phase_scope()
        T4 = min(256, TT)
        NT4 = T // T4
        B4 = T4 // 128
        Wa = P.sb("Wa", [128, 8, D], BF16)
        Wm = P.sb("Wm", [128, 8, D], BF16)
        Wo = P.sb("Wo", [128, 8, D], BF16)
        mgT = P.sb("mgT", [128, 8], F32)
        dma(mgT[:], mgT_d[l], writes=[mgT], slot=mgT)
        w4s = [P.sb("w4s%d" % i, [128, 8, 512], F32) for i in range(2)]
        ci = 0
        for (Wd, Wsb, scaled) in ((wa_d, Wa, False), (wm_d, Wm, True), (wo_d, Wo, False)):
            wvv = Wd[l].rearrange("(kc p) c -> p kc c", p=128)
            for c0 in (0, 512):
                ws_ = w4s[ci % 2]
                ci += 1
                dma(ws_[:], wvv[:, :, c0:c0 + 512], writes=[ws_], slot=ws_)
                for kc in range(8):
                    eng = ("gpsimd", "vector", "scalar")[kc % 3]
                    if scaled:
                        if eng == "scalar":
                            op(eng, lambda e, kc=kc, ws_=ws_, Wsb=Wsb, c0=c0: e.activation(out=Wsb[:, kc, c0:c0 + 512], in_=ws_[:, kc, :], func=AF.Copy, scale=mgT[:, kc:kc + 1]), reads=[ws_, mgT], writes=[Wsb])
                        else:
                            op(eng, lambda e, kc=kc, ws_=ws_, Wsb=Wsb, c0=c0: e.tensor_scalar(out=Wsb[:, kc, c0:c0 + 512], in0=ws_[:, kc, :], scalar1=mgT[:, kc:kc + 1], scalar2=None, op0=ALU.mult), reads=[ws_, mgT], writes=[Wsb])
                    else:
                        if eng == "scalar":
                            op(eng, lambda e, kc=kc, ws_=ws_, Wsb=Wsb, c0=c0: e.activation(out=Wsb[:, kc, c0:c0 + 512], in_=ws_[:, kc, :], func=AF.Copy), reads=[ws_], writes=[Wsb])
                        else:
                            op(eng, lambda e, kc=kc, ws_=ws_, Wsb=Wsb, c0=c0: e.tensor_copy(out=Wsb[:, kc, c0:c0 + 512], in_=ws_[:, kc, :]), reads=[ws_], writes=[Wsb])
        ag = [P.sb("ag%d" % i, [128, 8, T4], BF16) for i in range(2)]
        hnt = [P.sb("hnt%d" % i, [128, B4, D], BF16) for i in range(2)]
        hg = [P.sb("hg%d" % i, [128, 8, T4], BF16) for i in range(2)]
        ga = [P.sb("ga%d" % i, [128, 16, T4], BF16) for i in range(2)]
        x4 = [P.sb("x4_%d" % i, [128, 8, T4], F32) for i in range(2)]
        mgd = [P.sb("mgd%d" % i, [128, 8, T4], BF16) for i in range(2)]
        yo = [P.sb("yo%d" % i, [128, 8, T4], F32) for i in range(2)]
        ta_ = [P.sb("ta%d" % i, [128, T4], F32) for i in range(2)]
        tb_ = [P.sb("tb%d" % i, [128, T4], F32) for i in range(2)]
        psTr = [P.ps("psTr%d" % i, [128, T4]) for i in range(2)]
        psAo = [P.ps("psAo%d" % i, [128, T4]) for i in range(2)]
        psMo = [P.ps("psMo%d" % i, [128, T4]) for i in range(2)]
        psYo = [P.ps("psYo%d" % i, [128, T4]) for i in range(2)]
        xsv = x_src.rearrange("(c p) t -> p c t", p=128)
        xdv = x_dst.rearrange("(c p) t -> p c t", p=128)
        for t in range(NT4):
            i2 = t % 2
            sl = slice(t * T4, (t + 1) * T4)
            a_, hn4, hg_, ga_, x_, mg_, y_ = ag[i2], hnt[i2], hg[i2], ga[i2], x4[i2], mgd[i2], yo[i2]
            dma(a_[:], AGT[:, sl].rearrange("(c p) t -> p c t", p=128), writes=[a_], slot=a_)
            dma(hn4[:], HN[sl, :].rearrange("(b p) c -> p b c", p=128), writes=[hn4], slot=hn4)
            dma(ga_[:], GAT[:, sl].rearrange("(c p) t -> p c t", p=128), writes=[ga_], slot=ga_)
            dma(x_[:], xsv[:, :, sl], writes=[x_], slot=x_)
            for c in range(8):
                ptr = psTr[c % 2]
                mm([lambda e, ptr=ptr, b=b, c=c, hn4=hn4: e.matmul(ptr[:, b * 128:(b + 1) * 128], hn4[:, b, c * 128:(c + 1) * 128], cstbf[:, IDENT, :], start=True, stop=True) for b in range(B4)],
                   reads=[hn4, cstbf], writes=[ptr])
                eng = "scalar" if c % 2 == 0 else "vector"
                if eng == "scalar":
                    op(eng, lambda e, ptr=ptr, hg_=hg_, c=c: e.activation(out=hg_[:, c, :], in_=ptr[:], func=AF.Copy), reads=[ptr], writes=[hg_])
                else:
                    op(eng, lambda e, ptr=ptr, hg_=hg_, c=c: e.tensor_copy(out=hg_[:, c, :], in_=ptr[:]), reads=[ptr], writes=[hg_])
            for oc in range(8):
                pa, pm = psAo[oc % 2], psMo[oc % 2]
                mm([lambda e, pa=pa, kc=kc, oc=oc, a_=a_: e.matmul(pa[:], Wa[:, kc, oc * 128:(oc + 1) * 128], a_[:, kc, :], start=(kc == 0), stop=(kc == 7)) for kc in range(8)], reads=[Wa, a_], writes=[pa])
                mm([lambda e, pm=pm, kc=kc, oc=oc, hg_=hg_: e.matmul(pm[:], Wm[:, kc, oc * 128:(oc + 1) * 128], hg_[:, kc, :], start=(kc == 0), stop=(kc == 7)) for kc in range(8)], reads=[Wm, hg_], writes=[pm])
                ta, tb = ta_[oc % 2], tb_[oc % 2]
                op("vector", lambda e, ta=ta, pa=pa, ga_=ga_, oc=oc: e.tensor_tensor(out=ta[:], in0=pa[:], in1=ga_[:, oc, :], op=ALU.mult), reads=[pa, ga_], writes=[ta])
                op("vector", lambda e, tb=tb, pm=pm, ga_=ga_, oc=oc: e.tensor_tensor(out=tb[:], in0=pm[:], in1=ga_[:, 8 + oc, :], op=ALU.mult), reads=[pm, ga_], writes=[tb])
                op("gpsimd", lambda e, ta=ta, tb=tb, mg_=mg_, oc=oc: e.tensor_tensor(out=mg_[:, oc, :], in0=ta[:], in1=tb[:], op=ALU.add), reads=[ta, tb], writes=[mg_])
            for oc in range(8):
                py = psYo[oc % 2]
                mm([lambda e, py=py, kc=kc, oc=oc, mg_=mg_: e.matmul(py[:], Wo[:, kc, oc * 128:(oc + 1) * 128], mg_[:, kc, :], start=(kc == 0), stop=(kc == 7)) for kc in range(8)], reads=[Wo, mg_], writes=[py])
                op("vector", lambda e, py=py, y_=y_, x_=x_, oc=oc: e.tensor_tensor(out=y_[:, oc, :], in0=py[:], in1=x_[:, oc, :], op=ALU.add), reads=[py, x_], writes=[y_])
            dma(xdv[:, :, sl], y_[:], reads=[y_], slot=y_)
        sync_stores(yo)
        phase_end(mark)

    with nc.Block() as block:
        P.replay(block)
    P.close()
    return nc


def make_consts():
    c = np.zeros((128, 6, 128), np.float32)
    i = np.arange(128)
    c[:, 0, :] = np.eye(128)
    c[:, 1, :] = (i[:, None] <= i[None, :])
    c[:, 2, :] = (i[:, None] >= i[None, :])
    c[:, 3, :] = (i[:, None] // 64 == i[None, :] // 64)
    Pm = np.zeros((128, 128), np.float32)
    for m in range(128):
        d = m % 64
        if d < 8:
            Pm[m + 8, m] = -1.0
        elif d < 16:
            Pm[m - 8, m] = 1.0
    c[:, 4, :] = Pm
    c[:, 5, :] = 1.0
    return c


def rope_tables(pos):
    T = pos.shape[0]
    half = 8
    inv = np.power(np.float32(500000.0), -np.arange(half, dtype=np.float32) * np.float32(2.0) / np.float32(16)).astype(np.float32)
    ang = (pos.astype(np.float32)[:, None] * inv[None, :]).astype(np.float32)
    cos = np.cos(ang).astype(np.float32)
    sin = np.sin(ang).astype(np.float32)
    ct = np.ones((128, T), np.float32)
    st = np.zeros((128, T), np.float32)
    for p in range(128):
        d = p % 64
        if d < 16:
            ct[p] = cos[:, d % 8]
            st[p] = sin[:, d % 8]
    rk = np.stack([ct, st]).astype(np.float32)
    rq = (rk * np.float32(0.125)).astype(np.float32)
    return rq, rk


def prep_weights(inp, L):
    f = lambda a: np.ascontiguousarray(np.asarray(a, dtype=np.float32))
    b_in = f(inp["b_in"])[:L]
    starts = ([O_AQ + 128 * i for i in range(8)] + [O_AK + 128 * i for i in range(2)] + [O_AZ + 128 * i for i in range(8)]
              + [O_MQ + 128 * i for i in range(16)] + [O_MG + 128 * i for i in range(16)])
    biasT = np.ascontiguousarray(np.stack([b_in[:, st:st + 128] for st in starts], axis=-1))
    gT = np.ascontiguousarray(f(inp["norm_g"])[:L].reshape(L, 8, 128).transpose(0, 2, 1))
    mgT = np.ascontiguousarray(f(inp["m_norm_g"])[:L].reshape(L, 8, 128).transpose(0, 2, 1))
    gq = np.tile(f(inp["q_norm_g"])[:L], (1, 2))
    gk = np.tile(f(inp["k_norm_g"])[:L], (1, 2))
    gqk = np.ascontiguousarray(np.stack([gq, gk], axis=-1))
    cwT = np.ascontiguousarray(f(inp["conv_w"])[:L].reshape(L, 3, 16, 128).transpose(0, 3, 2, 1))
    return {"w_in": f(inp["w_in"])[:L], "b_in": b_in, "biasT": biasT, "gT": gT, "mgT": mgT, "gqk": gqk,
            "sink": f(inp["sink"])[:L], "cwT": cwT, "w_att_out": f(inp["w_att_out"])[:L],
            "w_m_out": f(inp["w_m_out"])[:L], "w_out": f(inp["w_out"])[:L], "consts": make_consts()}


_NC_CACHE = {}


def kernel(x_prompt, x_sample, norm_g, w_in, b_in, q_norm_g, k_norm_g, sink, conv_w, m_norm_g, w_att_out, w_m_out, w_out):
    inp = dict(norm_g=norm_g, w_in=w_in, b_in=b_in, q_norm_g=q_norm_g, k_norm_g=k_norm_g, sink=sink, conv_w=conv_w,
               m_norm_g=m_norm_g, w_att_out=w_att_out, w_m_out=w_m_out, w_out=w_out)
    cfg = Cfg(3, 16, 4, 4)
    xp = np.asarray(x_prompt, np.float32)
    xs = np.asarray(x_sample, np.float32)
    wts = prep_weights(inp, 4)
    SEG = 2048
    in_maps = []
    for c in range(8):
        if c < 4:
            segs = [xp[c, :SEG], xp[c, SEG:], xs[c]]
            pos = np.concatenate([np.arange(4096), np.arange(2048)]).astype(np.float32)
            fl = [1.0, 0.0]
        else:
            j = 4 + 3 * (c - 4)
            segs = [xs[j], xs[j + 1], xs[j + 2]]
            pos = np.concatenate([np.arange(2048)] * 3).astype(np.float32)
            fl = [0.0, 0.0]
        xT = np.ascontiguousarray(np.concatenate(segs, axis=0).T)
        rq, rk = rope_tables(pos)
        m = dict(wts)
        m.update({"xT": xT, "flags": np.tile(np.array(fl, np.float32)[None, :], (128, 1)), "ropeq": rq, "ropek": rk})
        in_maps.append(m)
    if "nc" not in _NC_CACHE:
        _NC_CACHE["nc"] = build(cfg)
    res = run_bass_kernel_spmd(_NC_CACHE["nc"], in_maps, core_ids=list(range(8)))
    yp = np.empty_like(xp)
    ys = np.empty_like(xs)
    for c in range(8):
        y = np.asarray(res.results[c]["yT"], np.float32).T
        if c < 4:
            yp[c, :SEG] = y[:SEG]
            yp[c, SEG:] = y[SEG:2 * SEG]
            ys[c] = y[2 * SEG:]
        else:
            j = 4 + 3 * (c - 4)
            for i in range(3):
                ys[j + i] = y[i * SEG:(i + 1) * SEG]
    return (yp, ys)
```
